# Optimizing a Trainium2 kernel written in Bass

```python
import math
import jax
import jax.numpy as jnp
from jax import lax
import numpy as np

D_MODEL = 1024
BATCH = 4
SEQ = 4096
DEPTH = 2

N_EVEN = (DEPTH + 1) // 2
N_ODD = DEPTH // 2
EPS = 1e-6
ROPE_THETA = 10000.0
Q_BLOCK = 128

MLA_HEADS = 8
MLA_Q_LORA = 384
MLA_KV_LORA = 256
MLA_NOPE = 64
MLA_ROPE = 32
MLA_V = 64
DIFF_HEADS = 4
DIFF_HD = 64
DIFF_V = 2 * DIFF_HD
A_IN = MLA_Q_LORA + MLA_KV_LORA + MLA_ROPE + 2 * DIFF_HEADS * 2 * DIFF_HD + DIFF_HEADS * DIFF_V
A_MIX = MLA_HEADS * MLA_V + DIFF_HEADS * DIFF_V

SSM_HEADS = 8
SSM_HEADDIM = 64
SSM_INNER = SSM_HEADS * SSM_HEADDIM
SSM_GROUPS = 2
SSM_STATE = 128
SSM_CONV = 4
SSM_CHUNK = 128
SSM_CONV_DIM = SSM_INNER + 2 * SSM_GROUPS * SSM_STATE
HG_HEADS = 4
HG_EXPAND = 128
HG_VDIM = 128
HG_KDIM_TOTAL = HG_HEADS * HG_EXPAND
HG_WIDTH = HG_HEADS * HG_VDIM
HG_CHUNK = 64
S_IN = SSM_INNER + SSM_CONV_DIM + SSM_HEADS + 2 * HG_KDIM_TOTAL + 2 * HG_WIDTH
S_MIX = SSM_INNER + HG_WIDTH

D_FF = -(-8 * D_MODEL // (3 * 256)) * 256

kernel_name = 'hybrid_mla_diffattn_ssd_hgrn2_block'

F32 = jnp.float32


def rms_norm(x, w):
    xf = x.astype(F32)
    y = xf * lax.rsqrt(jnp.mean(xf * xf, axis=-1, keepdims=True) + EPS)
    return (y * w.astype(F32)).astype(x.dtype)


def split_cols(y, sizes):
    offsets = [int(v) for v in np.cumsum(sizes)[:-1]]
    return jnp.split(y, offsets, axis=-1)


def rope_tables(seq_len, dim):
    inv_freq = 1.0 / (ROPE_THETA ** (jnp.arange(0, dim, 2, dtype=F32) / dim))
    ang = jnp.arange(seq_len, dtype=F32)[:, None] * inv_freq[None, :]
    ang = jnp.concatenate([ang, ang], axis=-1)
    return jnp.cos(ang), jnp.sin(ang)


def apply_rope(x, cos, sin):
    half = x.shape[-1] // 2
    x1, x2 = x[..., :half], x[..., half:]
    rot = jnp.concatenate([-x2, x1], axis=-1)
    return (x.astype(F32) * cos + rot.astype(F32) * sin).astype(x.dtype)


def causal_attention(q, k, v, scale):
    bsz, nh, s_len, dk = q.shape
    dv = v.shape[-1]
    nb = s_len // Q_BLOCK
    qb = q.reshape(bsz, nh, nb, Q_BLOCK, dk).transpose(2, 0, 1, 3, 4)
    kpos = jnp.arange(s_len)

    def one_block(args):
        qi, bi = args
        s = jnp.einsum('bhqd,bhkd->bhqk', qi, k, preferred_element_type=F32) * scale
        qpos = bi * Q_BLOCK + jnp.arange(Q_BLOCK)
        s = jnp.where(kpos[None, :] <= qpos[:, None], s, -jnp.inf)
        p = jax.nn.softmax(s, axis=-1)
        return jnp.einsum('bhqk,bhkd->bhqd', p.astype(v.dtype), v)

    out = lax.map(one_block, (qb, jnp.arange(nb)))
    return out.transpose(1, 2, 0, 3, 4).reshape(bsz, nh, s_len, dv)


def attention_mixer(h, w_in, q_norm, w_uq, kv_norm, w_ukv, lq1, lk1, lq2, lk2, subln, w_out, lambda_init):
    bsz, s_len, _ = h.shape
    proj = h @ w_in
    c_q, c_kv, k_rope, dq, dk, dv = split_cols(
        proj, [MLA_Q_LORA, MLA_KV_LORA, MLA_ROPE, DIFF_HEADS * 2 * DIFF_HD, DIFF_HEADS * 2 * DIFF_HD, DIFF_HEADS * DIFF_V])
    q = (rms_norm(c_q, q_norm) @ w_uq).reshape(bsz, s_len, MLA_HEADS, MLA_NOPE + MLA_ROPE).transpose(0, 2, 1, 3)
    q_nope, q_pe = q[..., :MLA_NOPE], q[..., MLA_NOPE:]
    kv = (rms_norm(c_kv, kv_norm) @ w_ukv).reshape(bsz, s_len, MLA_HEADS, MLA_NOPE + MLA_V).transpose(0, 2, 1, 3)
    k_nope, v_mla = kv[..., :MLA_NOPE], kv[..., MLA_NOPE:]
    cos_r, sin_r = rope_tables(s_len, MLA_ROPE)
    q_pe = apply_rope(q_pe, cos_r, sin_r)
    k_pe = apply_rope(k_rope[:, None], cos_r, sin_r)
    q_mla = jnp.concatenate([q_nope, q_pe], axis=-1)
    k_mla = jnp.concatenate([k_nope, jnp.broadcast_to(k_pe, (bsz, MLA_HEADS, s_len, MLA_ROPE))], axis=-1)
    o_mla = causal_attention(q_mla, k_mla, v_mla, (MLA_NOPE + MLA_ROPE) ** -0.5)
    o_mla = o_mla.transpose(0, 2, 1, 3).reshape(bsz, s_len, MLA_HEADS * MLA_V)
    cos_d, sin_d = rope_tables(s_len, DIFF_HD)
    dq = apply_rope(dq.reshape(bsz, s_len, DIFF_HEADS, 2, DIFF_HD).transpose(0, 3, 2, 1, 4), cos_d, sin_d)
    dk = apply_rope(dk.reshape(bsz, s_len, DIFF_HEADS, 2, DIFF_HD).transpose(0, 3, 2, 1, 4), cos_d, sin_d)
    dv = dv.reshape(bsz, s_len, DIFF_HEADS, DIFF_V).transpose(0, 2, 1, 3)
    v2 = jnp.broadcast_to(dv[:, None], (bsz, 2, DIFF_HEADS, s_len, DIFF_V)).reshape(bsz, 2 * DIFF_HEADS, s_len, DIFF_V)
    o2 = causal_attention(dq.reshape(bsz, 2 * DIFF_HEADS, s_len, DIFF_HD),
                          dk.reshape(bsz, 2 * DIFF_HEADS, s_len, DIFF_HD), v2, DIFF_HD ** -0.5)
    o2 = o2.reshape(bsz, 2, DIFF_HEADS, s_len, DIFF_V)
    lam = (jnp.exp(jnp.sum(lq1.astype(F32) * lk1.astype(F32))) - jnp.exp(jnp.sum(lq2.astype(F32) * lk2.astype(F32)))
           + lambda_init).astype(o2.dtype)
    o_diff = o2[:, 0] - lam * o2[:, 1]
    o_diff = rms_norm(o_diff, subln) * (1.0 - lambda_init)
    o_diff = o_diff.transpose(0, 2, 1, 3).reshape(bsz, s_len, DIFF_HEADS * DIFF_V)
    return jnp.concatenate([o_mla, o_diff], axis=-1) @ w_out


def causal_depthwise_conv(x, w, b):
    ch = x.shape[-1]
    y = lax.conv_general_dilated(x, w[:, None, :].astype(x.dtype), window_strides=(1,),
                                 padding=[(SSM_CONV - 1, 0)], dimension_numbers=('NWC', 'WIO', 'NWC'),
                                 feature_group_count=ch)
    return y + b


def ssd_chunked(xs, dt, a_head, b_in, c_in):
    bsz, s_len, _ = xs.shape
    nc, L = s_len // SSM_CHUNK, SSM_CHUNK
    G, R, P, N = SSM_GROUPS, SSM_HEADS // SSM_GROUPS, SSM_HEADDIM, SSM_STATE
    dtc = dt.reshape(bsz, nc, L, G, R)
    xh = xs.astype(F32).reshape(bsz, nc, L, G, R, P) * dtc[..., None]
    a = (dtc * a_head.reshape(G, R)).transpose(0, 1, 3, 4, 2)
    a_cs = jnp.cumsum(a, axis=-1)
    causal = jnp.tril(jnp.ones((L, L), dtype=bool))
    seg = jnp.exp(jnp.where(causal, a_cs[..., :, None] - a_cs[..., None, :], -jnp.inf))
    bc = b_in.astype(F32).reshape(bsz, nc, L, G, N)
    cc = c_in.astype(F32).reshape(bsz, nc, L, G, N)
    cb = jnp.einsum('bclgn,bcsgn->bcgls', cc, bc)
    y_diag = jnp.einsum('bcgls,bcgrls,bcsgrp->bclgrp', cb, seg, xh)
    decay_states = jnp.exp(a_cs[..., -1:] - a_cs)
    states = jnp.einsum('bclgn,bcgrl,bclgrp->bcgrpn', bc, decay_states, xh)
    chunk_decay = jnp.exp(a_cs[..., -1])

    def step(h_prev, inp):
        st, dec = inp
        return dec[..., None, None] * h_prev + st, h_prev

    h0 = jnp.zeros((bsz, G, R, P, N), F32)
    _, prev = lax.scan(step, h0, (states.transpose(1, 0, 2, 3, 4, 5), chunk_decay.transpose(1, 0, 2, 3)))
    prev = prev.transpose(1, 0, 2, 3, 4, 5)
    y_off = jnp.einsum('bclgn,bcgrpn,bcgrl->bclgrp', cc, prev, jnp.exp(a_cs))
    return (y_diag + y_off).reshape(bsz, s_len, SSM_HEADS, P)


def hgrn2_chunked(q, k, v, log_f):
    bsz, s_len, nh, kd = q.shape
    vd = v.shape[-1]
    nc, L = s_len // HG_CHUNK, HG_CHUNK

    def chunks(t):
        return t.astype(F32).reshape(bsz, nc, L, nh, t.shape[-1]).transpose(1, 0, 3, 2, 4)

    causal = jnp.tril(jnp.ones((L, L), dtype=bool))[None, None, :, :, None]

    def step(state, inp):
        qc, kc, vc, gc = inp
        g_cum = jnp.cumsum(gc, axis=2)
        g_last = g_cum[:, :, -1:, :]
        o_inter = jnp.einsum('bhlk,bhkv->bhlv', qc * jnp.exp(g_cum), state)
        decay = jnp.exp(jnp.where(causal, g_cum[:, :, :, None, :] - g_cum[:, :, None, :, :], -jnp.inf))
        scores = jnp.einsum('bhlk,bhlsk,bhsk->bhls', qc, decay, kc)
        o = o_inter + jnp.einsum('bhls,bhsv->bhlv', scores, vc)
        state = (jnp.exp(g_last[:, :, 0, :])[..., None] * state
                 + jnp.einsum('bhsk,bhsv->bhkv', kc * jnp.exp(g_last - g_cum), vc))
        return state, o

    s0 = jnp.zeros((bsz, nh, kd, vd), F32)
    _, o = lax.scan(step, s0, (chunks(q), chunks(k), chunks(v), chunks(log_f)))
    return o.transpose(1, 0, 3, 2, 4).reshape(bsz, s_len, nh, vd)


def recurrent_mixer(h, w_in, conv_w, conv_b, dt_bias, a_log, d_skip, ssm_norm, g_norm, lb, w_out):
    bsz, s_len, _ = h.shape
    proj = h @ w_in
    z, xbc, dt, hq, hf, hi, hg = split_cols(
        proj, [SSM_INNER, SSM_CONV_DIM, SSM_HEADS, HG_KDIM_TOTAL, HG_KDIM_TOTAL, HG_WIDTH, HG_WIDTH])
    xbc = jax.nn.silu(causal_depthwise_conv(xbc, conv_w, conv_b))
    xs, b_in, c_in = split_cols(xbc, [SSM_INNER, SSM_GROUPS * SSM_STATE, SSM_GROUPS * SSM_STATE])
    dt = jax.nn.softplus(dt.astype(F32) + dt_bias.astype(F32))
    a_head = -jnp.exp(a_log.astype(F32))
    y = ssd_chunked(xs, dt, a_head, b_in, c_in)
    y = y + d_skip.astype(F32)[:, None] * xs.astype(F32).reshape(bsz, s_len, SSM_HEADS, SSM_HEADDIM)
    y = y.astype(h.dtype).reshape(bsz, s_len, SSM_INNER) * jax.nn.silu(z)
    gsz = SSM_INNER // SSM_GROUPS
    y = rms_norm(y.reshape(bsz, s_len, SSM_GROUPS, gsz), ssm_norm.reshape(SSM_GROUPS, gsz)).reshape(bsz, s_len, SSM_INNER)
    lb = lb.astype(F32)
    xf = hf.astype(F32)
    log_f = jnp.logaddexp(jnp.log(lb), jnp.log1p(-lb) + jax.nn.log_sigmoid(xf))
    k_in = (1.0 - lb) * jax.nn.sigmoid(-xf)
    q = jax.nn.silu(hq).reshape(bsz, s_len, HG_HEADS, HG_EXPAND)
    o = hgrn2_chunked(q, k_in.reshape(bsz, s_len, HG_HEADS, HG_EXPAND), hi.reshape(bsz, s_len, HG_HEADS, HG_VDIM),
                      log_f.reshape(bsz, s_len, HG_HEADS, HG_EXPAND)).astype(h.dtype)
    o = rms_norm(o, g_norm) * jax.nn.silu(hg).reshape(bsz, s_len, HG_HEADS, HG_VDIM)
    o = o.reshape(bsz, s_len, HG_WIDTH)
    return jnp.concatenate([y, o], axis=-1) @ w_out


def swiglu(h, w_gate, w_up, w_down):
    return (jax.nn.silu(h @ w_gate) * (h @ w_up)) @ w_down


def setup_inputs(seed: int = 0) -> dict:
    key = jax.random.key(seed)
    ks = jax.random.split(key, 32)

    def nrm(k, shape, fan_in):
        return jax.random.normal(k, shape, F32) * (fan_in ** -0.5)

    def gain(k, shape):
        return 1.0 + 0.05 * jax.random.normal(k, shape, F32)

    dt0 = jnp.exp(jax.random.uniform(ks[20], (N_ODD, SSM_HEADS), F32, math.log(1e-3), math.log(1e-1)))
    return {
        'x': jax.random.normal(ks[0], (BATCH, SEQ, D_MODEL), F32),
        'norm_mix': gain(ks[1], (DEPTH, D_MODEL)),
        'norm_ffn': gain(ks[2], (DEPTH, D_MODEL)),
        'norm_final': gain(ks[3], (D_MODEL,)),
        'a_w_in': nrm(ks[4], (N_EVEN, D_MODEL, A_IN), D_MODEL),
        'a_q_norm': gain(ks[5], (N_EVEN, MLA_Q_LORA)),
        'a_w_uq': nrm(ks[6], (N_EVEN, MLA_Q_LORA, MLA_HEADS * (MLA_NOPE + MLA_ROPE)), MLA_Q_LORA),
        'a_kv_norm': gain(ks[7], (N_EVEN, MLA_KV_LORA)),
        'a_w_ukv': nrm(ks[8], (N_EVEN, MLA_KV_LORA, MLA_HEADS * (MLA_NOPE + MLA_V)), MLA_KV_LORA),
        'a_lq1': 0.1 * jax.random.normal(ks[9], (N_EVEN, DIFF_HD), F32),
        'a_lk1': 0.1 * jax.random.normal(ks[10], (N_EVEN, DIFF_HD), F32),
        'a_lq2': 0.1 * jax.random.normal(ks[11], (N_EVEN, DIFF_HD), F32),
        'a_lk2': 0.1 * jax.random.normal(ks[12], (N_EVEN, DIFF_HD), F32),
        'a_subln': gain(ks[13], (N_EVEN, DIFF_V)),
        'a_w_out': nrm(ks[14], (N_EVEN, A_MIX, D_MODEL), A_MIX),
        's_w_in': nrm(ks[15], (N_ODD, D_MODEL, S_IN), D_MODEL),
        's_conv_w': nrm(ks[16], (N_ODD, SSM_CONV, SSM_CONV_DIM), SSM_CONV),
        's_conv_b': 0.02 * jax.random.normal(ks[17], (N_ODD, SSM_CONV_DIM), F32),
        's_dt_bias': dt0 + jnp.log(-jnp.expm1(-dt0)),
        's_a_log': jnp.log(jax.random.uniform(ks[18], (N_ODD, SSM_HEADS), F32, 1.0, 16.0)),
        's_d': gain(ks[19], (N_ODD, SSM_HEADS)),
        's_norm': gain(ks[21], (N_ODD, SSM_INNER)),
        'h_g_norm': gain(ks[22], (N_ODD, HG_VDIM)),
        'h_lower_bound': 0.1 * jax.random.normal(ks[23], (DEPTH, HG_KDIM_TOTAL), F32),
        's_w_out': nrm(ks[24], (N_ODD, S_MIX, D_MODEL), S_MIX),
        'ffn_gate': nrm(ks[25], (DEPTH, D_MODEL, D_FF), D_MODEL),
        'ffn_up': nrm(ks[26], (DEPTH, D_MODEL, D_FF), D_MODEL),
        'ffn_down': nrm(ks[27], (DEPTH, D_FF, D_MODEL), D_FF),
    }


def reference(x, norm_mix, norm_ffn, norm_final, a_w_in, a_q_norm, a_w_uq, a_kv_norm, a_w_ukv,
              a_lq1, a_lk1, a_lq2, a_lk2, a_subln, a_w_out, s_w_in, s_conv_w, s_conv_b, s_dt_bias,
              s_a_log, s_d, s_norm, h_g_norm, h_lower_bound, s_w_out, ffn_gate, ffn_up, ffn_down):
    p_lb = jax.nn.softmax(h_lower_bound.astype(F32), axis=0)
    lb_all = jnp.cumsum(p_lb, axis=0) - p_lb[0:1]
    for l in range(DEPTH):
        hn = rms_norm(x, norm_mix[l])
        i = l // 2
        if l % 2 == 0:
            lambda_init = 0.8 - 0.6 * math.exp(-0.3 * l)
            m = attention_mixer(hn, a_w_in[i], a_q_norm[i], a_w_uq[i], a_kv_norm[i], a_w_ukv[i],
                                a_lq1[i], a_lk1[i], a_lq2[i], a_lk2[i], a_subln[i], a_w_out[i], lambda_init)
        else:
            m = recurrent_mixer(hn, s_w_in[i], s_conv_w[i], s_conv_b[i], s_dt_bias[i], s_a_log[i], s_d[i],
                                s_norm[i], h_g_norm[i], lb_all[l], s_w_out[i])
        x = x + m
        x = x + swiglu(rms_norm(x, norm_ffn[l]), ffn_gate[l], ffn_up[l], ffn_down[l])
    return rms_norm(x, norm_final)
```

```python
import contextlib
import math
from functools import partial
import numpy as np
import concourse.bass as bass
import concourse.mybir as mybir
from concourse.bass_utils import run_bass_kernel_spmd

F32 = mybir.dt.float32
BF16 = mybir.dt.bfloat16
AF = mybir.ActivationFunctionType
ALU = mybir.AluOpType
AX = mybir.AxisListType

EPS = 1e-6
SCHED_MASK = [True, True, True]
SCHED = dict(final=True, outproj1=True, ffn=True, outproj0=True, dproj=True, dattn=True, mlaattn=True, mlaprojB=True)
D = 1024
TOK = 2048
NT = 16
DFF = 2816

ENG_NAMES = ("pe", "act", "dve", "pool", "sp")
N_DMA_SEMS = 24


def _I(name, *args, **kwargs):
    return (name, args, kwargs)


class Buf:
    __slots__ = ("name", "last_w", "readers")

    def __init__(self, name=""):
        self.name = name
        self.last_w = None
        self.readers = []


class Op:
    __slots__ = ("idx", "eng", "fn", "dma", "deps", "signal", "sem", "val", "waits", "clock", "inc")

    def __init__(self, idx, eng, fn, dma):
        self.inc = 16 if dma else 1
        self.idx = idx
        self.eng = eng
        self.fn = fn
        self.dma = dma
        self.deps = {}
        self.signal = False
        self.sem = None
        self.val = 0
        self.waits = []
        self.clock = None


class Sched:
    def __init__(self, nc, stack):
        self.nc = nc
        self.ops = []
        self.bufs = []
        self.dma_rr = 0
        self.dma_rr2 = [0, 0]
        self.dma_last = [None] * N_DMA_SEMS
        self.counts = {}
        self.sems = {}
        for e in ENG_NAMES:
            self.sems[("eng", e)] = stack.enter_context(nc.semaphore("se_" + e))
            self.counts[("eng", e)] = 0
        for i in range(N_DMA_SEMS):
            self.sems[("dma", i)] = stack.enter_context(nc.semaphore("sd_%d" % i))
            self.counts[("dma", i)] = 0
        self.sems[("cc", 0)] = stack.enter_context(nc.semaphore("s_cc"))
        self.counts[("cc", 0)] = 0
        self.cc_last = None
        self.n_inst = 0
        self.autosched = True

    def buf(self, name=""):
        b = Buf(name)
        self.bufs.append(b)
        return b

    def cc(self, fn, reads=(), writes=()):
        op = self.add("pool", fn, reads, writes, dma=True, cc=True)
        return op

    def add(self, eng, fn, reads=(), writes=(), dma=False, cc=False):
        op = Op(len(self.ops), eng, fn, dma)
        self.ops.append(op)
        for b in reads:
            if b.last_w is not None:
                op.deps[b.last_w] = True
            b.readers.append(op.idx)
        for b in writes:
            if b.last_w is not None:
                op.deps.setdefault(b.last_w, False)
            for r in b.readers:
                if r != op.idx:
                    op.deps.setdefault(r, False)
            b.last_w = op.idx
            b.readers = []
        if cc:
            op.inc = 1
            if self.cc_last is not None:
                op.deps[self.cc_last] = True
            self.cc_last = op.idx
            op.sem = ("cc", 0)
        elif dma:
            half = N_DMA_SEMS // 2
            k = 1 if eng == "pool" else 0
            s = k * half + self.dma_rr2[k]
            self.dma_rr2[k] = (self.dma_rr2[k] + 1) % half
            if self.dma_last[s] is not None:
                op.deps[self.dma_last[s]] = True
            self.dma_last[s] = op.idx
            op.sem = ("dma", s)
        else:
            op.sem = ("eng", eng)
        return op

    def pe(self, fn, reads=(), writes=()):
        return self.add("pe", fn, reads, writes)

    def act(self, fn, reads=(), writes=()):
        return self.add("act", fn, reads, writes)

    def dve(self, fn, reads=(), writes=()):
        return self.add("dve", fn, reads, writes)

    def pool(self, fn, reads=(), writes=()):
        return self.add("pool", fn, reads, writes)

    def dma(self, q, fn, reads=(), writes=()):
        return self.add(q, fn, reads, writes, dma=True)

    @staticmethod
    def _est_ns(op):
        name, a, kw = op.fn
        out = kw.get("out", a[0] if a else None)
        try:
            shp = out.shape
            free = 1
            for d in shp[1:]:
                free *= d
        except Exception:
            free = 512
        if op.dma:
            return 30000.0 if name == "collective_compute" else 2500.0 + free * 0.5
        if name in ("matmul", "transpose"):
            f32 = False
            try:
                f32 = (kw.get("lhsT", kw.get("in_")).dtype == F32)
            except Exception:
                pass
            return (max(64, free) / 2.4 + 25) * (4 if f32 else 1) + 60
        if name == "activation":
            return free / 1.05 + 220
        if name == "reciprocal":
            return free * 6.5 + 100
        if op.eng == "pool":
            return free * 2.3 + 150
        return free * 1.05 + 100

    def _list_schedule(self, ops):
        import heapq
        n = len(ops)
        dur = [self._est_ns(o) for o in ops]
        succ = [[] for _ in range(n)]
        indeg = [0] * n
        for o in ops:
            for d in o.deps:
                succ[d].append(o.idx)
                indeg[o.idx] += 1
        prio = [0.0] * n
        for i in range(n - 1, -1, -1):
            m = 0.0
            for s_ in succ[i]:
                if prio[s_] > m:
                    m = prio[s_]
            prio[i] = dur[i] + m
        pending = {e: [] for e in ENG_NAMES}
        avail = {e: [] for e in ENG_NAMES}
        ready_t = [0.0] * n
        free = {e: 0.0 for e in ENG_NAMES}
        for i in range(n):
            if indeg[i] == 0:
                heapq.heappush(pending[ops[i].eng], (0.0, i))
        order = []
        LAT = 250.0
        while len(order) < n:
            best = None
            for e in ENG_NAMES:
                pe_, av = pending[e], avail[e]
                while pe_ and pe_[0][0] <= free[e]:
                    rt, i = heapq.heappop(pe_)
                    heapq.heappush(av, (-prio[i], i))
                if av:
                    cand = (free[e], av[0][0], e, True)
                elif pe_:
                    cand = (pe_[0][0], -prio[pe_[0][1]], e, False)
                else:
                    continue
                if best is None or cand[:2] < best[:2]:
                    best = cand
            st, _, e, from_av = best
            if from_av:
                _, i = heapq.heappop(avail[e])
            else:
                _, i = heapq.heappop(pending[e])
            o = ops[i]
            fin = st + dur[i]
            if o.dma:
                free[e] = st + (700.0 if e == "pool" else 120.0)
            else:
                free[e] = fin
            order.append(i)
            for s_ in succ[i]:
                r = fin + (LAT if (ops[s_].eng != e or o.dma) else 60.0)
                if r > ready_t[s_]:
                    ready_t[s_] = r
                indeg[s_] -= 1
                if indeg[s_] == 0:
                    heapq.heappush(pending[ops[s_].eng], (ready_t[s_], s_))
        return order

    def flush(self, sched=False, pe_groups=None):
        nc = self.nc
        ops = self.ops
        if not ops:
            return
        if pe_groups is None:
            pe_groups = sched
        if pe_groups:
            for op in ops:
                if op.eng == "pe" and op.fn[2].get("start", True):
                    for d in op.deps:
                        if ops[d].eng == "pe":
                            op.deps[d] = True
        for op in ops:
            for d, strict in op.deps.items():
                p = ops[d]
                if p.dma or p.eng != op.eng or strict:
                    p.signal = True
            if op.dma:
                op.signal = True
        order = self._list_schedule(ops) if (sched and self.autosched) else list(range(len(ops)))
        ops_o = [ops[i] for i in order]
        counts = self.counts
        for op in ops_o:
            if op.signal:
                counts[op.sem] += op.inc
                op.val = counts[op.sem]
        base = dict(self.base_counts) if hasattr(self, "base_counts") else {k: 0 for k in counts}
        eclock = {e: dict(base) for e in ENG_NAMES}
        for op in ops_o:
            ck = eclock[op.eng]
            need = {}
            for d, strict in op.deps.items():
                p = ops[d]
                if not (p.dma or p.eng != op.eng or strict):
                    continue
                if ck.get(p.sem, 0) >= p.val:
                    continue
                if need.get(p.sem, (0, None))[0] < p.val:
                    need[p.sem] = (p.val, p)
            items = sorted(need.items(), key=lambda kv: -kv[1][1].idx)
            for sem, (val, p) in items:
                if ck.get(sem, 0) >= val:
                    continue
                op.waits.append((sem, val))
                for k, v in p.clock.items():
                    if ck.get(k, 0) < v:
                        ck[k] = v
                if ck.get(sem, 0) < val:
                    ck[sem] = val
            if op.signal:
                c = dict(ck)
                c[op.sem] = op.val
                op.clock = c
        sems = self.sems
        final = dict(counts)
        per_eng = {e: [o for o in ops_o if o.eng == e] for e in ENG_NAMES}
        self.n_inst += len(ops)

        def run(engh, lst):
            for o in lst:
                for sem, val in o.waits:
                    engh.wait_ge(sems[sem], val)
                name, a, kw = o.fn
                ins = getattr(engh, name)(*a, **kw)
                if o.signal:
                    ins.then_inc(sems[o.sem], o.inc)
            for k, v in final.items():
                if v > base.get(k, 0):
                    engh.wait_ge(sems[k], v)

        with nc.Block() as block:
            @block.tensor
            def _(e):
                run(e, per_eng["pe"])

            @block.scalar
            def _(e):
                run(e, per_eng["act"])

            @block.vector
            def _(e):
                run(e, per_eng["dve"])

            @block.gpsimd
            def _(e):
                run(e, per_eng["pool"])

            @block.sync
            def _(e):
                run(e, per_eng["sp"])

        self.base_counts = dict(counts)
        self.ops = []
        self.dma_last = [None] * N_DMA_SEMS
        self.cc_last = None
        for b in self.bufs:
            b.last_w = None
            b.readers = []


class T:
    def __init__(self, S, h, name):
        self.h = h
        self.b = S.buf(name)

    def __getitem__(self, k):
        return self.h[k]


class Ctx:
    def __init__(self, nc, S):
        self.nc = nc
        self.S = S
        self.uid = 0

    def sb(self, st, name, shape, dt):
        self.uid += 1
        h = st.enter_context(self.nc.sbuf_tensor("%s_%d" % (name, self.uid), list(shape), dt))
        return T(self.S, h, name)

    def ps(self, st, name, shape, dt):
        self.uid += 1
        h = st.enter_context(self.nc.psum_tensor("%s_%d" % (name, self.uid), list(shape), dt))
        return T(self.S, h, name)


def rms_stats(S, src_ap, src_bufs, n, junk, ss, rstd, epsc, lnexp=False):
    S.act(_I("activation", out=junk[:, 0:n], in_=src_ap, func=AF.Square, accum_out=ss[:, 0:1]),
          reads=src_bufs, writes=[junk.b, ss.b])
    if lnexp:
        S.act(_I("activation", out=ss[:, 1:2], in_=ss[:, 0:1], func=AF.Ln, scale=1.0 / n, bias=epsc[:, 0:1]),
              reads=[ss.b, epsc.b], writes=[ss.b])
        S.act(_I("activation", out=rstd[:, 0:1], in_=ss[:, 1:2], func=AF.Exp, scale=-0.5), reads=[ss.b], writes=[rstd.b])
        return
    S.act(_I("activation", out=ss[:, 1:2], in_=ss[:, 0:1], func=AF.Sqrt, scale=1.0 / n, bias=epsc[:, 0:1]),
          reads=[ss.b, epsc.b], writes=[ss.b])
    S.dve(_I("reciprocal", out=rstd[:, 0:1], in_=ss[:, 1:2]), reads=[ss.b], writes=[rstd.b])


def build_program(stage="all"):
    nc = bass.Bass("TRN2", target_bir_lowering=False)

    def din(name, shape, dt=F32):
        return nc.dram_tensor(name, list(shape), dt, kind="ExternalInput").ap()

    x_d = din("x", [TOK, D])
    xp_d = din("xp", [TOK, D])
    nmix_d = din("norm_mix", [2, D])
    nffn_d = din("norm_ffn", [2, D])
    nfin_d = din("norm_final", [D])
    awin_d = din("a_w_in", [D, 2208])
    aqn_d = din("a_q_norm", [384])
    awuq_d = din("a_w_uq", [384, 768])
    akvn_d = din("a_kv_norm", [256])
    awukv_d = din("a_w_ukv", [256, 1024])
    alam_d = din("a_lam", [4, 64])
    asub_d = din("a_subln", [128, 1])
    awout_d = din("a_w_out", [D, D])
    fg_d = din("ffn_gate", [2, D, DFF])
    fu_d = din("ffn_up", [2, D, DFF])
    fd_d = din("ffn_down", [2, DFF, D])
    ident_d = din("ident", [128, 128])
    maskT_d = din("maskT", [128, 128])
    kones_d = din("kones", [128, 32])
    cos32_d = din("cos32", [128, 32, 32])
    sin32_d = din("sin32", [128, 32, 32])
    cos64_d = din("cos64", [128, 32, 64])
    sin64_d = din("sin64", [128, 32, 64])
    swin_d = din("s_w_in", [D, 3592])
    convw_d = din("s_conv_w", [128, 8, 4])
    convb_d = din("s_conv_b", [128, 8])
    sdtb_d = din("s_dt_bias", [8])
    salog_d = din("s_a_log", [8])
    sd_d = din("s_d", [8])
    snorm_d = din("s_norm", [512])
    hgn_d = din("h_g_norm", [128])
    hlb_d = din("h_lb", [128, 2, 4])
    swout_d = din("s_w_out", [D, D])
    flag_d = din("flag", [128, 1])
    rmask_d = din("rmask", [128, 512])
    out_d = nc.dram_tensor("out", [TOK, D], F32, kind="ExternalOutput").ap()
    gu_d = nc.dram_tensor("gu_bf", [2, 22, 128, 8, 256], BF16).ap()
    wd_d = nc.dram_tensor("wd_bf", [2, DFF, D], BF16).ap()
    pay_d = nc.dram_tensor("pay", [128, 1048], F32).ap()
    gath_d = nc.dram_tensor("gath", [256, 1048], F32).ap()

    with contextlib.ExitStack() as top:
        S = Sched(nc, top)
        C = Ctx(nc, S)

        OT = C.sb(top, "OT", [128, 8, TOK], BF16)
        ident = C.sb(top, "ident", [128, 128], BF16)
        epsc = C.sb(top, "epsc", [128, 1], F32)
        ones32 = C.sb(top, "ones32", [128, 128], F32)
        S.dma("pool", _I("dma_start", out=ident[:], in_=ident_d), writes=[ident.b])
        S.dve(_I("memset", epsc[:], EPS), writes=[epsc.b])
        S.dve(_I("memset", ones32[:], 1.0), writes=[ones32.b])
        S.flush(sched=SCHED.get("init", False))

        gub = [S.buf("gu0"), S.buf("gu1")]
        wdb = [S.buf("wd0"), S.buf("wd1")]

        def precast(layer):
            fgv_ = fg_d[layer].rearrange("(c p) n -> p c n", p=128)
            fuv_ = fu_d[layer].rearrange("(c p) n -> p c n", p=128)
            th = []
            for c in range(22):
                th.append(partial(S.dma, "pool", _I("dma_start", out=gu_d[layer, c, :, :, 0:128], in_=fgv_[:, :, c * 128:(c + 1) * 128]),
                                  (), [S.buf()]))
                th.append(partial(S.dma, "pool", _I("dma_start", out=gu_d[layer, c, :, :, 128:256], in_=fuv_[:, :, c * 128:(c + 1) * 128]),
                                  (), [S.buf()]))
            for c in range(0, 22, 2):
                th.append(partial(S.dma, "pool", _I("dma_start", out=wd_d[layer, c * 128:(c + 2) * 128, :],
                                                    in_=fd_d[layer][c * 128:(c + 2) * 128, :]), (), [S.buf()]))
            return th

        def layer0_attention():
            with contextlib.ExitStack() as L0:
                maskT = C.sb(L0, "maskT", [128, 128], BF16)
                kones = C.sb(L0, "kones", [128, 32], F32)
                konesb = C.sb(L0, "konesb", [128, 32], BF16)
                nmix = C.sb(L0, "nmix", [128, D], F32)
                S.dma("pool", _I("dma_start", out=maskT[:], in_=maskT_d), writes=[maskT.b])
                S.dma("sp", _I("dma_start", out=kones[:], in_=kones_d), writes=[kones.b])
                S.dma("pool", _I("dma_start", out=konesb[:], in_=kones_d), writes=[konesb.b])
                S.dma("sp", _I("dma_start", out=nmix[:], in_=nmix_d[0].partition_broadcast(128)), writes=[nmix.b])

                def hn_tile(st_bufs, t):
                    xt, hn, junk, ss, rstd, pT, hnT = st_bufs
                    src = xp_d if t < 16 else x_d
                    r0 = (t % 16) * 128
                    xt_ = xt[t % 2]
                    S.dma("sp", _I("dma_start", out=xt_[:], in_=src[r0:r0 + 128, :]), writes=[xt_.b])
                    rms_stats(S, xt_[:], [xt_.b], D, junk, ss, rstd, epsc)
                    hn_ = hn[t % 2]
                    S.dve(_I("scalar_tensor_tensor", out=hn_[:], in0=xt_[:], scalar=rstd[:, 0:1], in1=nmix[:],
                                                          op0=ALU.mult, op1=ALU.mult),
                          reads=[xt_.b, rstd.b, nmix.b], writes=[hn_.b])
                    pT_ = pT[t % 2]
                    for kc in range(8):
                        S.pe(_I("transpose", out=pT_[:, kc * 128:(kc + 1) * 128],
                                                          in_=hn_[:, kc * 128:(kc + 1) * 128], identity=ident[:]),
                             reads=[hn_.b, ident.b], writes=[pT_.b])
                    hnT_ = hnT[t % 2]
                    S.act(_I("activation", out=hnT_[:], in_=pT_[:], func=AF.Copy), reads=[pT_.b], writes=[hnT_.b])
                    return hnT_

                def rope(src3, src_bufs, nh, hd, cosb, sinb, tb_bufs, t1, t2, out3, out_bufs):
                    hh = hd // 2
                    S.dve(_I("tensor_tensor", out=t1[:, 0:nh * hd].rearrange("p (h d) -> p h d", d=hd), in0=src3,
                                                    in1=cosb, op=ALU.mult),
                          reads=src_bufs + tb_bufs, writes=[t1.b])
                    t2v = t2[:, 0:nh * hd].rearrange("p (h d) -> p h d", d=hd)
                    S.dve(_I("tensor_tensor", out=t2v[:, :, 0:hh], in0=src3[:, :, hh:hd], in1=sinb[:, :, 0:hh], op=ALU.mult),
                          reads=src_bufs + tb_bufs, writes=[t2.b])
                    S.dve(_I("tensor_tensor", out=t2v[:, :, hh:hd], in0=src3[:, :, 0:hh], in1=sinb[:, :, hh:hd], op=ALU.mult),
                          reads=src_bufs + tb_bufs, writes=[t2.b])
                    S.pool(_I("tensor_tensor", out=out3, in0=t1[:, 0:nh * hd].rearrange("p (h d) -> p h d", d=hd),
                                                     in1=t2v, op=ALU.add),
                           reads=[t1.b, t2.b], writes=out_bufs)

                with contextlib.ExitStack() as M:
                    ckvnT = C.sb(M, "ckvnT", [128, 2, 4096], BF16)
                    kpeT = C.sb(M, "kpeT", [32, 4096], BF16)
                    QT = C.sb(M, "QT", [96, 8, TOK], BF16)
                    Vall = C.sb(M, "Vall", [128, 32, 8, 65], BF16)
                    wukv = C.sb(M, "wukv", [128, 2, 1024], BF16)
                    S.dma("pool", _I("dma_start", out=wukv[:], in_=awukv_d.rearrange("(c p) n -> p c n", p=128)),
                          writes=[wukv.b])
                    with contextlib.ExitStack() as A:
                        cqnT = C.sb(A, "cqnT", [128, 3, TOK], BF16)
                        wA = C.sb(A, "wA", [128, 8, 672], BF16)
                        wuq = C.sb(A, "wuq", [128, 3, 768], BF16)
                        qn = C.sb(A, "qn", [128, 384], F32)
                        kvn = C.sb(A, "kvn", [128, 256], F32)
                        cos32 = C.sb(A, "cos32", [128, 32, 32], F32)
                        sin32 = C.sb(A, "sin32", [128, 32, 32], F32)
                        S.dma("pool", _I("dma_start", out=wA[:], in_=awin_d.rearrange("(c p) n -> p c n", p=128)[:, :, 0:672]),
                              writes=[wA.b])
                        S.dma("pool", _I("dma_start", out=wuq[:], in_=awuq_d.rearrange("(c p) n -> p c n", p=128)),
                              writes=[wuq.b])
                        S.dma("sp", _I("dma_start", out=qn[:], in_=aqn_d.partition_broadcast(128)), writes=[qn.b])
                        S.dma("sp", _I("dma_start", out=kvn[:], in_=akvn_d.partition_broadcast(128)), writes=[kvn.b])
                        S.dma("sp", _I("dma_start", out=cos32[:], in_=cos32_d), writes=[cos32.b])
                        S.dma("sp", _I("dma_start", out=sin32[:], in_=sin32_d), writes=[sin32.b])
                        xt = [C.sb(A, "xt", [128, D], F32) for _ in range(2)]
                        hn = [C.sb(A, "hn", [128, D], BF16) for _ in range(2)]
                        junk = C.sb(A, "junk", [128, D], F32)
                        ss = C.sb(A, "ss", [128, 2], F32)
                        rstd = C.sb(A, "rstd", [128, 1], F32)
                        hnT = [C.sb(A, "hnT", [128, D], BF16) for _ in range(2)]
                        cqn = C.sb(A, "cqn", [128, 384], BF16)
                        ckvn = C.sb(A, "ckvn", [128, 256], BF16)
                        kpe = C.sb(A, "kpe", [128, 32], BF16)
                        t1 = C.sb(A, "t1", [128, 512], F32)
                        t2 = C.sb(A, "t2", [128, 512], F32)
                        qbf = C.sb(A, "qbf", [128, 8, 96], BF16)
                        pT = [C.ps(A, "pT", [128, D], BF16) for _ in range(2)]
                        pQ = C.ps(A, "pQ", [128, 512], F32)
                        pKV = C.ps(A, "pKV", [128, 512], F32)
                        pT2 = C.ps(A, "pT2", [128, D], BF16)
                        pV = C.ps(A, "pV", [128, 512], F32)
                        ss1 = C.sb(A, "ss1", [128, 2], F32)
                        st2 = C.sb(A, "st2", [128, 4], F32)
                        rs2 = C.sb(A, "rs2", [128, 2], F32)
                        S.dve(_I("memset", st2[:], 1.0), writes=[st2.b])
                        rstd1 = C.sb(A, "rstd1", [128, 1], F32)
                        stb = (xt, hn, junk, ss1, rstd1, pT, hnT)
                        hn_next = {}
                        pc0 = precast(0)
                        for t in range(32):
                            for _ in range(2):
                                if pc0:
                                    pc0.pop(0)()
                            own = t >= 16
                            if t == 0:
                                hn_next[0] = hn_tile(stb, 0)
                            hnT_ = hn_next.pop(t)
                            if t + 1 < 32:
                                hn_next[t + 1] = hn_tile(stb, t + 1)
                            c0 = t * 128
                            for kc in range(8):
                                S.pe(_I("matmul", pKV[:, 0:288], lhsT=hnT_[:, kc * 128:(kc + 1) * 128],
                                                               rhs=wA[:, kc, 384:672], start=(kc == 0), stop=(kc == 7)),
                                     reads=[hnT_.b, wA.b], writes=[pKV.b])
                            if own:
                                for kc in range(8):
                                    S.pe(_I("matmul", pQ[:, 0:384], lhsT=hnT_[:, kc * 128:(kc + 1) * 128],
                                                                   rhs=wA[:, kc, 0:384], start=(kc == 0), stop=(kc == 7)),
                                         reads=[hnT_.b, wA.b], writes=[pQ.b])
                            S.act(_I("activation", out=junk[:, 0:256], in_=pKV[:, 0:256], func=AF.Square, scale=256.0 ** -0.5,
                                     accum_out=st2[:, 0:1]),
                                  reads=[pKV.b], writes=[junk.b, st2.b])
                            if own:
                                S.act(_I("activation", out=junk[:, 256:640], in_=pQ[:, 0:384], func=AF.Square, scale=384.0 ** -0.5,
                                         accum_out=st2[:, 1:2]),
                                      reads=[pQ.b], writes=[junk.b, st2.b])
                            S.act(_I("activation", out=st2[:, 2:4], in_=st2[:, 0:2], func=AF.Sqrt, bias=epsc[:, 0:1], scale=1.0),
                                  reads=[st2.b, epsc.b], writes=[st2.b])
                            S.dve(_I("reciprocal", out=rs2[:, 0:2], in_=st2[:, 2:4]), reads=[st2.b], writes=[rs2.b])
                            S.dve(_I("scalar_tensor_tensor", out=ckvn[:], in0=pKV[:, 0:256], scalar=rs2[:, 0:1], in1=kvn[:],
                                     op0=ALU.mult, op1=ALU.mult),
                                  reads=[pKV.b, rs2.b, kvn.b], writes=[ckvn.b])
                            if own:
                                S.dve(_I("scalar_tensor_tensor", out=cqn[:], in0=pQ[:, 0:384], scalar=rs2[:, 1:2], in1=qn[:],
                                         op0=ALU.mult, op1=ALU.mult),
                                      reads=[pQ.b, rs2.b, qn.b], writes=[cqn.b])
                            rope(pKV[:, 256:288].rearrange("p (h d) -> p h d", d=32), [pKV.b], 1, 32,
                                 cos32[:, t:t + 1, :], sin32[:, t:t + 1, :], [cos32.b, sin32.b], t1, t2,
                                 kpe[:].rearrange("p (h d) -> p h d", d=32), [kpe.b])
                            for c in range(2):
                                S.pe(_I("transpose", out=pT2[:, c * 128:(c + 1) * 128], in_=ckvn[:, c * 128:(c + 1) * 128], identity=ident[:]),
                                     reads=[ckvn.b, ident.b], writes=[pT2.b])
                            if own:
                                for c in range(3):
                                    S.pe(_I("transpose", out=pT2[:, 384 + c * 128:384 + (c + 1) * 128], in_=cqn[:, c * 128:(c + 1) * 128],
                                            identity=ident[:]),
                                         reads=[cqn.b, ident.b], writes=[pT2.b])
                            S.pe(_I("transpose", out=pT2[0:32, 256:384], in_=kpe[:], identity=ident[:]),
                                 reads=[kpe.b, ident.b], writes=[pT2.b])
                            S.dve(_I("tensor_copy", out=ckvnT[:, :, c0:c0 + 128], in_=pT2[:, 0:256].rearrange("p (c t) -> p c t", t=128)),
                                  reads=[pT2.b], writes=[ckvnT.b])
                            S.dve(_I("tensor_copy", out=kpeT[0:32, c0:c0 + 128], in_=pT2[0:32, 256:384]), reads=[pT2.b], writes=[kpeT.b])
                            if own:
                                o0 = (t - 16) * 128
                                S.dve(_I("tensor_copy", out=cqnT[:, :, o0:o0 + 128], in_=pT2[:, 384:768].rearrange("p (c t) -> p c t", t=128)),
                                      reads=[pT2.b], writes=[cqnT.b])
                        S.flush(sched=SCHED.get("mlaprojA", False))
                        S.dve(_I("tensor_copy", out=Vall[:, :, :, 64], in_=kones[:].unsqueeze(2).to_broadcast([128, 32, 8])),
                              reads=[kones.b], writes=[Vall.b])
                        for t in range(32):
                            c0 = t * 128
                            for kc in range(2):
                                S.pe(_I("matmul", pV[:], lhsT=ckvnT[:, kc, c0:c0 + 128], rhs=wukv[:, kc, 512:1024],
                                                               start=(kc == 0), stop=(kc == 1)),
                                     reads=[ckvnT.b, wukv.b], writes=[pV.b])
                            S.act(_I("activation", out=Vall[:, t, :, 0:64], in_=pV[:].rearrange("p (h d) -> p h d", d=64),
                                                         func=AF.Copy),
                                  reads=[pV.b], writes=[Vall.b])
                        for t in range(16):
                            o0 = t * 128
                            for kc in range(3):
                                S.pe(_I("matmul", pQ[:], lhsT=cqnT[:, kc, o0:o0 + 128], rhs=wuq[:, kc, 0:512],
                                                               start=(kc == 0), stop=(kc == 2)),
                                     reads=[cqnT.b, wuq.b], writes=[pQ.b])
                            for kc in range(3):
                                S.pe(_I("matmul", pKV[:, 0:256], lhsT=cqnT[:, kc, o0:o0 + 128], rhs=wuq[:, kc, 512:768],
                                                               start=(kc == 0), stop=(kc == 2)),
                                     reads=[cqnT.b, wuq.b], writes=[pKV.b])
                            S.act(_I("activation", out=qbf[:, :, 0:64], in_=pQ[:].rearrange("p (h d) -> p h d", d=64), func=AF.Copy),
                                  reads=[pQ.b], writes=[qbf.b])
                            rope(pKV[:, 0:256].rearrange("p (h d) -> p h d", d=32), [pKV.b], 8, 32,
                                 cos32[:, 16 + t:17 + t, :].to_broadcast([128, 8, 32]),
                                 sin32[:, 16 + t:17 + t, :].to_broadcast([128, 8, 32]), [cos32.b, sin32.b], t1, t2,
                                 qbf[:, :, 64:96], [qbf.b])
                            for h in range(8):
                                S.pe(_I("transpose", out=pT2[0:96, h * 128:(h + 1) * 128], in_=qbf[:, h, :], identity=ident[:]),
                                     reads=[qbf.b, ident.b], writes=[pT2.b])
                            S.act(_I("activation", out=QT[0:96, :, o0:o0 + 128],
                                                         in_=pT2[0:96, :].rearrange("p (h t) -> p h t", t=128), func=AF.Copy),
                                  reads=[pT2.b], writes=[QT.b])
                        S.flush(sched=SCHED.get("mlaprojB", False))
                    with contextlib.ExitStack() as Cc:
                        KT = [C.sb(Cc, "KT", [96, 4096], BF16) for _ in range(2)]
                        PT = [C.sb(Cc, "PT", [128, 512], BF16) for _ in range(3)]
                        rrow = [C.sb(Cc, "rrow", [128, 512], F32) for _ in range(2)]
                        deferred = []
                        rb = C.sb(Cc, "rb", [64, 512], F32)
                        pS = [C.ps(Cc, "pS", [128, 512], F32) for _ in range(3)]
                        pO = [C.ps(Cc, "pO", [128, 512], F32) for _ in range(2)]
                        pK = [C.ps(Cc, "pK", [64, 512], F32) for _ in range(2)]
                        pB = C.ps(Cc, "pB", [64, 512], F32)
                        for i in range(2):
                            S.act(_I("activation", out=KT[i][64:96, :], in_=kpeT[0:32, :], func=AF.Copy),
                                  reads=[kpeT.b], writes=[KT[i].b])
                        scale = 96.0 ** -0.5
                        pc1 = precast(1)

                        def build_kt(h):
                            KT_ = KT[h % 2]
                            for g in range(8):
                                pK_ = pK[g % 2]
                                for kc in range(2):
                                    S.pe(_I("matmul", pK_[:], lhsT=wukv[:, kc, h * 64:(h + 1) * 64], rhs=ckvnT[:, kc, g * 512:(g + 1) * 512],
                                            start=(kc == 0), stop=(kc == 1)),
                                         reads=[wukv.b, ckvnT.b], writes=[pK_.b])
                                S.dve(_I("tensor_copy", out=KT_[0:64, g * 512:(g + 1) * 512], in_=pK_[:]), reads=[pK_.b], writes=[KT_.b])

                        def mla_S(i, h, qg, kb, cc):
                            KT_ = KT[h % 2]
                            q0 = qg * 512
                            pS_ = pS[i % 3]
                            S.pe(_I("matmul", pS_[:, cc:512], lhsT=KT_[0:96, kb * 128:(kb + 1) * 128], rhs=QT[0:96, h, q0 + cc:q0 + 512],
                                    start=True, stop=True),
                                 reads=[KT_.b, QT.b], writes=[pS_.b])

                        def mla_rest(i, h, qg, kb, cc, diag, nkb, gi):
                            q0 = qg * 512
                            pS_, PT_, pO_ = pS[i % 3], PT[i % 3], pO[gi % 2]
                            if kb == 0 and qg == 0 and h + 1 < 8:
                                build_kt(h + 1)
                            S.act(_I("activation", out=PT_[:, cc:512], in_=pS_[:, cc:512], func=AF.Exp, scale=scale),
                                  reads=[pS_.b], writes=[PT_.b])
                            if diag:
                                S.pool(_I("tensor_tensor", out=PT_[:, cc:cc + 128], in0=PT_[:, cc:cc + 128], in1=maskT[:], op=ALU.mult),
                                       reads=[PT_.b, maskT.b], writes=[PT_.b])
                            S.pe(_I("matmul", pO_[0:65, cc:512], lhsT=Vall[:, kb, h, :], rhs=PT_[:, cc:512], start=(kb == 0), stop=(kb == nkb - 1)),
                                 reads=[Vall.b, PT_.b], writes=[pO_.b])
                            if kb == nkb - 1:
                                rr_ = rrow[gi % 2]
                                S.act(_I("activation", out=rr_[64:65, :], in_=pO_[64:65, :], func=AF.Ln), reads=[pO_.b], writes=[rr_.b])
                                S.act(_I("activation", out=rr_[64:65, :], in_=rr_[64:65, :], func=AF.Exp, scale=-1.0), reads=[rr_.b], writes=[rr_.b])
                                deferred.append((i + 4, partial(mla_epi2, h, qg, gi)))

                        def mla_epi2(h, qg, gi):
                            q0 = qg * 512
                            pO_, rr_ = pO[gi % 2], rrow[gi % 2]
                            S.pe(_I("matmul", pB[:], lhsT=ones32[64:65, 0:64], rhs=rr_[64:65, :], start=True, stop=True),
                                 reads=[ones32.b, rr_.b], writes=[pB.b])
                            S.dve(_I("tensor_copy", out=rb[:], in_=pB[:]), reads=[pB.b], writes=[rb.b])
                            r0 = (h % 2) * 64
                            S.dve(_I("tensor_tensor", out=OT[r0:r0 + 64, h // 2, q0:q0 + 512], in0=pO_[0:64, :], in1=rb[:], op=ALU.mult),
                                  reads=[pO_.b, rb.b], writes=[OT.b])

                        tasks = []
                        gi = 0
                        for h in range(8):
                            for qg in range(4):
                                nkb = 16 + 4 * qg + 4
                                for kb in range(nkb):
                                    j = kb - (16 + 4 * qg)
                                    cc = 0 if j < 0 else j * 128
                                    i = len(tasks)
                                    tasks.append((partial(mla_S, i, h, qg, kb, cc), partial(mla_rest, i, h, qg, kb, cc, j >= 0, nkb, gi)))
                                gi += 1
                        build_kt(0)
                        LOOK = 2
                        for i in range(min(LOOK, len(tasks))):
                            tasks[i][0]()
                        for i in range(len(tasks)):
                            if i + LOOK < len(tasks):
                                tasks[i + LOOK][0]()
                            tasks[i][1]()
                            while deferred and deferred[0][0] <= i:
                                deferred.pop(0)[1]()
                            if i % 12 == 0 and pc1:
                                pc1.pop(0)()
                        while deferred:
                            deferred.pop(0)[1]()
                        while pc1:
                            pc1.pop(0)()
                        S.flush(sched=SCHED.get("mlaattn", False))

                with contextlib.ExitStack() as Dd:
                    dqT = [C.sb(Dd, "dqT", [128, 4, TOK], BF16) for _ in range(2)]
                    S.pool(_I("memset", dqT[0][64:128, :, :], 0.0), writes=[dqT[0].b])
                    S.pool(_I("memset", dqT[1][0:64, :, :], 0.0), writes=[dqT[1].b])
                    dkT = C.sb(Dd, "dkT", [128, 4, 4096], BF16)
                    dvb = C.sb(Dd, "dvb", [128, 32, 512], BF16)
                    lamc = C.sb(Dd, "lamc", [128, 4], F32)
                    subl = C.sb(Dd, "subl", [128, 1], F32)
                    with contextlib.ExitStack() as A:
                        wD = C.sb(A, "wD", [128, 8, 1536], BF16)
                        cos64 = C.sb(A, "cos64", [128, 32, 64], F32)
                        sin64 = C.sb(A, "sin64", [128, 32, 64], F32)
                        lam_in = C.sb(A, "lam_in", [128, 4, 64], F32)
                        S.dma("pool", _I("dma_start", out=wD[:], in_=awin_d.rearrange("(c p) n -> p c n", p=128)[:, :, 672:2208]),
                              writes=[wD.b])
                        S.dma("sp", _I("dma_start", out=cos64[:], in_=cos64_d), writes=[cos64.b])
                        S.dma("sp", _I("dma_start", out=sin64[:], in_=sin64_d), writes=[sin64.b])
                        S.dma("sp", _I("dma_start", out=lam_in[:].rearrange("p a d -> p (a d)"),
                                                          in_=alam_d.rearrange("a d -> (a d)").partition_broadcast(128)),
                              writes=[lam_in.b])
                        S.dma("sp", _I("dma_start", out=subl[:], in_=asub_d), writes=[subl.b])
                        xt = [C.sb(A, "xt", [128, D], F32) for _ in range(2)]
                        hn = [C.sb(A, "hn", [128, D], BF16) for _ in range(2)]
                        junk = C.sb(A, "junk", [128, D], F32)
                        ss = C.sb(A, "ss", [128, 2], F32)
                        rstd = C.sb(A, "rstd", [128, 1], F32)
                        hnT = [C.sb(A, "hnT", [128, D], BF16) for _ in range(2)]
                        t1 = C.sb(A, "t1", [128, 512], F32)
                        t2 = C.sb(A, "t2", [128, 512], F32)
                        dqb = C.sb(A, "dqb", [128, 512], BF16)
                        dkb = C.sb(A, "dkb", [128, 512], BF16)
                        pT = [C.ps(A, "pT", [128, D], BF16) for _ in range(2)]
                        pq = C.ps(A, "pq", [128, 512], F32)
                        pk = C.ps(A, "pk", [128, 512], F32)
                        pv = C.ps(A, "pv", [128, 512], F32)
                        pT2 = C.ps(A, "pT2", [128, D], BF16)
                        ss1 = C.sb(A, "ss1", [128, 2], F32)
                        st2 = C.sb(A, "st2", [128, 4], F32)
                        rs2 = C.sb(A, "rs2", [128, 2], F32)
                        S.dve(_I("memset", st2[:], 1.0), writes=[st2.b])
                        rstd1 = C.sb(A, "rstd1", [128, 1], F32)
                        stb = (xt, hn, junk, ss1, rstd1, pT, hnT)
                        hn_next = {}
                        S.dve(_I("tensor_tensor", out=t1[:, 0:64], in0=lam_in[:, 0, :], in1=lam_in[:, 1, :], op=ALU.mult),
                              reads=[lam_in.b], writes=[t1.b])
                        S.dve(_I("tensor_tensor", out=t1[:, 64:128], in0=lam_in[:, 2, :], in1=lam_in[:, 3, :], op=ALU.mult),
                              reads=[lam_in.b], writes=[t1.b])
                        S.dve(_I("reduce_sum", out=lamc[:, 0:2], in_=t1[:, 0:128].rearrange("p (a d) -> p a d", d=64), axis=AX.X),
                              reads=[t1.b], writes=[lamc.b])
                        S.act(_I("activation", out=lamc[:, 0:2], in_=lamc[:, 0:2], func=AF.Exp), reads=[lamc.b], writes=[lamc.b])
                        S.dve(_I("tensor_tensor", out=lamc[:, 2:3], in0=lamc[:, 1:2], in1=lamc[:, 0:1], op=ALU.subtract),
                              reads=[lamc.b], writes=[lamc.b])
                        S.dve(_I("tensor_scalar_add", out=lamc[:, 2:3], in0=lamc[:, 2:3], scalar1=-0.2), reads=[lamc.b], writes=[lamc.b])
                        S.dve(_I("tensor_scalar_mul", out=subl[:], in0=subl[:], scalar1=0.8), reads=[subl.b], writes=[subl.b])
                        for t in range(32):
                            own = t >= 16
                            if t == 0:
                                hn_next[0] = hn_tile(stb, 0)
                            hnT_ = hn_next.pop(t)
                            if t + 1 < 32:
                                hn_next[t + 1] = hn_tile(stb, t + 1)
                            c0 = t * 128
                            for kc in range(8):
                                S.pe(_I("matmul", pk[:], lhsT=hnT_[:, kc * 128:(kc + 1) * 128], rhs=wD[:, kc, 512:1024],
                                                               start=(kc == 0), stop=(kc == 7)),
                                     reads=[hnT_.b, wD.b], writes=[pk.b])
                            for kc in range(8):
                                S.pe(_I("matmul", pv[:], lhsT=hnT_[:, kc * 128:(kc + 1) * 128], rhs=wD[:, kc, 1024:1536],
                                                               start=(kc == 0), stop=(kc == 7)),
                                     reads=[hnT_.b, wD.b], writes=[pv.b])
                            S.dve(_I("tensor_copy", out=dvb[:, t, :], in_=pv[:]), reads=[pv.b], writes=[dvb.b])
                            cb = cos64[:, t:t + 1, :].to_broadcast([128, 8, 64])
                            sbb = sin64[:, t:t + 1, :].to_broadcast([128, 8, 64])
                            rope(pk[:].rearrange("p (h d) -> p h d", d=64), [pk.b], 8, 64, cb, sbb, [cos64.b, sin64.b], t1, t2,
                                 dkb[:].rearrange("p (h d) -> p h d", d=64), [dkb.b])
                            for c in range(4):
                                S.pe(_I("transpose", out=pT2[:, c * 128:(c + 1) * 128], in_=dkb[:, c * 128:(c + 1) * 128],
                                                                identity=ident[:]),
                                     reads=[dkb.b, ident.b], writes=[pT2.b])
                            S.dve(_I("tensor_copy", out=dkT[:, :, c0:c0 + 128], in_=pT2[:, 0:512].rearrange("p (c t) -> p c t", t=128)),
                                  reads=[pT2.b], writes=[dkT.b])
                            if own:
                                o0 = (t - 16) * 128
                                for kc in range(8):
                                    S.pe(_I("matmul", pq[:], lhsT=hnT_[:, kc * 128:(kc + 1) * 128], rhs=wD[:, kc, 0:512],
                                                                   start=(kc == 0), stop=(kc == 7)),
                                         reads=[hnT_.b, wD.b], writes=[pq.b])
                                rope(pq[:].rearrange("p (h d) -> p h d", d=64), [pq.b], 8, 64, cb, sbb, [cos64.b, sin64.b], t1, t2,
                                     dqb[:].rearrange("p (h d) -> p h d", d=64), [dqb.b])
                                for c in range(4):
                                    S.pe(_I("transpose", out=pT2[:, 512 + c * 128:512 + (c + 1) * 128],
                                                                    in_=dqb[:, c * 128:(c + 1) * 128], identity=ident[:]),
                                         reads=[dqb.b, ident.b], writes=[pT2.b])
                                for s_ in range(2):
                                    S.act(_I("activation", out=dqT[s_][s_ * 64:(s_ + 1) * 64, :, o0:o0 + 128],
                                             in_=pT2[s_ * 64:(s_ + 1) * 64, 512:1024].rearrange("p (c t) -> p c t", t=128), func=AF.Copy),
                                          reads=[pT2.b], writes=[dqT[s_].b])
                        S.flush(sched=SCHED.get("dproj", False))
                    with contextlib.ExitStack() as Ee:
                        PT = [C.sb(Ee, "PT", [128, 512], BF16) for _ in range(4)]
                        rb = C.sb(Ee, "rb", [128, 512], F32)
                        o0s = C.sb(Ee, "o0s", [128, 512], F32)
                        ods = [C.sb(Ee, "od", [128, 512], F32) for _ in range(2)]
                        sqs = [C.sb(Ee, "sq", [128, 512], F32) for _ in range(2)]
                        deferred = []
                        pS = [C.ps(Ee, "pS", [128, 512], F32) for _ in range(3)]
                        pO = [C.ps(Ee, "pO", [128, 512], F32) for _ in range(2)]
                        pZ = [C.ps(Ee, "pZ", [128, 512], F32) for _ in range(2)]
                        pB = C.ps(Ee, "pB", [128, 512], F32)
                        scale = 64.0 ** -0.5

                        def d_S(i, h, s_, qg, kb, cc):
                            r0 = s_ * 64
                            q0 = qg * 512
                            pS_ = pS[i % 3]
                            S.pe(_I("matmul", pS_[:, cc:512], lhsT=dkT[:, h, kb * 128:(kb + 1) * 128],
                                    rhs=dqT[s_][:, h, q0 + cc:q0 + 512], start=True, stop=True),
                                 reads=[dkT.b, dqT[s_].b], writes=[pS_.b])

                        def d_rest(i, h, s_, qg, kb, cc, diag, nkb, gi):
                            q0 = qg * 512
                            pS_, PT_, pO_, pZ_ = pS[i % 3], PT[i % 4], pO[s_], pZ[s_]
                            S.act(_I("activation", out=PT_[:, cc:512], in_=pS_[:, cc:512], func=AF.Exp, scale=scale),
                                  reads=[pS_.b], writes=[PT_.b])
                            if diag:
                                S.pool(_I("tensor_tensor", out=PT_[:, cc:cc + 128], in0=PT_[:, cc:cc + 128], in1=maskT[:], op=ALU.mult),
                                       reads=[PT_.b, maskT.b], writes=[PT_.b])
                            S.pe(_I("matmul", pO_[:, cc:512], lhsT=dvb[:, kb, h * 128:(h + 1) * 128], rhs=PT_[:, cc:512],
                                    start=(kb == 0), stop=(kb == nkb - 1)),
                                 reads=[dvb.b, PT_.b], writes=[pO_.b])
                            S.pe(_I("matmul", pZ_[:, cc:512], lhsT=konesb[:, kb:kb + 1].to_broadcast([128, 128]), rhs=PT_[:, cc:512],
                                    start=(kb == 0), stop=(kb == nkb - 1)),
                                 reads=[konesb.b, PT_.b], writes=[pZ_.b])
                            if kb != nkb - 1:
                                return
                            S.act(_I("activation", out=rb[:], in_=pZ_[:], func=AF.Ln), reads=[pZ_.b], writes=[rb.b])
                            S.act(_I("activation", out=rb[:], in_=rb[:], func=AF.Exp, scale=-1.0), reads=[rb.b], writes=[rb.b])
                            if s_ == 0:
                                S.dve(_I("tensor_tensor", out=o0s[:], in0=pO_[:], in1=rb[:], op=ALU.mult),
                                      reads=[pO_.b, rb.b], writes=[o0s.b])
                                return
                            od = ods[gi % 2]
                            sq = sqs[gi % 2]
                            S.dve(_I("tensor_tensor", out=od[:], in0=pO_[:], in1=rb[:], op=ALU.mult), reads=[pO_.b, rb.b], writes=[od.b])
                            S.dve(_I("scalar_tensor_tensor", out=od[:], in0=od[:], scalar=lamc[:, 2:3], in1=o0s[:], op0=ALU.mult, op1=ALU.add),
                                  reads=[od.b, lamc.b, o0s.b], writes=[od.b])
                            S.dve(_I("tensor_tensor", out=sq[:], in0=od[:], in1=od[:], op=ALU.mult), reads=[od.b], writes=[sq.b])
                            deferred.append((i + 6, partial(d_epi2, h, qg, gi)))

                        def d_epi2(h, qg, gi):
                            q0 = qg * 512
                            od, sq = ods[gi % 2], sqs[gi % 2]
                            S.pe(_I("matmul", pB[:], lhsT=ones32[:], rhs=sq[:], start=True, stop=True), reads=[ones32.b, sq.b], writes=[pB.b])
                            S.act(_I("activation", out=sq[:], in_=pB[:], func=AF.Ln, scale=1.0 / 128, bias=epsc[:, 0:1]),
                                  reads=[pB.b, epsc.b], writes=[sq.b])
                            S.act(_I("activation", out=sq[:], in_=sq[:], func=AF.Exp, scale=-0.5), reads=[sq.b], writes=[sq.b])
                            S.dve(_I("scalar_tensor_tensor", out=OT[:, 4 + h, q0:q0 + 512], in0=od[:], scalar=subl[:, 0:1], in1=sq[:],
                                     op0=ALU.mult, op1=ALU.mult),
                                  reads=[od.b, subl.b, sq.b], writes=[OT.b])

                        tasks = []
                        gi = 0
                        for h in range(4):
                            for qg in range(4):
                                nkb = 16 + 4 * qg + 4
                                for kb in range(nkb):
                                    j = kb - (16 + 4 * qg)
                                    cc = 0 if j < 0 else j * 128
                                    for s_ in range(2):
                                        i = len(tasks)
                                        tasks.append((partial(d_S, i, h, s_, qg, kb, cc), partial(d_rest, i, h, s_, qg, kb, cc, j >= 0, nkb, gi)))
                                gi += 1
                        LOOK = 2
                        for i in range(min(LOOK, len(tasks))):
                            tasks[i][0]()
                        for i in range(len(tasks)):
                            if i + LOOK < len(tasks):
                                tasks[i + LOOK][0]()
                            tasks[i][1]()
                            while deferred and deferred[0][0] <= i:
                                deferred.pop(0)[1]()
                        while deferred:
                            deferred.pop(0)[1]()
                        S.flush(sched=SCHED.get("dattn", False))

        def outproj0():
            with contextlib.ExitStack() as Ff:
                wo = C.sb(Ff, "wo", [128, 8, D], BF16)
                xt = [C.sb(Ff, "xt", [128, D], F32) for _ in range(2)]
                pm = [C.ps(Ff, "pm", [128, 512], F32) for _ in range(4)]
                S.dma("pool", _I("dma_start", out=wo[:], in_=awout_d.rearrange("(c p) n -> p c n", p=128)), writes=[wo.b])
                for t in range(16):
                    xt_ = xt[t % 2]
                    S.dma("sp", _I("dma_start", out=xt_[:], in_=x_d[t * 128:(t + 1) * 128, :]), writes=[xt_.b])
                    for hf in range(2):
                        pm_ = pm[(2 * t + hf) % 4]
                        for kc in range(8):
                            S.pe(_I("matmul", pm_[:], lhsT=OT[:, kc, t * 128:(t + 1) * 128],
                                                                                rhs=wo[:, kc, hf * 512:(hf + 1) * 512],
                                                                                start=(kc == 0), stop=(kc == 7)),
                                 reads=[OT.b, wo.b], writes=[pm_.b])
                        S.dve(_I("tensor_tensor", out=xres[:, t, hf * 512:(hf + 1) * 512],
                                                                                      in0=pm_[:], in1=xt_[:, hf * 512:(hf + 1) * 512],
                                                                                      op=ALU.add),
                              reads=[pm_.b, xt_.b], writes=[xres.b])
                S.flush(sched=SCHED.get("outproj0", False))


        def ffn(layer):
            with contextlib.ExitStack() as Ff:
                nw = C.sb(Ff, "nw", [128, D], F32)
                S.dma("sp", _I("dma_start", out=nw[:], in_=nffn_d[layer].partition_broadcast(128)), writes=[nw.b])
                junk = C.sb(Ff, "junk", [128, D], F32)
                ss = C.sb(Ff, "ss", [128, 2], F32)
                rstd = C.sb(Ff, "rstd", [128, 1], F32)
                hn = [C.sb(Ff, "hn", [128, D], BF16) for _ in range(2)]
                hnTs = [C.sb(Ff, "hnT", [128, 8, 512], BF16) for _ in range(2)]
                hT = C.sb(Ff, "hT", [128, 22, 512], BF16)
                wg = [C.sb(Ff, "wg", [128, 8, 256], BF16) for _ in range(3)]
                wd = [C.sb(Ff, "wd", [128, D], BF16) for _ in range(3)]
                sg = [C.sb(Ff, "sg", [128, 512], F32) for _ in range(2)]
                bk = [C.ps(Ff, "bk", [128, 512], F32) for _ in range(8)]
                it = 0

                def prologue(g, tt):
                    t = g * 4 + tt
                    hnT = hnTs[g % 2]
                    rms_stats(S, xres[:, t, :], [xres.b], D, junk, ss, rstd, epsc)
                    hn_ = hn[t % 2]
                    S.dve(_I("scalar_tensor_tensor", out=hn_[:], in0=xres[:, t, :], scalar=rstd[:, 0:1], in1=nw[:],
                             op0=ALU.mult, op1=ALU.mult),
                          reads=[xres.b, rstd.b, nw.b], writes=[hn_.b])
                    pT_ = bk[t % 2]
                    pTv = pT_[:].bitcast(BF16)
                    for kc in range(8):
                        S.pe(_I("transpose", out=pTv[:, kc * 128:(kc + 1) * 128],
                                in_=hn_[:, kc * 128:(kc + 1) * 128], identity=ident[:]),
                             reads=[hn_.b, ident.b], writes=[pT_.b])
                    S.act(_I("activation", out=hnT[:, :, tt * 128:(tt + 1) * 128],
                             in_=pTv.rearrange("p (c t) -> p c t", t=128), func=AF.Copy),
                          reads=[pT_.b], writes=[hnT.b])

                for tt in range(4):
                    prologue(0, tt)
                for g in range(4):
                    hnT = hnTs[g % 2]
                    for c in range(22):
                        if g + 1 < 4 and c in (3, 8, 13, 18):
                            prologue(g + 1, (c - 3) // 5)
                        wg_ = wg[it % 3]
                        S.dma("sp", _I("dma_start", out=wg_[:], in_=gu_d[layer, c]), reads=[gub[layer]], writes=[wg_.b])
                        pg_, pu_, sg_ = bk[2 + it % 2], bk[4 + it % 2], sg[it % 2]
                        it += 1
                        for kc in range(8):
                            S.pe(_I("matmul", pg_[:], lhsT=wg_[:, kc, 0:128], rhs=hnT[:, kc, :], start=(kc == 0), stop=(kc == 7)),
                                 reads=[wg_.b, hnT.b], writes=[pg_.b])
                        for kc in range(8):
                            S.pe(_I("matmul", pu_[:], lhsT=wg_[:, kc, 128:256], rhs=hnT[:, kc, :], start=(kc == 0), stop=(kc == 7)),
                                 reads=[wg_.b, hnT.b], writes=[pu_.b])
                        S.act(_I("activation", out=sg_[:], in_=pg_[:], func=AF.Silu), reads=[pg_.b], writes=[sg_.b])
                        S.dve(_I("tensor_tensor", out=hT[:, c, :], in0=pu_[:], in1=sg_[:], op=ALU.mult),
                              reads=[pu_.b, sg_.b], writes=[hT.b])
                    for c in range(22):
                        wd_ = wd[c % 3]
                        S.dma("sp", _I("dma_start", out=wd_[:], in_=wd_d[layer, c * 128:(c + 1) * 128, :]), reads=[wdb[layer]], writes=[wd_.b])
                        for tt in range(4):
                            for hf in range(2):
                                pd_ = bk[tt * 2 + hf]
                                S.pe(_I("matmul", pd_[:], lhsT=hT[:, c, tt * 128:(tt + 1) * 128], rhs=wd_[:, hf * 512:(hf + 1) * 512],
                                        start=(c == 0), stop=(c == 21)),
                                     reads=[hT.b, wd_.b], writes=[pd_.b])
                    for tt in range(4):
                        t = g * 4 + tt
                        for hf in range(2):
                            pd_ = bk[tt * 2 + hf]
                            S.dve(_I("tensor_tensor", out=xres[:, t, hf * 512:(hf + 1) * 512], in0=pd_[:],
                                     in1=xres[:, t, hf * 512:(hf + 1) * 512], op=ALU.add),
                                  reads=[pd_.b, xres.b], writes=[xres.b])
                S.flush(sched=SCHED.get("ffn", False))


        def layer1():
            with contextlib.ExitStack() as L1:
                Sm = C.sb(L1, "Sm", [128, 512], F32)
                Sb = C.sb(L1, "Sb", [128, 512], BF16)
                Hm = C.sb(L1, "Hm", [128, 512], F32)
                Hb = C.sb(L1, "Hb", [128, 512], BF16)
                halo0 = C.sb(L1, "halo0", [128, 8, 3], F32)
                haloo = C.sb(L1, "haloo", [128, 8, 3], F32)
                recv = C.sb(L1, "recv", [128, 1048], F32)
                flag = C.sb(L1, "flag", [128, 1], F32)
                nm1 = C.sb(L1, "nm1", [128, D], F32)
                triU = C.sb(L1, "triU", [128, 128], F32)
                ident32 = C.sb(L1, "ident32", [128, 128], F32)
                onec = C.sb(L1, "onec", [128, 1], F32)
                cw = C.sb(L1, "cw", [128, 8, 4], F32)
                cbias = C.sb(L1, "cbias", [128, 8], F32)
                dtb = C.sb(L1, "dtb", [128, 8], F32)
                ahead = C.sb(L1, "ahead", [128, 8], F32)
                dsk = C.sb(L1, "dsk", [128, 8], F32)
                snw = C.sb(L1, "snw", [128, 512], F32)
                gnw = C.sb(L1, "gnw", [128, 128], F32)
                hlb = C.sb(L1, "hlb", [128, 2, 4], F32)
                lb = C.sb(L1, "lb", [128, 4], F32)
                oml = C.sb(L1, "oml", [128, 4], F32)
                rmask = C.sb(L1, "rmask", [128, 512], F32)
                payb = S.buf("pay")
                gathb = S.buf("gath")
                S.dma("sp", _I("dma_start", out=flag[:], in_=flag_d), writes=[flag.b])
                S.dma("sp", _I("dma_start", out=nm1[:], in_=nmix_d[1].partition_broadcast(128)), writes=[nm1.b])
                S.dma("sp", _I("dma_start", out=triU[:], in_=maskT_d), writes=[triU.b])
                S.dma("sp", _I("dma_start", out=ident32[:], in_=ident_d), writes=[ident32.b])
                S.dma("sp", _I("dma_start", out=cw[:], in_=convw_d), writes=[cw.b])
                S.dma("sp", _I("dma_start", out=cbias[:], in_=convb_d), writes=[cbias.b])
                S.dma("sp", _I("dma_start", out=dtb[:], in_=sdtb_d.partition_broadcast(128)), writes=[dtb.b])
                S.dma("sp", _I("dma_start", out=ahead[:], in_=salog_d.partition_broadcast(128)), writes=[ahead.b])
                S.dma("sp", _I("dma_start", out=dsk[:], in_=sd_d.partition_broadcast(128)), writes=[dsk.b])
                S.dma("sp", _I("dma_start", out=snw[:], in_=snorm_d.partition_broadcast(128)), writes=[snw.b])
                S.dma("sp", _I("dma_start", out=gnw[:], in_=hgn_d.partition_broadcast(128)), writes=[gnw.b])
                S.dma("sp", _I("dma_start", out=hlb[:], in_=hlb_d), writes=[hlb.b])
                S.dma("sp", _I("dma_start", out=rmask[:], in_=rmask_d), writes=[rmask.b])
                S.dve(_I("memset", onec[:], 1.0), writes=[onec.b])
                S.act(_I("activation", out=ahead[:], in_=ahead[:], func=AF.Exp), reads=[ahead.b], writes=[ahead.b])
                S.dve(_I("tensor_scalar_mul", out=ahead[:], in0=ahead[:], scalar1=-1.0), reads=[ahead.b], writes=[ahead.b])
                S.dve(_I("tensor_tensor", out=lb[:], in0=hlb[:, 1, :], in1=hlb[:, 0, :], op=ALU.subtract), reads=[hlb.b], writes=[lb.b])
                S.act(_I("activation", out=lb[:], in_=lb[:], func=AF.Exp, scale=-1.0), reads=[lb.b], writes=[lb.b])
                S.dve(_I("tensor_scalar_add", out=lb[:], in0=lb[:], scalar1=1.0), reads=[lb.b], writes=[lb.b])
                S.dve(_I("reciprocal", out=lb[:], in_=lb[:]), reads=[lb.b], writes=[lb.b])
                S.dve(_I("tensor_scalar", out=oml[:], in0=lb[:], scalar1=-1.0, scalar2=1.0, op0=ALU.mult, op1=ALU.add),
                      reads=[lb.b], writes=[oml.b])
                cwd = C.sb(L1, "cwd", [128, 8, 4, 128], BF16)
                for c in range(8):
                    for j in range(4):
                        S.dve(_I("tensor_scalar_mul", out=cwd[:, c, j, :], in0=ident32[:], scalar1=cw[:, c, j:j + 1]),
                              reads=[ident32.b, cw.b], writes=[cwd.b])
                for tz in (Sm, Hm, halo0):
                    S.dve(_I("memset", tz[:], 0.0), writes=[tz.b])
                S.dve(_I("memset", Sb[:], 0.0), writes=[Sb.b])
                S.dve(_I("memset", Hb[:], 0.0), writes=[Hb.b])
                S.flush(sched=SCHED.get("l1setup", False))

                def hn_tile1(stb, bk, t):
                    junk, ss, rstd, hn, hnT = stb
                    rms_stats(S, xres[:, t, :], [xres.b], D, junk, ss, rstd, epsc, lnexp=True)
                    hn_ = hn[t % 2]
                    S.dve(_I("scalar_tensor_tensor", out=hn_[:], in0=xres[:, t, :], scalar=rstd[:, 0:1], in1=nm1[:],
                             op0=ALU.mult, op1=ALU.mult),
                          reads=[xres.b, rstd.b, nm1.b], writes=[hn_.b])
                    pT_ = bk[t % 2]
                    pTv = pT_[:].bitcast(BF16)
                    for kc in range(8):
                        S.pe(_I("transpose", out=pTv[:, kc * 128:(kc + 1) * 128], in_=hn_[:, kc * 128:(kc + 1) * 128],
                                identity=ident[:]),
                             reads=[hn_.b, ident.b], writes=[pT_.b])
                    hnT_ = hnT[t % 2]
                    S.act(_I("activation", out=hnT_[:], in_=pTv, func=AF.Copy), reads=[pT_.b], writes=[hnT_.b])
                    return hnT_

                swv = swin_d.rearrange("(c p) n -> p c n", p=128)

                def ssd_pass(full, P, junk, bk, hn_cache, bm):
                    if True:
                        Wx = C.sb(P, "Wx", [128, 8, 1032], BF16)
                        for kc in range(8):
                            S.dma("pool", _I("dma_start", out=Wx[:, kc, :], in_=swv[:, kc, 512:1544]), writes=[Wx.b])
                        if full:
                            Wz = C.sb(P, "Wz", [128, 8, 512], BF16)
                            for kc in range(8):
                                S.dma("pool", _I("dma_start", out=Wz[:, kc, :], in_=swv[:, kc, 0:512]), writes=[Wz.b])
                            mask32 = C.sb(P, "mask32", [128, 128], F32)
                            S.dma("sp", _I("dma_start", out=mask32[:], in_=maskT_d), writes=[mask32.b])
                        xc = C.sb(P, "xc", [128, 8, 131], BF16)
                        acc = C.sb(P, "acc", [128, 8, 128], F32)
                        tmp = C.sb(P, "tmp", [128, 8, 128], F32)
                        xs32 = C.sb(P, "xs32", [128, 4, 128], F32)
                        bc16 = C.sb(P, "bc16", [128, 4, 128], BF16)
                        dt = C.sb(P, "dt", [128, 8], F32)
                        av = C.sb(P, "av", [128, 8], F32)
                        nav = C.sb(P, "nav", [128, 8], F32)
                        acs = C.sb(P, "acs", [128, 8], F32)
                        eacs = C.sb(P, "eacs", [128, 8], F32)
                        dst = C.sb(P, "dst", [128, 8], F32)
                        dec = C.sb(P, "dec", [128, 8], F32)
                        xh = C.sb(P, "xh", [128, 512], BF16)
                        xhd = C.sb(P, "xhd", [128, 512], BF16)
                        Btok = C.sb(P, "Btok", [128, 256], BF16)
                        if full:
                            skp = C.sb(P, "skp", [128, 512], F32)
                            cbm = C.sb(P, "cbm", [128, 2, 128], F32)
                            em = C.sb(P, "em", [128, 8, 128], F32)
                            MT = C.sb(P, "MT", [128, 8, 128], BF16)
                            yv = C.sb(P, "yv", [128, 512], F32)
                            sz = C.sb(P, "sz", [128, 512], F32)
                            yn = C.sb(P, "yn", [128, 512], BF16)
                            ssg = C.sb(P, "ssg", [128, 4], F32)
                        S.dve(_I("tensor_copy", out=xc[:, :, 0:3], in_=halo0[:]), reads=[halo0.b], writes=[xc.b])
                        for t in range(16):
                            yield
                            hnT_ = hn_cache[t]
                            pX = (bk[bm["pX0"]], bk[bm["pX1"]])
                            for j in range(8):
                                for kc in range(8):
                                    S.pe(_I("matmul", pX[j // 4][:, (j % 4) * 128:(j % 4 + 1) * 128], lhsT=Wx[:, kc, j * 128:(j + 1) * 128],
                                            rhs=hnT_[:, kc * 128:(kc + 1) * 128], start=(kc == 0), stop=(kc == 7)),
                                         reads=[Wx.b, hnT_.b], writes=[pX[j // 4].b])
                            for i in range(2):
                                S.act(_I("activation", out=xc[:, i * 4:(i + 1) * 4, 3:131],
                                         in_=pX[i][:].rearrange("p (c t) -> p c t", t=128), func=AF.Copy),
                                      reads=[pX[i].b], writes=[xc.b])
                            psm = bk[bm["psm"]]
                            for kc in range(8):
                                S.pe(_I("matmul", psm[:, 0:8], lhsT=hnT_[:, kc * 128:(kc + 1) * 128], rhs=Wx[:, kc, 1024:1032],
                                        start=(kc == 0), stop=(kc == 7)),
                                     reads=[Wx.b, hnT_.b], writes=[psm.b])
                            for c in range(8):
                                for j in range(4):
                                    S.pe(_I("matmul", pX[c // 4][:, (c % 4) * 128:(c % 4 + 1) * 128], lhsT=cwd[:, c, j, :], rhs=xc[:, c, j:j + 128],
                                            start=(j == 0), stop=(j == 3)),
                                         reads=[cwd.b, xc.b], writes=[pX[c // 4].b])
                            for i in range(2):
                                S.dve(_I("tensor_tensor", out=acc[:, i * 4:(i + 1) * 4, :], in0=pX[i][:].rearrange("p (c t) -> p c t", t=128),
                                         in1=cbias[:, i * 4:(i + 1) * 4].unsqueeze(2).to_broadcast([128, 4, 128]), op=ALU.add),
                                      reads=[pX[i].b, cbias.b], writes=[acc.b])
                            if t == 15:
                                S.dve(_I("tensor_copy", out=haloo[:], in_=xc[:, :, 128:131]), reads=[xc.b], writes=[haloo.b])
                            else:
                                S.dve(_I("tensor_copy", out=xc[:, :, 0:3], in_=xc[:, :, 128:131]), reads=[xc.b], writes=[xc.b])
                            S.act(_I("activation", out=tmp[:], in_=acc[:], func=AF.Exp, scale=-1.0), reads=[acc.b], writes=[tmp.b])
                            S.act(_I("activation", out=tmp[:], in_=tmp[:], func=AF.Ln, bias=onec[:, 0:1], scale=1.0), reads=[tmp.b, onec.b], writes=[tmp.b])
                            S.act(_I("activation", out=tmp[:], in_=tmp[:], func=AF.Exp, scale=-1.0), reads=[tmp.b], writes=[tmp.b])
                            S.dve(_I("tensor_tensor", out=xs32[:], in0=acc[:, 0:4, :], in1=tmp[:, 0:4, :], op=ALU.mult), reads=[acc.b, tmp.b], writes=[xs32.b])
                            S.dve(_I("tensor_tensor", out=bc16[:], in0=acc[:, 4:8, :], in1=tmp[:, 4:8, :], op=ALU.mult), reads=[acc.b, tmp.b], writes=[bc16.b])
                            S.dve(_I("tensor_tensor", out=dt[:], in0=psm[:, 0:8], in1=dtb[:], op=ALU.add), reads=[psm.b, dtb.b], writes=[dt.b])
                            S.act(_I("activation", out=dt[:], in_=dt[:], func=AF.Exp), reads=[dt.b], writes=[dt.b])
                            S.act(_I("activation", out=dt[:], in_=dt[:], func=AF.Ln, bias=onec[:, 0:1], scale=1.0), reads=[dt.b, onec.b], writes=[dt.b])
                            S.dve(_I("tensor_tensor", out=av[:], in0=dt[:], in1=ahead[:], op=ALU.mult), reads=[dt.b, ahead.b], writes=[av.b])
                            S.pe(_I("matmul", psm[:, 16:24], lhsT=triU[:], rhs=av[:], start=True, stop=True), reads=[triU.b, av.b], writes=[psm.b])
                            S.pe(_I("matmul", psm[:, 32:40], lhsT=ones32[:], rhs=av[:], start=True, stop=True), reads=[ones32.b, av.b], writes=[psm.b])
                            S.act(_I("activation", out=acs[:], in_=psm[:, 16:24], func=AF.Copy), reads=[psm.b], writes=[acs.b])
                            S.dve(_I("tensor_tensor", out=dst[:], in0=psm[:, 32:40], in1=acs[:], op=ALU.subtract), reads=[psm.b, acs.b], writes=[dst.b])
                            S.act(_I("activation", out=dst[:], in_=dst[:], func=AF.Exp), reads=[dst.b], writes=[dst.b])
                            S.act(_I("activation", out=dec[:], in_=psm[:, 32:40], func=AF.Exp), reads=[psm.b], writes=[dec.b])
                            S.dve(_I("tensor_tensor", out=dst[:], in0=dst[:], in1=dt[:], op=ALU.mult), reads=[dst.b, dt.b], writes=[dst.b])
                            pxs = bk[bm["pxs"]]
                            for j in range(4):
                                S.pe(_I("transpose", out=pxs[:, j * 128:(j + 1) * 128], in_=xs32[:, j, :], identity=ident32[:]),
                                     reads=[xs32.b, ident32.b], writes=[pxs.b])
                            pxs3 = pxs[:].rearrange("p (h d) -> p h d", d=64)
                            S.dve(_I("tensor_tensor", out=xh[:].rearrange("p (h d) -> p h d", d=64), in0=pxs3,
                                     in1=dt[:].unsqueeze(2).to_broadcast([128, 8, 64]), op=ALU.mult),
                                  reads=[pxs.b, dt.b], writes=[xh.b])
                            S.dve(_I("tensor_tensor", out=xhd[:].rearrange("p (h d) -> p h d", d=64), in0=pxs3,
                                     in1=dst[:].unsqueeze(2).to_broadcast([128, 8, 64]), op=ALU.mult),
                                  reads=[pxs.b, dst.b], writes=[xhd.b])
                            if full:
                                S.dve(_I("tensor_tensor", out=skp[:].rearrange("p (h d) -> p h d", d=64), in0=pxs3,
                                         in1=dsk[:].unsqueeze(2).to_broadcast([128, 8, 64]), op=ALU.mult),
                                      reads=[pxs.b, dsk.b], writes=[skp.b])
                            pbt = bk[bm["pbt"]]
                            pbtv = pbt[:].bitcast(BF16)
                            for g in range(2):
                                S.pe(_I("transpose", out=pbtv[:, g * 128:(g + 1) * 128], in_=bc16[:, g, :], identity=ident[:]),
                                     reads=[bc16.b, ident.b], writes=[pbt.b])
                            S.act(_I("activation", out=Btok[:], in_=pbtv[:, 0:256], func=AF.Copy), reads=[pbt.b], writes=[Btok.b])
                            if full:
                                S.act(_I("activation", out=eacs[:], in_=psm[:, 16:24], func=AF.Exp), reads=[psm.b], writes=[eacs.b])
                                S.dve(_I("tensor_scalar_mul", out=nav[:], in0=av[:], scalar1=-1.0), reads=[av.b], writes=[nav.b])
                                pyo = bk[bm["pyo"]]
                                for g in range(2):
                                    S.pe(_I("matmul", pyo[:, g * 256:(g + 1) * 256], lhsT=bc16[:, 2 + g, :], rhs=Sb[:, g * 256:(g + 1) * 256],
                                            start=True, stop=True),
                                         reads=[bc16.b, Sb.b], writes=[pyo.b])
                            pst = bk[bm["pst"]]
                            for g in range(2):
                                S.pe(_I("matmul", pst[:, g * 256:(g + 1) * 256], lhsT=Btok[:, g * 128:(g + 1) * 128], rhs=xhd[:, g * 256:(g + 1) * 256],
                                        start=True, stop=True),
                                     reads=[Btok.b, xhd.b], writes=[pst.b])
                            S.dve(_I("tensor_tensor", out=Sm[:].rearrange("p (h d) -> p h d", d=64), in0=Sm[:].rearrange("p (h d) -> p h d", d=64),
                                     in1=dec[:].unsqueeze(2).to_broadcast([128, 8, 64]), op=ALU.mult),
                                  reads=[Sm.b, dec.b], writes=[Sm.b])
                            S.dve(_I("tensor_tensor", out=Sm[:], in0=Sm[:], in1=pst[:], op=ALU.add), reads=[Sm.b, pst.b], writes=[Sm.b])
                            S.act(_I("activation", out=Sb[:], in_=Sm[:], func=AF.Copy), reads=[Sm.b], writes=[Sb.b])
                            if not full:
                                continue
                            pcb = bk[bm["pcb"]]
                            for g in range(2):
                                S.pe(_I("matmul", pcb[:, 128 + g * 128:128 + (g + 1) * 128], lhsT=bc16[:, g, :], rhs=bc16[:, 2 + g, :], start=True, stop=True),
                                     reads=[bc16.b], writes=[pcb.b])
                            S.dve(_I("tensor_tensor", out=cbm[:], in0=pcb[:, 128:384].rearrange("p (g l) -> p g l", l=128),
                                     in1=mask32[:].unsqueeze(1).to_broadcast([128, 2, 128]), op=ALU.mult),
                                  reads=[pcb.b, mask32.b], writes=[cbm.b])
                            pe_ = (bk[bm["pe0"]], bk[bm["pe1"]])
                            for h in range(8):
                                o_ = pe_[h // 4][:, (h % 4) * 128:(h % 4 + 1) * 128]
                                S.pe(_I("matmul", o_, lhsT=av[:, h:h + 1].to_broadcast([128, 128]), rhs=triU[:], start=True, stop=False),
                                     reads=[av.b, triU.b], writes=[pe_[h // 4].b])
                                S.pe(_I("matmul", o_, lhsT=triU[:], rhs=nav[:, h:h + 1].to_broadcast([128, 128]), start=False, stop=True),
                                     reads=[nav.b, triU.b], writes=[pe_[h // 4].b])
                            for i in range(2):
                                S.dve(_I("tensor_scalar_min", out=em[:, i * 4:(i + 1) * 4, :], in0=pe_[i][:].rearrange("p (h l) -> p h l", l=128),
                                         scalar1=0.0),
                                      reads=[pe_[i].b], writes=[em.b])
                            S.act(_I("activation", out=em[:], in_=em[:], func=AF.Exp), reads=[em.b], writes=[em.b])
                            for g in range(2):
                                S.pool(_I("tensor_tensor", out=MT[:, g * 4:(g + 1) * 4, :], in0=em[:, g * 4:(g + 1) * 4, :],
                                          in1=cbm[:, g:g + 1, :].to_broadcast([128, 4, 128]), op=ALU.mult),
                                       reads=[em.b, cbm.b], writes=[MT.b])
                            py = bk[bm["py"]]
                            for h in range(8):
                                S.pe(_I("matmul", py[:, h * 64:(h + 1) * 64], lhsT=MT[:, h, :], rhs=xh[:, h * 64:(h + 1) * 64], start=True, stop=True),
                                     reads=[MT.b, xh.b], writes=[py.b])
                            S.dve(_I("tensor_tensor", out=yv[:].rearrange("p (h d) -> p h d", d=64), in0=pyo[:].rearrange("p (h d) -> p h d", d=64),
                                     in1=eacs[:].unsqueeze(2).to_broadcast([128, 8, 64]), op=ALU.mult),
                                  reads=[pyo.b, eacs.b], writes=[yv.b])
                            S.dve(_I("tensor_tensor", out=yv[:], in0=yv[:], in1=py[:], op=ALU.add), reads=[yv.b, py.b], writes=[yv.b])
                            S.pool(_I("tensor_tensor", out=yv[:], in0=yv[:], in1=skp[:], op=ALU.add), reads=[yv.b, skp.b], writes=[yv.b])
                            pz = bk[bm["pz"]]
                            for kc in range(8):
                                S.pe(_I("matmul", pz[:], lhsT=hnT_[:, kc * 128:(kc + 1) * 128], rhs=Wz[:, kc, :], start=(kc == 0), stop=(kc == 7)),
                                     reads=[hnT_.b, Wz.b], writes=[pz.b])
                            S.act(_I("activation", out=sz[:], in_=pz[:], func=AF.Exp, scale=-1.0), reads=[pz.b], writes=[sz.b])
                            S.act(_I("activation", out=sz[:], in_=sz[:], func=AF.Ln, bias=onec[:, 0:1], scale=1.0), reads=[sz.b, onec.b], writes=[sz.b])
                            S.act(_I("activation", out=sz[:], in_=sz[:], func=AF.Exp, scale=-1.0), reads=[sz.b], writes=[sz.b])
                            S.dve(_I("tensor_tensor", out=sz[:], in0=sz[:], in1=pz[:], op=ALU.mult), reads=[sz.b, pz.b], writes=[sz.b])
                            S.dve(_I("tensor_tensor", out=yv[:], in0=yv[:], in1=sz[:], op=ALU.mult), reads=[yv.b, sz.b], writes=[yv.b])
                            for g in range(2):
                                S.act(_I("activation", out=junk[:, 0:256], in_=yv[:, g * 256:(g + 1) * 256], func=AF.Square, accum_out=ssg[:, g:g + 1]),
                                      reads=[yv.b], writes=[junk.b, ssg.b])
                            S.act(_I("activation", out=ssg[:, 2:4], in_=ssg[:, 0:2], func=AF.Ln, scale=1.0 / 256, bias=epsc[:, 0:1]),
                                  reads=[ssg.b, epsc.b], writes=[ssg.b])
                            S.act(_I("activation", out=ssg[:, 2:4], in_=ssg[:, 2:4], func=AF.Exp, scale=-0.5), reads=[ssg.b], writes=[ssg.b])
                            for g in range(2):
                                S.dve(_I("scalar_tensor_tensor", out=yn[:, g * 256:(g + 1) * 256], in0=yv[:, g * 256:(g + 1) * 256],
                                         scalar=ssg[:, 2 + g:3 + g], in1=snw[:, g * 256:(g + 1) * 256], op0=ALU.mult, op1=ALU.mult),
                                      reads=[yv.b, ssg.b, snw.b], writes=[yn.b])
                            pTf = bk[bm["pTf"]]
                            pTfv = pTf[:].bitcast(BF16)
                            for j in range(4):
                                S.pe(_I("transpose", out=pTfv[:, j * 128:(j + 1) * 128], in_=yn[:, j * 128:(j + 1) * 128], identity=ident[:]),
                                     reads=[yn.b, ident.b], writes=[pTf.b])
                            S.act(_I("activation", out=OT[:, 0:4, t * 128:(t + 1) * 128], in_=pTfv[:, 0:512].rearrange("p (c t) -> p c t", t=128),
                                     func=AF.Copy),
                                  reads=[pTf.b], writes=[OT.b])

                def hgrn_pass(full, P, junk, bk, hn_cache, bm):
                    if True:
                        lo, hi_ = (0, 2048) if full else (512, 1536)
                        Wh = C.sb(P, "Wh", [128, 8, hi_ - lo], BF16)
                        for kc in range(8):
                            S.dma("pool", _I("dma_start", out=Wh[:, kc, :], in_=swv[:, kc, 1544 + lo:1544 + hi_]), writes=[Wh.b])
                        OF = 512 - lo
                        OI = 1024 - lo
                        sig = C.sb(P, "sig", [128, 512], F32)
                        gl = C.sb(P, "gl", [128, 512], F32)
                        kin = C.sb(P, "kin", [128, 512], F32)
                        gcum = C.sb(P, "gcum", [128, 512], F32)
                        rr = C.sb(P, "rr", [128, 512], F32)
                        kdl = C.sb(P, "kdl", [128, 512], BF16)
                        egl = C.sb(P, "egl", [128, 8], F32)
                        vb = C.sb(P, "vb", [64, 512], BF16)
                        kdlT = C.sb(P, "kdlT", [64, 512], BF16)
                        if full:
                            mask32 = C.sb(P, "mask32", [128, 128], F32)
                            S.dma("sp", _I("dma_start", out=mask32[:], in_=maskT_d), writes=[mask32.b])
                            qs = C.sb(P, "qs", [128, 512], F32)
                            eg = C.sb(P, "eg", [128, 512], F32)
                            qg = C.sb(P, "qg", [128, 512], BF16)
                            kd = C.sb(P, "kd", [128, 512], BF16)
                            AT = C.sb(P, "AT", [64, 4, 64], BF16)
                            on = C.sb(P, "on", [64, 512], F32)
                            sgt = C.sb(P, "sgt", [64, 512], F32)
                            ob = C.sb(P, "ob", [64, 512], BF16)
                            ss4 = C.sb(P, "ss4", [64, 8], F32)
                        for t in range(16):
                            yield
                            hnT_ = hn_cache[t]
                            pf = bk[bm["pf"]]
                            for j in range(4):
                                for kc in range(8):
                                    S.pe(_I("matmul", pf[:, j * 128:(j + 1) * 128], lhsT=Wh[:, kc, OF + j * 128:OF + (j + 1) * 128],
                                            rhs=hnT_[:, kc * 128:(kc + 1) * 128], start=(kc == 0), stop=(kc == 7)),
                                         reads=[Wh.b, hnT_.b], writes=[pf.b])
                            if full:
                                pq = bk[bm["pq"]]
                                for j in range(4):
                                    for kc in range(8):
                                        S.pe(_I("matmul", pq[:, j * 128:(j + 1) * 128], lhsT=Wh[:, kc, j * 128:(j + 1) * 128],
                                                rhs=hnT_[:, kc * 128:(kc + 1) * 128], start=(kc == 0), stop=(kc == 7)),
                                             reads=[Wh.b, hnT_.b], writes=[pq.b])
                            v3 = lambda tl: tl[:].rearrange("p (h t) -> p h t", t=128)
                            S.act(_I("activation", out=sig[:], in_=pf[:], func=AF.Exp, scale=-1.0), reads=[pf.b], writes=[sig.b])
                            S.act(_I("activation", out=sig[:], in_=sig[:], func=AF.Ln, bias=onec[:, 0:1], scale=1.0), reads=[sig.b, onec.b], writes=[sig.b])
                            S.act(_I("activation", out=sig[:], in_=sig[:], func=AF.Exp, scale=-1.0), reads=[sig.b], writes=[sig.b])
                            S.dve(_I("tensor_tensor", out=v3(sig), in0=v3(sig), in1=oml[:].unsqueeze(2).to_broadcast([128, 4, 128]), op=ALU.mult),
                                  reads=[sig.b, oml.b], writes=[sig.b])
                            S.dve(_I("tensor_tensor", out=v3(sig), in0=v3(sig), in1=lb[:].unsqueeze(2).to_broadcast([128, 4, 128]), op=ALU.add),
                                  reads=[sig.b, lb.b], writes=[sig.b])
                            S.act(_I("activation", out=gl[:], in_=sig[:], func=AF.Ln), reads=[sig.b], writes=[gl.b])
                            S.dve(_I("tensor_scalar", out=kin[:], in0=sig[:], scalar1=-1.0, scalar2=1.0, op0=ALU.mult, op1=ALU.add),
                                  reads=[sig.b], writes=[kin.b])
                            S.dve(_I("tensor_tensor_scan", out=gcum[:], data0=rmask[:], data1=gl[:], initial=0.0, op0=ALU.mult, op1=ALU.add),
                                  reads=[rmask.b, gl.b], writes=[gcum.b])
                            g8 = gcum[:].rearrange("p (a l) -> p a l", l=64)
                            S.dve(_I("tensor_tensor", out=rr[:].rearrange("p (a l) -> p a l", l=64), in0=g8[:, :, 63:64].to_broadcast([128, 8, 64]),
                                     in1=g8, op=ALU.subtract),
                                  reads=[gcum.b], writes=[rr.b])
                            S.act(_I("activation", out=rr[:], in_=rr[:], func=AF.Exp), reads=[rr.b], writes=[rr.b])
                            S.dve(_I("tensor_tensor", out=kdl[:], in0=kin[:], in1=rr[:], op=ALU.mult), reads=[kin.b, rr.b], writes=[kdl.b])
                            S.act(_I("activation", out=egl[:].unsqueeze(2), in_=g8[:, :, 63:64], func=AF.Exp), reads=[gcum.b], writes=[egl.b])
                            if full:
                                S.act(_I("activation", out=qs[:], in_=pq[:], func=AF.Exp, scale=-1.0), reads=[pq.b], writes=[qs.b])
                                S.act(_I("activation", out=qs[:], in_=qs[:], func=AF.Ln, bias=onec[:, 0:1], scale=1.0), reads=[qs.b, onec.b], writes=[qs.b])
                                S.act(_I("activation", out=qs[:], in_=qs[:], func=AF.Exp, scale=-1.0), reads=[qs.b], writes=[qs.b])
                                S.dve(_I("tensor_tensor", out=qs[:], in0=qs[:], in1=pq[:], op=ALU.mult), reads=[qs.b, pq.b], writes=[qs.b])
                                S.act(_I("activation", out=eg[:], in_=gcum[:], func=AF.Exp), reads=[gcum.b], writes=[eg.b])
                                S.dve(_I("tensor_tensor", out=qg[:], in0=qs[:], in1=eg[:], op=ALU.mult), reads=[qs.b, eg.b], writes=[qg.b])
                                S.act(_I("activation", out=eg[:], in_=gcum[:], func=AF.Exp, scale=-1.0), reads=[gcum.b], writes=[eg.b])
                                S.dve(_I("tensor_tensor", out=kd[:], in0=kin[:], in1=eg[:], op=ALU.mult), reads=[kin.b, eg.b], writes=[kd.b])
                            for c in range(2):
                                pi_ = bk[bm["pi"]]
                                for kc in range(8):
                                    S.pe(_I("matmul", pi_[0:64, :], lhsT=hnT_[:, kc * 128 + c * 64:kc * 128 + c * 64 + 64], rhs=Wh[:, kc, OI:OI + 512],
                                            start=(kc == 0), stop=(kc == 7)),
                                         reads=[hnT_.b, Wh.b], writes=[pi_.b])
                                S.act(_I("activation", out=vb[:], in_=pi_[0:64, :], func=AF.Copy), reads=[pi_.b], writes=[vb.b])
                                pk = bk[bm["pk"]]
                                pkv = pk[:].bitcast(BF16)
                                for h in range(4):
                                    S.pe(_I("transpose", out=pkv[0:64, h * 128:(h + 1) * 128], in_=kdl[:, h * 128 + c * 64:h * 128 + c * 64 + 64],
                                            identity=ident[:]),
                                         reads=[kdl.b, ident.b], writes=[pk.b])
                                S.act(_I("activation", out=kdlT[:], in_=pkv[0:64, 0:512], func=AF.Copy), reads=[pk.b], writes=[kdlT.b])
                                if full:
                                    psc = bk[bm["psc"]]
                                    for h in range(4):
                                        sl = slice(h * 128 + c * 64, h * 128 + c * 64 + 64)
                                        S.pe(_I("matmul", psc[0:64, h * 64:(h + 1) * 64], lhsT=kd[:, sl], rhs=qg[:, sl], start=True, stop=True),
                                             reads=[kd.b, qg.b], writes=[psc.b])
                                    S.dve(_I("tensor_tensor", out=AT[:], in0=psc[0:64, 0:256].rearrange("p (h l) -> p h l", l=64),
                                             in1=mask32[0:64, 0:64].unsqueeze(1).to_broadcast([64, 4, 64]), op=ALU.mult),
                                          reads=[psc.b, mask32.b], writes=[AT.b])
                                    po = bk[bm["po"]]
                                    for h in range(4):
                                        sl = slice(h * 128 + c * 64, h * 128 + c * 64 + 64)
                                        S.pe(_I("matmul", po[0:64, h * 128:(h + 1) * 128], lhsT=AT[:, h, :], rhs=vb[:, h * 128:(h + 1) * 128],
                                                start=True, stop=False),
                                             reads=[AT.b, vb.b], writes=[po.b])
                                        S.pe(_I("matmul", po[0:64, h * 128:(h + 1) * 128], lhsT=qg[:, sl], rhs=Hb[:, h * 128:(h + 1) * 128],
                                                start=False, stop=True),
                                             reads=[qg.b, Hb.b], writes=[po.b])
                                pst = bk[bm["hpst"]]
                                for h in range(4):
                                    S.pe(_I("matmul", pst[:, h * 128:(h + 1) * 128], lhsT=kdlT[:, h * 128:(h + 1) * 128], rhs=vb[:, h * 128:(h + 1) * 128],
                                            start=True, stop=True),
                                         reads=[kdlT.b, vb.b], writes=[pst.b])
                                S.dve(_I("tensor_tensor", out=Hm[:].rearrange("p (h v) -> p h v", v=128), in0=Hm[:].rearrange("p (h v) -> p h v", v=128),
                                         in1=egl[:].rearrange("p (h c) -> p h c", c=2)[:, :, c:c + 1].to_broadcast([128, 4, 128]), op=ALU.mult),
                                      reads=[Hm.b, egl.b], writes=[Hm.b])
                                S.dve(_I("tensor_tensor", out=Hm[:], in0=Hm[:], in1=pst[:], op=ALU.add), reads=[Hm.b, pst.b], writes=[Hm.b])
                                S.act(_I("activation", out=Hb[:], in_=Hm[:], func=AF.Copy), reads=[Hm.b], writes=[Hb.b])
                                if not full:
                                    continue
                                for h in range(4):
                                    S.act(_I("activation", out=junk[0:64, 0:128], in_=po[0:64, h * 128:(h + 1) * 128], func=AF.Square,
                                             accum_out=ss4[:, h:h + 1]),
                                          reads=[po.b], writes=[junk.b, ss4.b])
                                S.act(_I("activation", out=ss4[:, 4:8], in_=ss4[:, 0:4], func=AF.Ln, scale=1.0 / 128, bias=epsc[0:64, 0:1]),
                                      reads=[ss4.b, epsc.b], writes=[ss4.b])
                                S.act(_I("activation", out=ss4[:, 4:8], in_=ss4[:, 4:8], func=AF.Exp, scale=-0.5), reads=[ss4.b], writes=[ss4.b])
                                S.dve(_I("tensor_tensor", out=on[:].rearrange("p (h v) -> p h v", v=128), in0=po[0:64, :].rearrange("p (h v) -> p h v", v=128),
                                         in1=ss4[:, 4:8].unsqueeze(2).to_broadcast([64, 4, 128]), op=ALU.mult),
                                      reads=[po.b, ss4.b], writes=[on.b])
                                S.pool(_I("tensor_tensor", out=on[:].rearrange("p (h v) -> p h v", v=128), in0=on[:].rearrange("p (h v) -> p h v", v=128),
                                          in1=gnw[0:64, :].unsqueeze(1).to_broadcast([64, 4, 128]), op=ALU.mult),
                                       reads=[on.b, gnw.b], writes=[on.b])
                                pg = bk[bm["pg"]]
                                for kc in range(8):
                                    S.pe(_I("matmul", pg[0:64, :], lhsT=hnT_[:, kc * 128 + c * 64:kc * 128 + c * 64 + 64], rhs=Wh[:, kc, 1536:2048],
                                            start=(kc == 0), stop=(kc == 7)),
                                         reads=[hnT_.b, Wh.b], writes=[pg.b])
                                S.act(_I("activation", out=sgt[:], in_=pg[0:64, :], func=AF.Exp, scale=-1.0), reads=[pg.b], writes=[sgt.b])
                                S.act(_I("activation", out=sgt[:], in_=sgt[:], func=AF.Ln, bias=onec[0:64, 0:1], scale=1.0), reads=[sgt.b, onec.b], writes=[sgt.b])
                                S.act(_I("activation", out=sgt[:], in_=sgt[:], func=AF.Exp, scale=-1.0), reads=[sgt.b], writes=[sgt.b])
                                S.dve(_I("tensor_tensor", out=sgt[:], in0=sgt[:], in1=pg[0:64, :], op=ALU.mult), reads=[sgt.b, pg.b], writes=[sgt.b])
                                S.dve(_I("tensor_tensor", out=ob[:], in0=on[:], in1=sgt[:], op=ALU.mult), reads=[on.b, sgt.b], writes=[ob.b])
                                pT2 = bk[bm["pT2"]]
                                pT2v = pT2[:].bitcast(BF16)
                                for h in range(4):
                                    S.pe(_I("transpose", out=pT2v[:, h * 64:(h + 1) * 64], in_=ob[:, h * 128:(h + 1) * 128], identity=ident[0:64, 0:64]),
                                         reads=[ob.b, ident.b], writes=[pT2.b])
                                S.act(_I("activation", out=OT[:, 4:8, t * 128 + c * 64:t * 128 + c * 64 + 64],
                                         in_=pT2v[:, 0:256].rearrange("p (h t) -> p h t", t=64), func=AF.Copy),
                                      reads=[pT2.b], writes=[OT.b])

                SSD_FULL = dict(pX0=2, pX1=3, psm=4, pxs=5, pbt=6, pst=6, pyo=7, pcb=4, pe0=2, pe1=3, py=2, pz=5, pTf=3)
                HG_FULL = dict(pf=2, pq=3, pi=4, pk=5, psc=6, po=7, hpst=2, pg=3, pT2=6)
                SSD_ST = dict(pX0=2, pX1=3, psm=4, pxs=2, pbt=3, pst=3)
                HG_ST = dict(pf=5, hpst=5, pi=6, pk=7)

                def run_passes(specs, sched=False, pe_groups=False):
                    with contextlib.ExitStack() as P:
                        junk = C.sb(P, "junk", [128, D], F32)
                        ss = C.sb(P, "ss", [128, 2], F32)
                        rstd = C.sb(P, "rstd", [128, 1], F32)
                        hn = [C.sb(P, "hn", [128, D], BF16) for _ in range(2)]
                        hnT = [C.sb(P, "hnT", [128, D], BF16) for _ in range(2)]
                        stb = (junk, ss, rstd, hn, hnT)
                        bk = [C.ps(P, "bk", [128, 512], F32) for _ in range(8)]
                        hn_cache = {}
                        gens = [fn(full, P, junk, bk, hn_cache, bm) for fn, full, bm in specs]
                        for g in gens:
                            next(g)
                        hn_cache[0] = hn_tile1(stb, bk, 0)
                        for t in range(16):
                            if t + 1 < 16:
                                hn_cache[t + 1] = hn_tile1(stb, bk, t + 1)
                            for g in gens:
                                next(g, None)
                        S.flush(sched=sched, pe_groups=pe_groups)

                run_passes([(ssd_pass, False, SSD_ST), (hgrn_pass, False, HG_ST)], sched=SCHED_MASK[0])
                S.dma("sp", _I("dma_start", out=pay_d[:, 0:512], in_=Sm[:]), reads=[Sm.b], writes=[payb])
                S.dma("sp", _I("dma_start", out=pay_d[:, 512:1024], in_=Hm[:]), reads=[Hm.b], writes=[payb])
                S.dma("sp", _I("dma_start", out=pay_d[:, 1024:1048], in_=haloo[:].rearrange("p c j -> p (c j)")), reads=[haloo.b], writes=[payb])
                S.cc(_I("collective_compute", "AllGather", ALU.bypass, replica_groups=[[0, 1], [2, 3], [4, 5], [6, 7]],
                        ins=[pay_d.opt()], outs=[gath_d.opt()]),
                     reads=[payb], writes=[gathb])
                S.dma("sp", _I("dma_start", out=recv[:], in_=gath_d[0:128, :]), reads=[gathb], writes=[recv.b])
                S.dve(_I("tensor_scalar_mul", out=Sm[:], in0=recv[:, 0:512], scalar1=flag[:, 0:1]), reads=[recv.b, flag.b], writes=[Sm.b])
                S.dve(_I("tensor_scalar_mul", out=Hm[:], in0=recv[:, 512:1024], scalar1=flag[:, 0:1]), reads=[recv.b, flag.b], writes=[Hm.b])
                S.dve(_I("tensor_scalar_mul", out=halo0[:].rearrange("p c j -> p (c j)"), in0=recv[:, 1024:1048], scalar1=flag[:, 0:1]),
                      reads=[recv.b, flag.b], writes=[halo0.b])
                S.act(_I("activation", out=Sb[:], in_=Sm[:], func=AF.Copy), reads=[Sm.b], writes=[Sb.b])
                S.act(_I("activation", out=Hb[:], in_=Hm[:], func=AF.Copy), reads=[Hm.b], writes=[Hb.b])
                S.flush(sched=SCHED.get("exchange", False))
                run_passes([(ssd_pass, True, SSD_FULL)], sched=SCHED_MASK[1], pe_groups=SCHED_MASK[1])
                run_passes([(hgrn_pass, True, HG_FULL)], sched=SCHED_MASK[2])

        def outproj1():
            with contextlib.ExitStack() as Ff:
                wo = C.sb(Ff, "wo", [128, 8, D], BF16)
                pm = [C.ps(Ff, "pm", [128, 512], F32) for _ in range(4)]
                S.dma("pool", _I("dma_start", out=wo[:], in_=swout_d.rearrange("(c p) n -> p c n", p=128)), writes=[wo.b])
                for t in range(16):
                    for hf in range(2):
                        pm_ = pm[(2 * t + hf) % 4]
                        for kc in range(8):
                            S.pe(_I("matmul", pm_[:], lhsT=OT[:, kc, t * 128:(t + 1) * 128], rhs=wo[:, kc, hf * 512:(hf + 1) * 512],
                                    start=(kc == 0), stop=(kc == 7)),
                                 reads=[OT.b, wo.b], writes=[pm_.b])
                        S.dve(_I("tensor_tensor", out=xres[:, t, hf * 512:(hf + 1) * 512], in0=pm_[:], in1=xres[:, t, hf * 512:(hf + 1) * 512],
                                 op=ALU.add),
                              reads=[pm_.b, xres.b], writes=[xres.b])
                S.flush(sched=SCHED.get("outproj1", False))

        def final_store():
            with contextlib.ExitStack() as Ff:
                nw = C.sb(Ff, "nw", [128, D], F32)
                S.dma("sp", _I("dma_start", out=nw[:], in_=nfin_d.partition_broadcast(128)), writes=[nw.b])
                junk = C.sb(Ff, "junk", [128, D], F32)
                ss = C.sb(Ff, "ss", [128, 2], F32)
                rstd = C.sb(Ff, "rstd", [128, 1], F32)
                ot = [C.sb(Ff, "ot", [128, D], F32) for _ in range(2)]
                for t in range(16):
                    rms_stats(S, xres[:, t, :], [xres.b], D, junk, ss, rstd, epsc)
                    ot_ = ot[t % 2]
                    S.dve(_I("scalar_tensor_tensor", out=ot_[:], in0=xres[:, t, :], scalar=rstd[:, 0:1], in1=nw[:], op0=ALU.mult, op1=ALU.mult),
                          reads=[xres.b, rstd.b, nw.b], writes=[ot_.b])
                    S.dma("sp", _I("dma_start", out=out_d[t * 128:(t + 1) * 128, :], in_=ot_[:]), reads=[ot_.b])
                S.flush(sched=SCHED.get("final", False))

        def store_x():
            for t in range(16):
                S.dma("sp", _I("dma_start", out=out_d[t * 128:(t + 1) * 128, :], in_=xres[:, t, :]), reads=[xres.b])
            S.flush(sched=SCHED.get("storex", False))

        layer0_attention()
        xres = C.sb(top, "xres", [128, NT, D], F32)
        outproj0()
        if stage == "attn0":
            store_x()
            return nc
        ffn(0)
        if stage == "l0":
            store_x()
            return nc
        layer1()
        outproj1()
        if stage == "l1mix":
            store_x()
            return nc
        ffn(1)
        final_store()
    return nc


def _rope_tables(pos, dim):
    inv = (1.0 / (10000.0 ** (np.arange(0, dim, 2, dtype=np.float32) / np.float32(dim)))).astype(np.float32)
    ang = pos.astype(np.float32)[:, None] * inv[None, :]
    ang = np.concatenate([ang, ang], axis=-1)
    cos = np.cos(ang).astype(np.float32)
    sin = np.sin(ang).astype(np.float32)
    half = dim // 2
    sin_s = np.concatenate([-sin[:, :half], sin[:, half:]], axis=-1)
    f = lambda a: np.ascontiguousarray(a.reshape(32, 128, dim).transpose(1, 0, 2))
    return f(cos), f(sin_s)


def make_in_maps(inp):
    x = np.asarray(inp["x"], np.float32)
    f = lambda k: np.ascontiguousarray(np.asarray(inp[k], np.float32))
    wuq = f("a_w_uq")[0].reshape(384, 8, 96)
    wuq = np.ascontiguousarray(np.concatenate([wuq[:, :, :64].reshape(384, 512), wuq[:, :, 64:].reshape(384, 256)], axis=1))
    wukv = f("a_w_ukv")[0].reshape(256, 8, 128)
    wukv = np.ascontiguousarray(np.concatenate([wukv[:, :, :64].reshape(256, 512), wukv[:, :, 64:].reshape(256, 512)], axis=1))
    lam = np.ascontiguousarray(np.stack([f("a_lq1")[0], f("a_lk1")[0], f("a_lq2")[0], f("a_lk2")[0]]))
    rmask = np.ones((128, 512), np.float32)
    rmask[:, ::64] = 0.0
    shared = {
        "norm_mix": f("norm_mix"), "norm_ffn": f("norm_ffn"), "norm_final": f("norm_final"),
        "a_w_in": f("a_w_in")[0], "a_q_norm": f("a_q_norm")[0], "a_w_uq": wuq, "a_kv_norm": f("a_kv_norm")[0],
        "a_w_ukv": wukv, "a_lam": lam, "a_subln": f("a_subln")[0].reshape(128, 1).copy(), "a_w_out": f("a_w_out")[0],
        "ffn_gate": f("ffn_gate"), "ffn_up": f("ffn_up"), "ffn_down": f("ffn_down"),
        "s_w_in": f("s_w_in")[0], "s_w_out": f("s_w_out")[0],
        "s_conv_w": np.ascontiguousarray(f("s_conv_w")[0].T.reshape(8, 128, 4).transpose(1, 0, 2)),
        "s_conv_b": np.ascontiguousarray(f("s_conv_b")[0].reshape(8, 128).T),
        "s_dt_bias": f("s_dt_bias")[0], "s_a_log": f("s_a_log")[0], "s_d": f("s_d")[0], "s_norm": f("s_norm")[0],
        "h_g_norm": f("h_g_norm")[0],
        "h_lb": np.ascontiguousarray(f("h_lower_bound").reshape(2, 4, 128).transpose(2, 0, 1)),
        "rmask": rmask,
        "ident": np.eye(128, dtype=np.float32),
        "maskT": np.triu(np.ones((128, 128), np.float32)),
    }
    maps = []
    for c in range(8):
        b, hf = c // 2, c % 2
        m = dict(shared)
        m["x"] = np.ascontiguousarray(x[b, hf * TOK:(hf + 1) * TOK])
        if hf == 1:
            m["xp"] = np.ascontiguousarray(x[b, 0:TOK])
            pos = np.arange(4096)
            kones = np.ones((128, 32), np.float32)
        else:
            m["xp"] = np.zeros((TOK, D), np.float32)
            pos = np.concatenate([np.arange(TOK), np.arange(TOK)])
            kones = np.ones((128, 32), np.float32)
            kones[:, :16] = 0.0
        m["kones"] = kones
        m["flag"] = np.full((128, 1), float(hf), np.float32)
        m["cos32"], m["sin32"] = _rope_tables(pos, 32)
        m["cos64"], m["sin64"] = _rope_tables(pos, 64)
        maps.append(m)
    return maps


_NC_CACHE = {}


def kernel(**inputs):
    stage = inputs.pop("_stage", "all")
    if stage not in _NC_CACHE:
        _NC_CACHE[stage] = build_program(stage)
    nc = _NC_CACHE[stage]
    maps = make_in_maps(inputs)
    res = run_bass_kernel_spmd(nc, maps, core_ids=list(range(8)))
    out = np.zeros((4, 4096, D), np.float32)
    for c in range(8):
        b, hf = c // 2, c % 2
        out[b, hf * TOK:(hf + 1) * TOK] = res.results[c]["out"]
    return out
```

```python
import contextlib
import math
from functools import partial
import numpy as np
import concourse.bass as bass
import concourse.mybir as mybir
from concourse.bass_utils import run_bass_kernel_spmd

F32 = mybir.dt.float32
BF16 = mybir.dt.bfloat16
AF = mybir.ActivationFunctionType
ALU = mybir.AluOpType
AX = mybir.AxisListType

EPS = 1e-6
SCHED_MASK = [True, True, True]
SCHED = dict(final=True, outproj1=True, ffn=True, outproj0=True, dproj=True, dattn=True, mlaattn=True, mlaprojB=True)
D = 1024
TOK = 2048
NT = 16
DFF = 2816

ENG_NAMES = ("pe", "act", "dve", "pool", "sp")
N_DMA_SEMS = 24


def _I(name, *args, **kwargs):
    return (name, args, kwargs)


class Buf:
    __slots__ = ("name", "last_w", "readers")

    def __init__(self, name=""):
        self.name = name
        self.last_w = None
        self.readers = []


class Op:
    __slots__ = ("idx", "eng", "fn", "dma", "deps", "signal", "sem", "val", "waits", "clock", "inc")

    def __init__(self, idx, eng, fn, dma):
        self.inc = 16 if dma else 1
        self.idx = idx
        self.eng = eng
        self.fn = fn
        self.dma = dma
        self.deps = {}
        self.signal = False
        self.sem = None
        self.val = 0
        self.waits = []
        self.clock = None


class Sched:
    def __init__(self, nc, stack):
        self.nc = nc
        self.ops = []
        self.bufs = []
        self.dma_rr = 0
        self.dma_rr2 = [0, 0]
        self.dma_last = [None] * N_DMA_SEMS
        self.counts = {}
        self.sems = {}
        for e in ENG_NAMES:
            self.sems[("eng", e)] = stack.enter_context(nc.semaphore("se_" + e))
            self.counts[("eng", e)] = 0
        for i in range(N_DMA_SEMS):
            self.sems[("dma", i)] = stack.enter_context(nc.semaphore("sd_%d" % i))
            self.counts[("dma", i)] = 0
        self.sems[("cc", 0)] = stack.enter_context(nc.semaphore("s_cc"))
        self.counts[("cc", 0)] = 0
        self.cc_last = None
        self.n_inst = 0
        self.autosched = True

    def buf(self, name=""):
        b = Buf(name)
        self.bufs.append(b)
        return b

    def cc(self, fn, reads=(), writes=()):
        op = self.add("pool", fn, reads, writes, dma=True, cc=True)
        return op

    def add(self, eng, fn, reads=(), writes=(), dma=False, cc=False):
        op = Op(len(self.ops), eng, fn, dma)
        self.ops.append(op)
        for b in reads:
            if b.last_w is not None:
                op.deps[b.last_w] = True
            b.readers.append(op.idx)
        for b in writes:
            if b.last_w is not None:
                op.deps.setdefault(b.last_w, False)
            for r in b.readers:
                if r != op.idx:
                    op.deps.setdefault(r, False)
            b.last_w = op.idx
            b.readers = []
        if cc:
            op.inc = 1
            if self.cc_last is not None:
                op.deps[self.cc_last] = True
            self.cc_last = op.idx
            op.sem = ("cc", 0)
        elif dma:
            half = N_DMA_SEMS // 2
            k = 1 if eng == "pool" else 0
            s = k * half + self.dma_rr2[k]
            self.dma_rr2[k] = (self.dma_rr2[k] + 1) % half
            if self.dma_last[s] is not None:
                op.deps[self.dma_last[s]] = True
            self.dma_last[s] = op.idx
            op.sem = ("dma", s)
        else:
            op.sem = ("eng", eng)
        return op

    def pe(self, fn, reads=(), writes=()):
        return self.add("pe", fn, reads, writes)

    def act(self, fn, reads=(), writes=()):
        return self.add("act", fn, reads, writes)

    def dve(self, fn, reads=(), writes=()):
        return self.add("dve", fn, reads, writes)

    def pool(self, fn, reads=(), writes=()):
        return self.add("pool", fn, reads, writes)

    def dma(self, q, fn, reads=(), writes=()):
        return self.add(q, fn, reads, writes, dma=True)

    @staticmethod
    def _est_ns(op):
        name, a, kw = op.fn
        out = kw.get("out", a[0] if a else None)
        try:
            shp = out.shape
            free = 1
            for d in shp[1:]:
                free *= d
        except Exception:
            free = 512
        if op.dma:
            return 30000.0 if name == "collective_compute" else 2500.0 + free * 0.5
        if name in ("matmul", "transpose"):
            f32 = False
            try:
                f32 = (kw.get("lhsT", kw.get("in_")).dtype == F32)
            except Exception:
                pass
            return (max(64, free) / 2.4 + 25) * (4 if f32 else 1) + 60
        if name == "activation":
            return free / 1.05 + 220
        if name == "reciprocal":
            return free * 6.5 + 100
        if op.eng == "pool":
            return free * 2.3 + 150
        return free * 1.05 + 100

    def _list_schedule(self, ops):
        import heapq
        n = len(ops)
        dur = [self._est_ns(o) for o in ops]
        succ = [[] for _ in range(n)]
        indeg = [0] * n
        for o in ops:
            for d in o.deps:
                succ[d].append(o.idx)
                indeg[o.idx] += 1
        prio = [0.0] * n
        for i in range(n - 1, -1, -1):
            m = 0.0
            for s_ in succ[i]:
                if prio[s_] > m:
                    m = prio[s_]
            prio[i] = dur[i] + m
        pending = {e: [] for e in ENG_NAMES}
        avail = {e: [] for e in ENG_NAMES}
        ready_t = [0.0] * n
        free = {e: 0.0 for e in ENG_NAMES}
        for i in range(n):
            if indeg[i] == 0:
                heapq.heappush(pending[ops[i].eng], (0.0, i))
        order = []
        LAT = 250.0
        while len(order) < n:
            best = None
            for e in ENG_NAMES:
                pe_, av = pending[e], avail[e]
                while pe_ and pe_[0][0] <= free[e]:
                    rt, i = heapq.heappop(pe_)
                    heapq.heappush(av, (-prio[i], i))
                if av:
                    cand = (free[e], av[0][0], e, True)
                elif pe_:
                    cand = (pe_[0][0], -prio[pe_[0][1]], e, False)
                else:
                    continue
                if best is None or cand[:2] < best[:2]:
                    best = cand
            st, _, e, from_av = best
            if from_av:
                _, i = heapq.heappop(avail[e])
            else:
                _, i = heapq.heappop(pending[e])
            o = ops[i]
            fin = st + dur[i]
            if o.dma:
                free[e] = st + (700.0 if e == "pool" else 120.0)
            else:
                free[e] = fin
            order.append(i)
            for s_ in succ[i]:
                r = fin + (LAT if (ops[s_].eng != e or o.dma) else 60.0)
                if r > ready_t[s_]:
                    ready_t[s_] = r
                indeg[s_] -= 1
                if indeg[s_] == 0:
                    heapq.heappush(pending[ops[s_].eng], (ready_t[s_], s_))
        return order

    def flush(self, sched=False, pe_groups=None):
        nc = self.nc
        ops = self.ops
        if not ops:
            return
        if pe_groups is None:
            pe_groups = sched
        if pe_groups:
            for op in ops:
                if op.eng == "pe" and op.fn[2].get("start", True):
                    for d in op.deps:
                        if ops[d].eng == "pe":
                            op.deps[d] = True
        for op in ops:
            for d, strict in op.deps.items():
                p = ops[d]
                if p.dma or p.eng != op.eng or strict:
                    p.signal = True
            if op.dma:
                op.signal = True
        order = self._list_schedule(ops) if (sched and self.autosched) else list(range(len(ops)))
        ops_o = [ops[i] for i in order]
        counts = self.counts
        for op in ops_o:
            if op.signal:
                counts[op.sem] += op.inc
                op.val = counts[op.sem]
        base = dict(self.base_counts) if hasattr(self, "base_counts") else {k: 0 for k in counts}
        eclock = {e: dict(base) for e in ENG_NAMES}
        for op in ops_o:
            ck = eclock[op.eng]
            need = {}
            for d, strict in op.deps.items():
                p = ops[d]
                if not (p.dma or p.eng != op.eng or strict):
                    continue
                if ck.get(p.sem, 0) >= p.val:
                    continue
                if need.get(p.sem, (0, None))[0] < p.val:
                    need[p.sem] = (p.val, p)
            items = sorted(need.items(), key=lambda kv: -kv[1][1].idx)
            for sem, (val, p) in items:
                if ck.get(sem, 0) >= val:
                    continue
                op.waits.append((sem, val))
                for k, v in p.clock.items():
                    if ck.get(k, 0) < v:
                        ck[k] = v
                if ck.get(sem, 0) < val:
                    ck[sem] = val
            if op.signal:
                c = dict(ck)
                c[op.sem] = op.val
                op.clock = c
        sems = self.sems
        final = dict(counts)
        per_eng = {e: [o for o in ops_o if o.eng == e] for e in ENG_NAMES}
        self.n_inst += len(ops)

        def run(engh, lst):
            for o in lst:
                for sem, val in o.waits:
                    engh.wait_ge(sems[sem], val)
                name, a, kw = o.fn
                ins = getattr(engh, name)(*a, **kw)
                if o.signal:
                    ins.then_inc(sems[o.sem], o.inc)
            for k, v in final.items():
                if v > base.get(k, 0):
                    engh.wait_ge(sems[k], v)

        with nc.Block() as block:
            @block.tensor
            def _(e):
                run(e, per_eng["pe"])

            @block.scalar
            def _(e):
                run(e, per_eng["act"])

            @block.vector
            def _(e):
                run(e, per_eng["dve"])

            @block.gpsimd
            def _(e):
                run(e, per_eng["pool"])

            @block.sync
            def _(e):
                run(e, per_eng["sp"])

        self.base_counts = dict(counts)
        self.ops = []
        self.dma_last = [None] * N_DMA_SEMS
        self.cc_last = None
        for b in self.bufs:
            b.last_w = None
            b.readers = []


class T:
    def __init__(self, S, h, name):
        self.h = h
        self.b = S.buf(name)

    def __getitem__(self, k):
        return self.h[k]


class Ctx:
    def __init__(self, nc, S):
        self.nc = nc
        self.S = S
        self.uid = 0

    def sb(self, st, name, shape, dt):
        self.uid += 1
        h = st.enter_context(self.nc.sbuf_tensor("%s_%d" % (name, self.uid), list(shape), dt))
        return T(self.S, h, name)

    def ps(self, st, name, shape, dt):
        self.uid += 1
        h = st.enter_context(self.nc.psum_tensor("%s_%d" % (name, self.uid), list(shape), dt))
        return T(self.S, h, name)


def rms_stats(S, src_ap, src_bufs, n, junk, ss, rstd, epsc, lnexp=False):
    S.act(_I("activation", out=junk[:, 0:n], in_=src_ap, func=AF.Square, accum_out=ss[:, 0:1]),
          reads=src_bufs, writes=[junk.b, ss.b])
    if lnexp:
        S.act(_I("activation", out=ss[:, 1:2], in_=ss[:, 0:1], func=AF.Ln, scale=1.0 / n, bias=epsc[:, 0:1]),
              reads=[ss.b, epsc.b], writes=[ss.b])
        S.act(_I("activation", out=rstd[:, 0:1], in_=ss[:, 1:2], func=AF.Exp, scale=-0.5), reads=[ss.b], writes=[rstd.b])
        return
    S.act(_I("activation", out=ss[:, 1:2], in_=ss[:, 0:1], func=AF.Sqrt, scale=1.0 / n, bias=epsc[:, 0:1]),
          reads=[ss.b, epsc.b], writes=[ss.b])
    S.dve(_I("reciprocal", out=rstd[:, 0:1], in_=ss[:, 1:2]), reads=[ss.b], writes=[rstd.b])


def build_program(stage="all"):
    nc = bass.Bass("TRN2", target_bir_lowering=False)

    def din(name, shape, dt=F32):
        return nc.dram_tensor(name, list(shape), dt, kind="ExternalInput").ap()

    x_d = din("x", [TOK, D])
    xp_d = din("xp", [TOK, D])
    nmix_d = din("norm_mix", [2, D])
    nffn_d = din("norm_ffn", [2, D])
    nfin_d = din("norm_final", [D])
    awin_d = din("a_w_in", [D, 2208])
    aqn_d = din("a_q_norm", [384])
    awuq_d = din("a_w_uq", [384, 768])
    akvn_d = din("a_kv_norm", [256])
    awukv_d = din("a_w_ukv", [256, 1024])
    alam_d = din("a_lam", [4, 64])
    asub_d = din("a_subln", [128, 1])
    awout_d = din("a_w_out", [D, D])
    fg_d = din("ffn_gate", [2, D, DFF])
    fu_d = din("ffn_up", [2, D, DFF])
    fd_d = din("ffn_down", [2, DFF, D])
    ident_d = din("ident", [128, 128])
    maskT_d = din("maskT", [128, 128])
    kones_d = din("kones", [128, 32])
    cos32_d = din("cos32", [128, 32, 32])
    sin32_d = din("sin32", [128, 32, 32])
    cos64_d = din("cos64", [128, 32, 64])
    sin64_d = din("sin64", [128, 32, 64])
    swin_d = din("s_w_in", [D, 3592])
    convw_d = din("s_conv_w", [128, 8, 4])
    convb_d = din("s_conv_b", [128, 8])
    sdtb_d = din("s_dt_bias", [8])
    salog_d = din("s_a_log", [8])
    sd_d = din("s_d", [8])
    snorm_d = din("s_norm", [512])
    hgn_d = din("h_g_norm", [128])
    hlb_d = din("h_lb", [128, 2, 4])
    swout_d = din("s_w_out", [D, D])
    flag_d = din("flag", [128, 1])
    rmask_d = din("rmask", [128, 512])
    out_d = nc.dram_tensor("out", [TOK, D], F32, kind="ExternalOutput").ap()
    gu_d = nc.dram_tensor("gu_bf", [2, 22, 128, 8, 256], BF16).ap()
    wd_d = nc.dram_tensor("wd_bf", [2, DFF, D], BF16).ap()
    pay_d = nc.dram_tensor("pay", [128, 1048], F32).ap()
    gath_d = nc.dram_tensor("gath", [256, 1048], F32).ap()

    with contextlib.ExitStack() as top:
        S = Sched(nc, top)
        C = Ctx(nc, S)

        OT = C.sb(top, "OT", [128, 8, TOK], BF16)
        ident = C.sb(top, "ident", [128, 128], BF16)
        epsc = C.sb(top, "epsc", [128, 1], F32)
        ones32 = C.sb(top, "ones32", [128, 128], F32)
        S.dma("pool", _I("dma_start", out=ident[:], in_=ident_d), writes=[ident.b])
        S.dve(_I("memset", epsc[:], EPS), writes=[epsc.b])
        S.dve(_I("memset", ones32[:], 1.0), writes=[ones32.b])
        S.flush(sched=SCHED.get("init", False))

        gub = [S.buf("gu0"), S.buf("gu1")]
        wdb = [S.buf("wd0"), S.buf("wd1")]

        def precast(layer):
            fgv_ = fg_d[layer].rearrange("(c p) n -> p c n", p=128)
            fuv_ = fu_d[layer].rearrange("(c p) n -> p c n", p=128)
            th = []
            for c in range(22):
                th.append(partial(S.dma, "pool", _I("dma_start", out=gu_d[layer, c, :, :, 0:128], in_=fgv_[:, :, c * 128:(c + 1) * 128]),
                                  (), [S.buf()]))
                th.append(partial(S.dma, "pool", _I("dma_start", out=gu_d[layer, c, :, :, 128:256], in_=fuv_[:, :, c * 128:(c + 1) * 128]),
                                  (), [S.buf()]))
            for c in range(0, 22, 2):
                th.append(partial(S.dma, "pool", _I("dma_start", out=wd_d[layer, c * 128:(c + 2) * 128, :],
                                                    in_=fd_d[layer][c * 128:(c + 2) * 128, :]), (), [S.buf()]))
            return th

        def layer0_attention():
            with contextlib.ExitStack() as L0:
                maskT = C.sb(L0, "maskT", [128, 128], BF16)
                kones = C.sb(L0, "kones", [128, 32], F32)
                konesb = C.sb(L0, "konesb", [128, 32], BF16)
                nmix = C.sb(L0, "nmix", [128, D], F32)
                S.dma("pool", _I("dma_start", out=maskT[:], in_=maskT_d), writes=[maskT.b])
                S.dma("sp", _I("dma_start", out=kones[:], in_=kones_d), writes=[kones.b])
                S.dma("pool", _I("dma_start", out=konesb[:], in_=kones_d), writes=[konesb.b])
                S.dma("sp", _I("dma_start", out=nmix[:], in_=nmix_d[0].partition_broadcast(128)), writes=[nmix.b])

                def hn_tile(st_bufs, t):
                    xt, hn, junk, ss, rstd, pT, hnT = st_bufs
                    src = xp_d if t < 16 else x_d
                    r0 = (t % 16) * 128
                    xt_ = xt[t % 2]
                    S.dma("sp", _I("dma_start", out=xt_[:], in_=src[r0:r0 + 128, :]), writes=[xt_.b])
                    rms_stats(S, xt_[:], [xt_.b], D, junk, ss, rstd, epsc)
                    hn_ = hn[t % 2]
                    S.dve(_I("scalar_tensor_tensor", out=hn_[:], in0=xt_[:], scalar=rstd[:, 0:1], in1=nmix[:],
                                                          op0=ALU.mult, op1=ALU.mult),
                          reads=[xt_.b, rstd.b, nmix.b], writes=[hn_.b])
                    pT_ = pT[t % 2]
                    for kc in range(8):
                        S.pe(_I("transpose", out=pT_[:, kc * 128:(kc + 1) * 128],
                                                          in_=hn_[:, kc * 128:(kc + 1) * 128], identity=ident[:]),
                             reads=[hn_.b, ident.b], writes=[pT_.b])
                    hnT_ = hnT[t % 2]
                    S.act(_I("activation", out=hnT_[:], in_=pT_[:], func=AF.Copy), reads=[pT_.b], writes=[hnT_.b])
                    return hnT_

                def rope(src3, src_bufs, nh, hd, cosb, sinb, tb_bufs, t1, t2, out3, out_bufs):
                    hh = hd // 2
                    S.dve(_I("tensor_tensor", out=t1[:, 0:nh * hd].rearrange("p (h d) -> p h d", d=hd), in0=src3,
                                                    in1=cosb, op=ALU.mult),
                          reads=src_bufs + tb_bufs, writes=[t1.b])
                    t2v = t2[:, 0:nh * hd].rearrange("p (h d) -> p h d", d=hd)
                    S.dve(_I("tensor_tensor", out=t2v[:, :, 0:hh], in0=src3[:, :, hh:hd], in1=sinb[:, :, 0:hh], op=ALU.mult),
                          reads=src_bufs + tb_bufs, writes=[t2.b])
                    S.dve(_I("tensor_tensor", out=t2v[:, :, hh:hd], in0=src3[:, :, 0:hh], in1=sinb[:, :, hh:hd], op=ALU.mult),
                          reads=src_bufs + tb_bufs, writes=[t2.b])
                    S.pool(_I("tensor_tensor", out=out3, in0=t1[:, 0:nh * hd].rearrange("p (h d) -> p h d", d=hd),
                                                     in1=t2v, op=ALU.add),
                           reads=[t1.b, t2.b], writes=out_bufs)

                with contextlib.ExitStack() as M:
                    ckvnT = C.sb(M, "ckvnT", [128, 2, 4096], BF16)
                    kpeT = C.sb(M, "kpeT", [32, 4096], BF16)
                    QT = C.sb(M, "QT", [96, 8, TOK], BF16)
                    Vall = C.sb(M, "Vall", [128, 32, 8, 65], BF16)
                    wukv = C.sb(M, "wukv", [128, 2, 1024], BF16)
                    S.dma("pool", _I("dma_start", out=wukv[:], in_=awukv_d.rearrange("(c p) n -> p c n", p=128)),
                          writes=[wukv.b])
                    with contextlib.ExitStack() as A:
                        cqnT = C.sb(A, "cqnT", [128, 3, TOK], BF16)
                        wA = C.sb(A, "wA", [128, 8, 672], BF16)
                        wuq = C.sb(A, "wuq", [128, 3, 768], BF16)
                        qn = C.sb(A, "qn", [128, 384], F32)
                        kvn = C.sb(A, "kvn", [128, 256], F32)
                        cos32 = C.sb(A, "cos32", [128, 32, 32], F32)
                        sin32 = C.sb(A, "sin32", [128, 32, 32], F32)
                        S.dma("pool", _I("dma_start", out=wA[:], in_=awin_d.rearrange("(c p) n -> p c n", p=128)[:, :, 0:672]),
                              writes=[wA.b])
                        S.dma("pool", _I("dma_start", out=wuq[:], in_=awuq_d.rearrange("(c p) n -> p c n", p=128)),
                              writes=[wuq.b])
                        S.dma("sp", _I("dma_start", out=qn[:], in_=aqn_d.partition_broadcast(128)), writes=[qn.b])
                        S.dma("sp", _I("dma_start", out=kvn[:], in_=akvn_d.partition_broadcast(128)), writes=[kvn.b])
                        S.dma("sp", _I("dma_start", out=cos32[:], in_=cos32_d), writes=[cos32.b])
                        S.dma("sp", _I("dma_start", out=sin32[:], in_=sin32_d), writes=[sin32.b])
                        xt = [C.sb(A, "xt", [128, D], F32) for _ in range(2)]
                        hn = [C.sb(A, "hn", [128, D], BF16) for _ in range(2)]
                        junk = C.sb(A, "junk", [128, D], F32)
                        ss = C.sb(A, "ss", [128, 2], F32)
                        rstd = C.sb(A, "rstd", [128, 1], F32)
                        hnT = [C.sb(A, "hnT", [128, D], BF16) for _ in range(2)]
                        cqn = C.sb(A, "cqn", [128, 384], BF16)
                        ckvn = C.sb(A, "ckvn", [128, 256], BF16)
                        kpe = C.sb(A, "kpe", [128, 32], BF16)
                        t1 = C.sb(A, "t1", [128, 512], F32)
                        t2 = C.sb(A, "t2", [128, 512], F32)
                        qbf = C.sb(A, "qbf", [128, 8, 96], BF16)
                        pT = [C.ps(A, "pT", [128, D], BF16) for _ in range(2)]
                        pQ = C.ps(A, "pQ", [128, 512], F32)
                        pKV = C.ps(A, "pKV", [128, 512], F32)
                        pT2 = C.ps(A, "pT2", [128, D], BF16)
                        pV = C.ps(A, "pV", [128, 512], F32)
                        ss1 = C.sb(A, "ss1", [128, 2], F32)
                        st2 = C.sb(A, "st2", [128, 4], F32)
                        rs2 = C.sb(A, "rs2", [128, 2], F32)
                        S.dve(_I("memset", st2[:], 1.0), writes=[st2.b])
                        rstd1 = C.sb(A, "rstd1", [128, 1], F32)
                        stb = (xt, hn, junk, ss1, rstd1, pT, hnT)
                        hn_next = {}
                        pc0 = precast(0)
                        for t in range(32):
                            for _ in range(2):
                                if pc0:
                                    pc0.pop(0)()
                            own = t >= 16
                            if t == 0:
                                hn_next[0] = hn_tile(stb, 0)
                            hnT_ = hn_next.pop(t)
                            if t + 1 < 32:
                                hn_next[t + 1] = hn_tile(stb, t + 1)
                            c0 = t * 128
                            for kc in range(8):
                                S.pe(_I("matmul", pKV[:, 0:288], lhsT=hnT_[:, kc * 128:(kc + 1) * 128],
                                                               rhs=wA[:, kc, 384:672], start=(kc == 0), stop=(kc == 7)),
                                     reads=[hnT_.b, wA.b], writes=[pKV.b])
                            if own:
                                for kc in range(8):
                                    S.pe(_I("matmul", pQ[:, 0:384], lhsT=hnT_[:, kc * 128:(kc + 1) * 128],
                                                                   rhs=wA[:, kc, 0:384], start=(kc == 0), stop=(kc == 7)),
                                         reads=[hnT_.b, wA.b], writes=[pQ.b])
                            S.act(_I("activation", out=junk[:, 0:256], in_=pKV[:, 0:256], func=AF.Square, scale=256.0 ** -0.5,
                                     accum_out=st2[:, 0:1]),
                                  reads=[pKV.b], writes=[junk.b, st2.b])
                            if own:
                                S.act(_I("activation", out=junk[:, 256:640], in_=pQ[:, 0:384], func=AF.Square, scale=384.0 ** -0.5,
                                         accum_out=st2[:, 1:2]),
                                      reads=[pQ.b], writes=[junk.b, st2.b])
                            S.act(_I("activation", out=st2[:, 2:4], in_=st2[:, 0:2], func=AF.Sqrt, bias=epsc[:, 0:1], scale=1.0),
                                  reads=[st2.b, epsc.b], writes=[st2.b])
                            S.dve(_I("reciprocal", out=rs2[:, 0:2], in_=st2[:, 2:4]), reads=[st2.b], writes=[rs2.b])
                            S.dve(_I("scalar_tensor_tensor", out=ckvn[:], in0=pKV[:, 0:256], scalar=rs2[:, 0:1], in1=kvn[:],
                                     op0=ALU.mult, op1=ALU.mult),
                                  reads=[pKV.b, rs2.b, kvn.b], writes=[ckvn.b])
                            if own:
                                S.dve(_I("scalar_tensor_tensor", out=cqn[:], in0=pQ[:, 0:384], scalar=rs2[:, 1:2], in1=qn[:],
                                         op0=ALU.mult, op1=ALU.mult),
                                      reads=[pQ.b, rs2.b, qn.b], writes=[cqn.b])
                            rope(pKV[:, 256:288].rearrange("p (h d) -> p h d", d=32), [pKV.b], 1, 32,
                                 cos32[:, t:t + 1, :], sin32[:, t:t + 1, :], [cos32.b, sin32.b], t1, t2,
                                 kpe[:].rearrange("p (h d) -> p h d", d=32), [kpe.b])
                            for c in range(2):
                                S.pe(_I("transpose", out=pT2[:, c * 128:(c + 1) * 128], in_=ckvn[:, c * 128:(c + 1) * 128], identity=ident[:]),
                                     reads=[ckvn.b, ident.b], writes=[pT2.b])
                            if own:
                                for c in range(3):
                                    S.pe(_I("transpose", out=pT2[:, 384 + c * 128:384 + (c + 1) * 128], in_=cqn[:, c * 128:(c + 1) * 128],
                                            identity=ident[:]),
                                         reads=[cqn.b, ident.b], writes=[pT2.b])
                            S.pe(_I("transpose", out=pT2[0:32, 256:384], in_=kpe[:], identity=ident[:]),
                                 reads=[kpe.b, ident.b], writes=[pT2.b])
                            S.dve(_I("tensor_copy", out=ckvnT[:, :, c0:c0 + 128], in_=pT2[:, 0:256].rearrange("p (c t) -> p c t", t=128)),
                                  reads=[pT2.b], writes=[ckvnT.b])
                            S.dve(_I("tensor_copy", out=kpeT[0:32, c0:c0 + 128], in_=pT2[0:32, 256:384]), reads=[pT2.b], writes=[kpeT.b])
                            if own:
                                o0 = (t - 16) * 128
                                S.dve(_I("tensor_copy", out=cqnT[:, :, o0:o0 + 128], in_=pT2[:, 384:768].rearrange("p (c t) -> p c t", t=128)),
                                      reads=[pT2.b], writes=[cqnT.b])
                        S.flush(sched=SCHED.get("mlaprojA", False))
                        S.dve(_I("tensor_copy", out=Vall[:, :, :, 64], in_=kones[:].unsqueeze(2).to_broadcast([128, 32, 8])),
                              reads=[kones.b], writes=[Vall.b])
                        for t in range(32):
                            c0 = t * 128
                            for kc in range(2):
                                S.pe(_I("matmul", pV[:], lhsT=ckvnT[:, kc, c0:c0 + 128], rhs=wukv[:, kc, 512:1024],
                                                               start=(kc == 0), stop=(kc == 1)),
                                     reads=[ckvnT.b, wukv.b], writes=[pV.b])
                            S.act(_I("activation", out=Vall[:, t, :, 0:64], in_=pV[:].rearrange("p (h d) -> p h d", d=64),
                                                         func=AF.Copy),
                                  reads=[pV.b], writes=[Vall.b])
                        for t in range(16):
                            o0 = t * 128
                            for kc in range(3):
                                S.pe(_I("matmul", pQ[:], lhsT=cqnT[:, kc, o0:o0 + 128], rhs=wuq[:, kc, 0:512],
                                                               start=(kc == 0), stop=(kc == 2)),
                                     reads=[cqnT.b, wuq.b], writes=[pQ.b])
                            for kc in range(3):
                                S.pe(_I("matmul", pKV[:, 0:256], lhsT=cqnT[:, kc, o0:o0 + 128], rhs=wuq[:, kc, 512:768],
                                                               start=(kc == 0), stop=(kc == 2)),
                                     reads=[cqnT.b, wuq.b], writes=[pKV.b])
                            S.act(_I("activation", out=qbf[:, :, 0:64], in_=pQ[:].rearrange("p (h d) -> p h d", d=64), func=AF.Copy),
                                  reads=[pQ.b], writes=[qbf.b])
                            rope(pKV[:, 0:256].rearrange("p (h d) -> p h d", d=32), [pKV.b], 8, 32,
                                 cos32[:, 16 + t:17 + t, :].to_broadcast([128, 8, 32]),
                                 sin32[:, 16 + t:17 + t, :].to_broadcast([128, 8, 32]), [cos32.b, sin32.b], t1, t2,
                                 qbf[:, :, 64:96], [qbf.b])
                            for h in range(8):
                                S.pe(_I("transpose", out=pT2[0:96, h * 128:(h + 1) * 128], in_=qbf[:, h, :], identity=ident[:]),
                                     reads=[qbf.b, ident.b], writes=[pT2.b])
                            S.act(_I("activation", out=QT[0:96, :, o0:o0 + 128],
                                                         in_=pT2[0:96, :].rearrange("p (h t) -> p h t", t=128), func=AF.Copy),
                                  reads=[pT2.b], writes=[QT.b])
                        S.flush(sched=SCHED.get("mlaprojB", False))
                    with contextlib.ExitStack() as Cc:
                        KT = [C.sb(Cc, "KT", [96, 4096], BF16) for _ in range(2)]
                        PT = [C.sb(Cc, "PT", [128, 512], BF16) for _ in range(3)]
                        rrow = [C.sb(Cc, "rrow", [128, 512], F32) for _ in range(2)]
                        deferred = []
                        rb = C.sb(Cc, "rb", [64, 512], F32)
                        pS = [C.ps(Cc, "pS", [128, 512], F32) for _ in range(3)]
                        pO = [C.ps(Cc, "pO", [128, 512], F32) for _ in range(2)]
                        pK = [C.ps(Cc, "pK", [64, 512], F32) for _ in range(2)]
                        pB = C.ps(Cc, "pB", [64, 512], F32)
                        for i in range(2):
                            S.act(_I("activation", out=KT[i][64:96, :], in_=kpeT[0:32, :], func=AF.Copy),
                                  reads=[kpeT.b], writes=[KT[i].b])
                        scale = 96.0 ** -0.5
                        pc1 = precast(1)

                        def build_kt(h):
                            KT_ = KT[h % 2]
                            for g in range(8):
                                pK_ = pK[g % 2]
                                for kc in range(2):
                                    S.pe(_I("matmul", pK_[:], lhsT=wukv[:, kc, h * 64:(h + 1) * 64], rhs=ckvnT[:, kc, g * 512:(g + 1) * 512],
                                            start=(kc == 0), stop=(kc == 1)),
                                         reads=[wukv.b, ckvnT.b], writes=[pK_.b])
                                S.dve(_I("tensor_copy", out=KT_[0:64, g * 512:(g + 1) * 512], in_=pK_[:]), reads=[pK_.b], writes=[KT_.b])

                        def mla_S(i, h, qg, kb, cc):
                            KT_ = KT[h % 2]
                            q0 = qg * 512
                            pS_ = pS[i % 3]
                            S.pe(_I("matmul", pS_[:, cc:512], lhsT=KT_[0:96, kb * 128:(kb + 1) * 128], rhs=QT[0:96, h, q0 + cc:q0 + 512],
                                    start=True, stop=True),
                                 reads=[KT_.b, QT.b], writes=[pS_.b])

                        def mla_rest(i, h, qg, kb, cc, diag, nkb, gi):
                            q0 = qg * 512
                            pS_, PT_, pO_ = pS[i % 3], PT[i % 3], pO[gi % 2]
                            if kb == 0 and qg == 0 and h + 1 < 8:
                                build_kt(h + 1)
                            S.act(_I("activation", out=PT_[:, cc:512], in_=pS_[:, cc:512], func=AF.Exp, scale=scale),
                                  reads=[pS_.b], writes=[PT_.b])
                            if diag:
                                S.pool(_I("tensor_tensor", out=PT_[:, cc:cc + 128], in0=PT_[:, cc:cc + 128], in1=maskT[:], op=ALU.mult),
                                       reads=[PT_.b, maskT.b], writes=[PT_.b])
                            S.pe(_I("matmul", pO_[0:65, cc:512], lhsT=Vall[:, kb, h, :], rhs=PT_[:, cc:512], start=(kb == 0), stop=(kb == nkb - 1)),
                                 reads=[Vall.b, PT_.b], writes=[pO_.b])
                            if kb == nkb - 1:
                                rr_ = rrow[gi % 2]
                                S.act(_I("activation", out=rr_[64:65, :], in_=pO_[64:65, :], func=AF.Ln), reads=[pO_.b], writes=[rr_.b])
                                S.act(_I("activation", out=rr_[64:65, :], in_=rr_[64:65, :], func=AF.Exp, scale=-1.0), reads=[rr_.b], writes=[rr_.b])
                                deferred.append((i + 4, partial(mla_epi2, h, qg, gi)))

                        def mla_epi2(h, qg, gi):
                            q0 = qg * 512
                            pO_, rr_ = pO[gi % 2], rrow[gi % 2]
                            S.pe(_I("matmul", pB[:], lhsT=ones32[64:65, 0:64], rhs=rr_[64:65, :], start=True, stop=True),
                                 reads=[ones32.b, rr_.b], writes=[pB.b])
                            S.dve(_I("tensor_copy", out=rb[:], in_=pB[:]), reads=[pB.b], writes=[rb.b])
                            r0 = (h % 2) * 64
                            S.dve(_I("tensor_tensor", out=OT[r0:r0 + 64, h // 2, q0:q0 + 512], in0=pO_[0:64, :], in1=rb[:], op=ALU.mult),
                                  reads=[pO_.b, rb.b], writes=[OT.b])

                        tasks = []
                        gi = 0
                        for h in range(8):
                            for qg in range(4):
                                nkb = 16 + 4 * qg + 4
                                for kb in range(nkb):
                                    j = kb - (16 + 4 * qg)
                                    cc = 0 if j < 0 else j * 128
                                    i = len(tasks)
                                    tasks.append((partial(mla_S, i, h, qg, kb, cc), partial(mla_rest, i, h, qg, kb, cc, j >= 0, nkb, gi)))
                                gi += 1
                        build_kt(0)
                        LOOK = 2
                        for i in range(min(LOOK, len(tasks))):
                            tasks[i][0]()
                        for i in range(len(tasks)):
                            if i + LOOK < len(tasks):
                                tasks[i + LOOK][0]()
                            tasks[i][1]()
                            while deferred and deferred[0][0] <= i:
                                deferred.pop(0)[1]()
                            if i % 12 == 0 and pc1:
                                pc1.pop(0)()
                        while deferred:
                            deferred.pop(0)[1]()
                        while pc1:
                            pc1.pop(0)()
                        S.flush(sched=SCHED.get("mlaattn", False))

                with contextlib.ExitStack() as Dd:
                    dqT = [C.sb(Dd, "dqT", [128, 4, TOK], BF16) for _ in range(2)]
                    S.pool(_I("memset", dqT[0][64:128, :, :], 0.0), writes=[dqT[0].b])
                    S.pool(_I("memset", dqT[1][0:64, :, :], 0.0), writes=[dqT[1].b])
                    dkT = C.sb(Dd, "dkT", [128, 4, 4096], BF16)
                    dvb = C.sb(Dd, "dvb", [128, 32, 512], BF16)
                    lamc = C.sb(Dd, "lamc", [128, 4], F32)
                    subl = C.sb(Dd, "subl", [128, 1], F32)
                    with contextlib.ExitStack() as A:
                        wD = C.sb(A, "wD", [128, 8, 1536], BF16)
                        cos64 = C.sb(A, "cos64", [128, 32, 64], F32)
                        sin64 = C.sb(A, "sin64", [128, 32, 64], F32)
                        lam_in = C.sb(A, "lam_in", [128, 4, 64], F32)
                        S.dma("pool", _I("dma_start", out=wD[:], in_=awin_d.rearrange("(c p) n -> p c n", p=128)[:, :, 672:2208]),
                              writes=[wD.b])
                        S.dma("sp", _I("dma_start", out=cos64[:], in_=cos64_d), writes=[cos64.b])
                        S.dma("sp", _I("dma_start", out=sin64[:], in_=sin64_d), writes=[sin64.b])
                        S.dma("sp", _I("dma_start", out=lam_in[:].rearrange("p a d -> p (a d)"),
                                                          in_=alam_d.rearrange("a d -> (a d)").partition_broadcast(128)),
                              writes=[lam_in.b])
                        S.dma("sp", _I("dma_start", out=subl[:], in_=asub_d), writes=[subl.b])
                        xt = [C.sb(A, "xt", [128, D], F32) for _ in range(2)]
                        hn = [C.sb(A, "hn", [128, D], BF16) for _ in range(2)]
                        junk = C.sb(A, "junk", [128, D], F32)
                        ss = C.sb(A, "ss", [128, 2], F32)
                        rstd = C.sb(A, "rstd", [128, 1], F32)
                        hnT = [C.sb(A, "hnT", [128, D], BF16) for _ in range(2)]
                        t1 = C.sb(A, "t1", [128, 512], F32)
                        t2 = C.sb(A, "t2", [128, 512], F32)
                        dqb = C.sb(A, "dqb", [128, 512], BF16)
                        dkb = C.sb(A, "dkb", [128, 512], BF16)
                        pT = [C.ps(A, "pT", [128, D], BF16) for _ in range(2)]
                        pq = C.ps(A, "pq", [128, 512], F32)
                        pk = C.ps(A, "pk", [128, 512], F32)
                        pv = C.ps(A, "pv", [128, 512], F32)
                        pT2 = C.ps(A, "pT2", [128, D], BF16)
                        ss1 = C.sb(A, "ss1", [128, 2], F32)
                        st2 = C.sb(A, "st2", [128, 4], F32)
                        rs2 = C.sb(A, "rs2", [128, 2], F32)
                        S.dve(_I("memset", st2[:], 1.0), writes=[st2.b])
                        rstd1 = C.sb(A, "rstd1", [128, 1], F32)
                        stb = (xt, hn, junk, ss1, rstd1, pT, hnT)
                        hn_next = {}
                        S.dve(_I("tensor_tensor", out=t1[:, 0:64], in0=lam_in[:, 0, :], in1=lam_in[:, 1, :], op=ALU.mult),
                              reads=[lam_in.b], writes=[t1.b])
                        S.dve(_I("tensor_tensor", out=t1[:, 64:128], in0=lam_in[:, 2, :], in1=lam_in[:, 3, :], op=ALU.mult),
                              reads=[lam_in.b], writes=[t1.b])
                        S.dve(_I("reduce_sum", out=lamc[:, 0:2], in_=t1[:, 0:128].rearrange("p (a d) -> p a d", d=64), axis=AX.X),
                              reads=[t1.b], writes=[lamc.b])
                        S.act(_I("activation", out=lamc[:, 0:2], in_=lamc[:, 0:2], func=AF.Exp), reads=[lamc.b], writes=[lamc.b])
                        S.dve(_I("tensor_tensor", out=lamc[:, 2:3], in0=lamc[:, 1:2], in1=lamc[:, 0:1], op=ALU.subtract),
                              reads=[lamc.b], writes=[lamc.b])
                        S.dve(_I("tensor_scalar_add", out=lamc[:, 2:3], in0=lamc[:, 2:3], scalar1=-0.2), reads=[lamc.b], writes=[lamc.b])
                        S.dve(_I("tensor_scalar_mul", out=subl[:], in0=subl[:], scalar1=0.8), reads=[subl.b], writes=[subl.b])
                        for t in range(32):
                            own = t >= 16
                            if t == 0:
                                hn_next[0] = hn_tile(stb, 0)
                            hnT_ = hn_next.pop(t)
                            if t + 1 < 32:
                                hn_next[t + 1] = hn_tile(stb, t + 1)
                            c0 = t * 128
                            for kc in range(8):
                                S.pe(_I("matmul", pk[:], lhsT=hnT_[:, kc * 128:(kc + 1) * 128], rhs=wD[:, kc, 512:1024],
                                                               start=(kc == 0), stop=(kc == 7)),
                                     reads=[hnT_.b, wD.b], writes=[pk.b])
                            for kc in range(8):
                                S.pe(_I("matmul", pv[:], lhsT=hnT_[:, kc * 128:(kc + 1) * 128], rhs=wD[:, kc, 1024:1536],
                                                               start=(kc == 0), stop=(kc == 7)),
                                     reads=[hnT_.b, wD.b], writes=[pv.b])
                            S.dve(_I("tensor_copy", out=dvb[:, t, :], in_=pv[:]), reads=[pv.b], writes=[dvb.b])
                            cb = cos64[:, t:t + 1, :].to_broadcast([128, 8, 64])
                            sbb = sin64[:, t:t + 1, :].to_broadcast([128, 8, 64])
                            rope(pk[:].rearrange("p (h d) -> p h d", d=64), [pk.b], 8, 64, cb, sbb, [cos64.b, sin64.b], t1, t2,
                                 dkb[:].rearrange("p (h d) -> p h d", d=64), [dkb.b])
                            for c in range(4):
                                S.pe(_I("transpose", out=pT2[:, c * 128:(c + 1) * 128], in_=dkb[:, c * 128:(c + 1) * 128],
                                                                identity=ident[:]),
                                     reads=[dkb.b, ident.b], writes=[pT2.b])
                            S.dve(_I("tensor_copy", out=dkT[:, :, c0:c0 + 128], in_=pT2[:, 0:512].rearrange("p (c t) -> p c t", t=128)),
                                  reads=[pT2.b], writes=[dkT.b])
                            if own:
                                o0 = (t - 16) * 128
                                for kc in range(8):
                                    S.pe(_I("matmul", pq[:], lhsT=hnT_[:, kc * 128:(kc + 1) * 128], rhs=wD[:, kc, 0:512],
                                                                   start=(kc == 0), stop=(kc == 7)),
                                         reads=[hnT_.b, wD.b], writes=[pq.b])
                                rope(pq[:].rearrange("p (h d) -> p h d", d=64), [pq.b], 8, 64, cb, sbb, [cos64.b, sin64.b], t1, t2,
                                     dqb[:].rearrange("p (h d) -> p h d", d=64), [dqb.b])
                                for c in range(4):
                                    S.pe(_I("transpose", out=pT2[:, 512 + c * 128:512 + (c + 1) * 128],
                                                                    in_=dqb[:, c * 128:(c + 1) * 128], identity=ident[:]),
                                         reads=[dqb.b, ident.b], writes=[pT2.b])
                                for s_ in range(2):
                                    S.act(_I("activation", out=dqT[s_][s_ * 64:(s_ + 1) * 64, :, o0:o0 + 128],
                                             in_=pT2[s_ * 64:(s_ + 1) * 64, 512:1024].rearrange("p (c t) -> p c t", t=128), func=AF.Copy),
                                          reads=[pT2.b], writes=[dqT[s_].b])
                        S.flush(sched=SCHED.get("dproj", False))
                    with contextlib.ExitStack() as Ee:
                        PT = [C.sb(Ee, "PT", [128, 512], BF16) for _ in range(4)]
                        rb = C.sb(Ee, "rb", [128, 512], F32)
                        o0s = C.sb(Ee, "o0s", [128, 512], F32)
                        ods = [C.sb(Ee, "od", [128, 512], F32) for _ in range(2)]
                        sqs = [C.sb(Ee, "sq", [128, 512], F32) for _ in range(2)]
                        deferred = []
                        pS = [C.ps(Ee, "pS", [128, 512], F32) for _ in range(3)]
                        pO = [C.ps(Ee, "pO", [128, 512], F32) for _ in range(2)]
                        pZ = [C.ps(Ee, "pZ", [128, 512], F32) for _ in range(2)]
                        pB = C.ps(Ee, "pB", [128, 512], F32)
                        scale = 64.0 ** -0.5

                        def d_S(i, h, s_, qg, kb, cc):
                            r0 = s_ * 64
                            q0 = qg * 512
                            pS_ = pS[i % 3]
                            S.pe(_I("matmul", pS_[:, cc:512], lhsT=dkT[:, h, kb * 128:(kb + 1) * 128],
                                    rhs=dqT[s_][:, h, q0 + cc:q0 + 512], start=True, stop=True),
                                 reads=[dkT.b, dqT[s_].b], writes=[pS_.b])

                        def d_rest(i, h, s_, qg, kb, cc, diag, nkb, gi):
                            q0 = qg * 512
                            pS_, PT_, pO_, pZ_ = pS[i % 3], PT[i % 4], pO[s_], pZ[s_]
                            S.act(_I("activation", out=PT_[:, cc:512], in_=pS_[:, cc:512], func=AF.Exp, scale=scale),
                                  reads=[pS_.b], writes=[PT_.b])
                            if diag:
                                S.pool(_I("tensor_tensor", out=PT_[:, cc:cc + 128], in0=PT_[:, cc:cc + 128], in1=maskT[:], op=ALU.mult),
                                       reads=[PT_.b, maskT.b], writes=[PT_.b])
                            S.pe(_I("matmul", pO_[:, cc:512], lhsT=dvb[:, kb, h * 128:(h + 1) * 128], rhs=PT_[:, cc:512],
                                    start=(kb == 0), stop=(kb == nkb - 1)),
                                 reads=[dvb.b, PT_.b], writes=[pO_.b])
                            S.pe(_I("matmul", pZ_[:, cc:512], lhsT=konesb[:, kb:kb + 1].to_broadcast([128, 128]), rhs=PT_[:, cc:512],
                                    start=(kb == 0), stop=(kb == nkb - 1)),
                                 reads=[konesb.b, PT_.b], writes=[pZ_.b])
                            if kb != nkb - 1:
                                return
                            S.act(_I("activation", out=rb[:], in_=pZ_[:], func=AF.Ln), reads=[pZ_.b], writes=[rb.b])
                            S.act(_I("activation", out=rb[:], in_=rb[:], func=AF.Exp, scale=-1.0), reads=[rb.b], writes=[rb.b])
                            if s_ == 0:
                                S.dve(_I("tensor_tensor", out=o0s[:], in0=pO_[:], in1=rb[:], op=ALU.mult),
                                      reads=[pO_.b, rb.b], writes=[o0s.b])
                                return
                            od = ods[gi % 2]
                            sq = sqs[gi % 2]
                            S.dve(_I("tensor_tensor", out=od[:], in0=pO_[:], in1=rb[:], op=ALU.mult), reads=[pO_.b, rb.b], writes=[od.b])
                            S.dve(_I("scalar_tensor_tensor", out=od[:], in0=od[:], scalar=lamc[:, 2:3], in1=o0s[:], op0=ALU.mult, op1=ALU.add),
                                  reads=[od.b, lamc.b, o0s.b], writes=[od.b])
                            S.dve(_I("tensor_tensor", out=sq[:], in0=od[:], in1=od[:], op=ALU.mult), reads=[od.b], writes=[sq.b])
                            deferred.append((i + 6, partial(d_epi2, h, qg, gi)))

                        def d_epi2(h, qg, gi):
                            q0 = qg * 512
                            od, sq = ods[gi % 2], sqs[gi % 2]
                            S.pe(_I("matmul", pB[:], lhsT=ones32[:], rhs=sq[:], start=True, stop=True), reads=[ones32.b, sq.b], writes=[pB.b])
                            S.act(_I("activation", out=sq[:], in_=pB[:], func=AF.Ln, scale=1.0 / 128, bias=epsc[:, 0:1]),
                                  reads=[pB.b, epsc.b], writes=[sq.b])
                            S.act(_I("activation", out=sq[:], in_=sq[:], func=AF.Exp, scale=-0.5), reads=[sq.b], writes=[sq.b])
                            S.dve(_I("scalar_tensor_tensor", out=OT[:, 4 + h, q0:q0 + 512], in0=od[:], scalar=subl[:, 0:1], in1=sq[:],
                                     op0=ALU.mult, op1=ALU.mult),
                                  reads=[od.b, subl.b, sq.b], writes=[OT.b])

                        tasks = []
                        gi = 0
                        for h in range(4):
                            for qg in range(4):
                                nkb = 16 + 4 * qg + 4
                                for kb in range(nkb):
                                    j = kb - (16 + 4 * qg)
                                    cc = 0 if j < 0 else j * 128
                                    for s_ in range(2):
                                        i = len(tasks)
                                        tasks.append((partial(d_S, i, h, s_, qg, kb, cc), partial(d_rest, i, h, s_, qg, kb, cc, j >= 0, nkb, gi)))
                                gi += 1
                        LOOK = 2
                        for i in range(min(LOOK, len(tasks))):
                            tasks[i][0]()
                        for i in range(len(tasks)):
                            if i + LOOK < len(tasks):
                                tasks[i + LOOK][0]()
                            tasks[i][1]()
                            while deferred and deferred[0][0] <= i:
                                deferred.pop(0)[1]()
                        while deferred:
                            deferred.pop(0)[1]()
                        S.flush(sched=SCHED.get("dattn", False))

        def outproj0():
            with contextlib.ExitStack() as Ff:
                wo = C.sb(Ff, "wo", [128, 8, D], BF16)
                xt = [C.sb(Ff, "xt", [128, D], F32) for _ in range(2)]
                pm = [C.ps(Ff, "pm", [128, 512], F32) for _ in range(4)]
                S.dma("pool", _I("dma_start", out=wo[:], in_=awout_d.rearrange("(c p) n -> p c n", p=128)), writes=[wo.b])
                for t in range(16):
                    xt_ = xt[t % 2]
                    S.dma("sp", _I("dma_start", out=xt_[:], in_=x_d[t * 128:(t + 1) * 128, :]), writes=[xt_.b])
                    for hf in range(2):
                        pm_ = pm[(2 * t + hf) % 4]
                        for kc in range(8):
                            S.pe(_I("matmul", pm_[:], lhsT=OT[:, kc, t * 128:(t + 1) * 128],
                                                                                rhs=wo[:, kc, hf * 512:(hf + 1) * 512],
                                                                                start=(kc == 0), stop=(kc == 7)),
                                 reads=[OT.b, wo.b], writes=[pm_.b])
                        S.dve(_I("tensor_tensor", out=xres[:, t, hf * 512:(hf + 1) * 512],
                                                                                      in0=pm_[:], in1=xt_[:, hf * 512:(hf + 1) * 512],
                                                                                      op=ALU.add),
                              reads=[pm_.b, xt_.b], writes=[xres.b])
                S.flush(sched=SCHED.get("outproj0", False))


        def ffn(layer):
            with contextlib.ExitStack() as Ff:
                nw = C.sb(Ff, "nw", [128, D], F32)
                S.dma("sp", _I("dma_start", out=nw[:], in_=nffn_d[layer].partition_broadcast(128)), writes=[nw.b])
                junk = C.sb(Ff, "junk", [128, D], F32)
                ss = C.sb(Ff, "ss", [128, 2], F32)
                rstd = C.sb(Ff, "rstd", [128, 1], F32)
                hn = [C.sb(Ff, "hn", [128, D], BF16) for _ in range(2)]
                hnTs = [C.sb(Ff, "hnT", [128, 8, 512], BF16) for _ in range(2)]
                hT = C.sb(Ff, "hT", [128, 22, 512], BF16)
                wg = [C.sb(Ff, "wg", [128, 8, 256], BF16) for _ in range(4)]
                wd = [C.sb(Ff, "wd", [128, D], BF16) for _ in range(4)]
                sg = [C.sb(Ff, "sg", [128, 512], F32) for _ in range(2)]
                bk = [C.ps(Ff, "bk", [128, 512], F32) for _ in range(8)]
                it = 0

                def prologue(g, tt):
                    t = g * 4 + tt
                    hnT = hnTs[g % 2]
                    rms_stats(S, xres[:, t, :], [xres.b], D, junk, ss, rstd, epsc)
                    hn_ = hn[t % 2]
                    S.dve(_I("scalar_tensor_tensor", out=hn_[:], in0=xres[:, t, :], scalar=rstd[:, 0:1], in1=nw[:],
                             op0=ALU.mult, op1=ALU.mult),
                          reads=[xres.b, rstd.b, nw.b], writes=[hn_.b])
                    pT_ = bk[t % 2]
                    pTv = pT_[:].bitcast(BF16)
                    for kc in range(8):
                        S.pe(_I("transpose", out=pTv[:, kc * 128:(kc + 1) * 128],
                                in_=hn_[:, kc * 128:(kc + 1) * 128], identity=ident[:]),
                             reads=[hn_.b, ident.b], writes=[pT_.b])
                    S.act(_I("activation", out=hnT[:, :, tt * 128:(tt + 1) * 128],
                             in_=pTv.rearrange("p (c t) -> p c t", t=128), func=AF.Copy),
                          reads=[pT_.b], writes=[hnT.b])

                for tt in range(4):
                    prologue(0, tt)
                for g in range(4):
                    hnT = hnTs[g % 2]
                    for c in range(22):
                        if g + 1 < 4 and c in (3, 8, 13, 18):
                            prologue(g + 1, (c - 3) // 5)
                        wg_ = wg[it % 4]
                        S.dma("sp", _I("dma_start", out=wg_[:], in_=gu_d[layer, c]), reads=[gub[layer]], writes=[wg_.b])
                        pg_, pu_, sg_ = bk[2 + it % 2], bk[4 + it % 2], sg[it % 2]
                        it += 1
                        for kc in range(8):
                            S.pe(_I("matmul", pg_[:], lhsT=wg_[:, kc, 0:128], rhs=hnT[:, kc, :], start=(kc == 0), stop=(kc == 7)),
                                 reads=[wg_.b, hnT.b], writes=[pg_.b])
                        for kc in range(8):
                            S.pe(_I("matmul", pu_[:], lhsT=wg_[:, kc, 128:256], rhs=hnT[:, kc, :], start=(kc == 0), stop=(kc == 7)),
                                 reads=[wg_.b, hnT.b], writes=[pu_.b])
                        S.act(_I("activation", out=sg_[:], in_=pg_[:], func=AF.Silu), reads=[pg_.b], writes=[sg_.b])
                        S.dve(_I("tensor_tensor", out=hT[:, c, :], in0=pu_[:], in1=sg_[:], op=ALU.mult),
                              reads=[pu_.b, sg_.b], writes=[hT.b])
                    for c in range(22):
                        wd_ = wd[c % 4]
                        S.dma("sp", _I("dma_start", out=wd_[:], in_=wd_d[layer, c * 128:(c + 1) * 128, :]), reads=[wdb[layer]], writes=[wd_.b])
                        for tt in range(4):
                            for hf in range(2):
                                pd_ = bk[tt * 2 + hf]
                                S.pe(_I("matmul", pd_[:], lhsT=hT[:, c, tt * 128:(tt + 1) * 128], rhs=wd_[:, hf * 512:(hf + 1) * 512],
                                        start=(c == 0), stop=(c == 21)),
                                     reads=[hT.b, wd_.b], writes=[pd_.b])
                    for tt in range(4):
                        t = g * 4 + tt
                        for hf in range(2):
                            pd_ = bk[tt * 2 + hf]
                            S.dve(_I("tensor_tensor", out=xres[:, t, hf * 512:(hf + 1) * 512], in0=pd_[:],
                                     in1=xres[:, t, hf * 512:(hf + 1) * 512], op=ALU.add),
                                  reads=[pd_.b, xres.b], writes=[xres.b])
                S.flush(sched=SCHED.get("ffn", False))


        def layer1():
            with contextlib.ExitStack() as L1:
                Sm = C.sb(L1, "Sm", [128, 512], F32)
                Sb = C.sb(L1, "Sb", [128, 512], BF16)
                Hm = C.sb(L1, "Hm", [128, 512], F32)
                Hb = C.sb(L1, "Hb", [128, 512], BF16)
                halo0 = C.sb(L1, "halo0", [128, 8, 3], F32)
                haloo = C.sb(L1, "haloo", [128, 8, 3], F32)
                recv = C.sb(L1, "recv", [128, 1048], F32)
                flag = C.sb(L1, "flag", [128, 1], F32)
                nm1 = C.sb(L1, "nm1", [128, D], F32)
                triU = C.sb(L1, "triU", [128, 128], F32)
                ident32 = C.sb(L1, "ident32", [128, 128], F32)
                onec = C.sb(L1, "onec", [128, 1], F32)
                cw = C.sb(L1, "cw", [128, 8, 4], F32)
                cbias = C.sb(L1, "cbias", [128, 8], F32)
                dtb = C.sb(L1, "dtb", [128, 8], F32)
                ahead = C.sb(L1, "ahead", [128, 8], F32)
                dsk = C.sb(L1, "dsk", [128, 8], F32)
                snw = C.sb(L1, "snw", [128, 512], F32)
                gnw = C.sb(L1, "gnw", [128, 128], F32)
                hlb = C.sb(L1, "hlb", [128, 2, 4], F32)
                lb = C.sb(L1, "lb", [128, 4], F32)
                oml = C.sb(L1, "oml", [128, 4], F32)
                rmask = C.sb(L1, "rmask", [128, 512], F32)
                payb = S.buf("pay")
                gathb = S.buf("gath")
                S.dma("sp", _I("dma_start", out=flag[:], in_=flag_d), writes=[flag.b])
                S.dma("sp", _I("dma_start", out=nm1[:], in_=nmix_d[1].partition_broadcast(128)), writes=[nm1.b])
                S.dma("sp", _I("dma_start", out=triU[:], in_=maskT_d), writes=[triU.b])
                S.dma("sp", _I("dma_start", out=ident32[:], in_=ident_d), writes=[ident32.b])
                S.dma("sp", _I("dma_start", out=cw[:], in_=convw_d), writes=[cw.b])
                S.dma("sp", _I("dma_start", out=cbias[:], in_=convb_d), writes=[cbias.b])
                S.dma("sp", _I("dma_start", out=dtb[:], in_=sdtb_d.partition_broadcast(128)), writes=[dtb.b])
                S.dma("sp", _I("dma_start", out=ahead[:], in_=salog_d.partition_broadcast(128)), writes=[ahead.b])
                S.dma("sp", _I("dma_start", out=dsk[:], in_=sd_d.partition_broadcast(128)), writes=[dsk.b])
                S.dma("sp", _I("dma_start", out=snw[:], in_=snorm_d.partition_broadcast(128)), writes=[snw.b])
                S.dma("sp", _I("dma_start", out=gnw[:], in_=hgn_d.partition_broadcast(128)), writes=[gnw.b])
                S.dma("sp", _I("dma_start", out=hlb[:], in_=hlb_d), writes=[hlb.b])
                S.dma("sp", _I("dma_start", out=rmask[:], in_=rmask_d), writes=[rmask.b])
                S.dve(_I("memset", onec[:], 1.0), writes=[onec.b])
                S.act(_I("activation", out=ahead[:], in_=ahead[:], func=AF.Exp), reads=[ahead.b], writes=[ahead.b])
                S.dve(_I("tensor_scalar_mul", out=ahead[:], in0=ahead[:], scalar1=-1.0), reads=[ahead.b], writes=[ahead.b])
                S.dve(_I("tensor_tensor", out=lb[:], in0=hlb[:, 1, :], in1=hlb[:, 0, :], op=ALU.subtract), reads=[hlb.b], writes=[lb.b])
                S.act(_I("activation", out=lb[:], in_=lb[:], func=AF.Exp, scale=-1.0), reads=[lb.b], writes=[lb.b])
                S.dve(_I("tensor_scalar_add", out=lb[:], in0=lb[:], scalar1=1.0), reads=[lb.b], writes=[lb.b])
                S.dve(_I("reciprocal", out=lb[:], in_=lb[:]), reads=[lb.b], writes=[lb.b])
                S.dve(_I("tensor_scalar", out=oml[:], in0=lb[:], scalar1=-1.0, scalar2=1.0, op0=ALU.mult, op1=ALU.add),
                      reads=[lb.b], writes=[oml.b])
                cwd = C.sb(L1, "cwd", [128, 8, 4, 128], BF16)
                for c in range(8):
                    for j in range(4):
                        S.dve(_I("tensor_scalar_mul", out=cwd[:, c, j, :], in0=ident32[:], scalar1=cw[:, c, j:j + 1]),
                              reads=[ident32.b, cw.b], writes=[cwd.b])
                for tz in (Sm, Hm, halo0):
                    S.dve(_I("memset", tz[:], 0.0), writes=[tz.b])
                S.dve(_I("memset", Sb[:], 0.0), writes=[Sb.b])
                S.dve(_I("memset", Hb[:], 0.0), writes=[Hb.b])
                S.flush(sched=SCHED.get("l1setup", False))

                def hn_tile1(stb, bk, t):
                    junk, ss, rstd, hn, hnT = stb
                    rms_stats(S, xres[:, t, :], [xres.b], D, junk, ss, rstd, epsc, lnexp=True)
                    hn_ = hn[t % 2]
                    S.dve(_I("scalar_tensor_tensor", out=hn_[:], in0=xres[:, t, :], scalar=rstd[:, 0:1], in1=nm1[:],
                             op0=ALU.mult, op1=ALU.mult),
                          reads=[xres.b, rstd.b, nm1.b], writes=[hn_.b])
                    pT_ = bk[t % 2]
                    pTv = pT_[:].bitcast(BF16)
                    for kc in range(8):
                        S.pe(_I("transpose", out=pTv[:, kc * 128:(kc + 1) * 128], in_=hn_[:, kc * 128:(kc + 1) * 128],
                                identity=ident[:]),
                             reads=[hn_.b, ident.b], writes=[pT_.b])
                    hnT_ = hnT[t % 2]
                    S.act(_I("activation", out=hnT_[:], in_=pTv, func=AF.Copy), reads=[pT_.b], writes=[hnT_.b])
                    return hnT_

                swv = swin_d.rearrange("(c p) n -> p c n", p=128)

                def ssd_pass(full, P, junk, bk, hn_cache, bm):
                    if True:
                        Wx = C.sb(P, "Wx", [128, 8, 1032], BF16)
                        for kc in range(8):
                            S.dma("pool", _I("dma_start", out=Wx[:, kc, :], in_=swv[:, kc, 512:1544]), writes=[Wx.b])
                        if full:
                            Wz = C.sb(P, "Wz", [128, 8, 512], BF16)
                            for kc in range(8):
                                S.dma("pool", _I("dma_start", out=Wz[:, kc, :], in_=swv[:, kc, 0:512]), writes=[Wz.b])
                            mask32 = C.sb(P, "mask32", [128, 128], F32)
                            S.dma("sp", _I("dma_start", out=mask32[:], in_=maskT_d), writes=[mask32.b])
                        xc = C.sb(P, "xc", [128, 8, 131], BF16)
                        acc = C.sb(P, "acc", [128, 8, 128], F32)
                        tmp = C.sb(P, "tmp", [128, 8, 128], F32)
                        xs32 = C.sb(P, "xs32", [128, 4, 128], F32)
                        bc16 = C.sb(P, "bc16", [128, 4, 128], BF16)
                        dt = C.sb(P, "dt", [128, 8], F32)
                        av = C.sb(P, "av", [128, 8], F32)
                        nav = C.sb(P, "nav", [128, 8], F32)
                        acs = C.sb(P, "acs", [128, 8], F32)
                        eacs = C.sb(P, "eacs", [128, 8], F32)
                        dst = C.sb(P, "dst", [128, 8], F32)
                        dec = C.sb(P, "dec", [128, 8], F32)
                        xh = C.sb(P, "xh", [128, 512], BF16)
                        xhd = C.sb(P, "xhd", [128, 512], BF16)
                        Btok = C.sb(P, "Btok", [128, 256], BF16)
                        if full:
                            skp = C.sb(P, "skp", [128, 512], F32)
                            cbm = C.sb(P, "cbm", [128, 2, 128], F32)
                            em = C.sb(P, "em", [128, 8, 128], F32)
                            MT = C.sb(P, "MT", [128, 8, 128], BF16)
                            yv = C.sb(P, "yv", [128, 512], F32)
                            sz = C.sb(P, "sz", [128, 512], F32)
                            yn = C.sb(P, "yn", [128, 512], BF16)
                            ssg = C.sb(P, "ssg", [128, 4], F32)
                        S.dve(_I("tensor_copy", out=xc[:, :, 0:3], in_=halo0[:]), reads=[halo0.b], writes=[xc.b])
                        for t in range(16):
                            yield
                            hnT_ = hn_cache[t]
                            pX = (bk[bm["pX0"]], bk[bm["pX1"]])
                            for j in range(8):
                                for kc in range(8):
                                    S.pe(_I("matmul", pX[j // 4][:, (j % 4) * 128:(j % 4 + 1) * 128], lhsT=Wx[:, kc, j * 128:(j + 1) * 128],
                                            rhs=hnT_[:, kc * 128:(kc + 1) * 128], start=(kc == 0), stop=(kc == 7)),
                                         reads=[Wx.b, hnT_.b], writes=[pX[j // 4].b])
                            for i in range(2):
                                S.act(_I("activation", out=xc[:, i * 4:(i + 1) * 4, 3:131],
                                         in_=pX[i][:].rearrange("p (c t) -> p c t", t=128), func=AF.Copy),
                                      reads=[pX[i].b], writes=[xc.b])
                            psm = bk[bm["psm"]]
                            for kc in range(8):
                                S.pe(_I("matmul", psm[:, 0:8], lhsT=hnT_[:, kc * 128:(kc + 1) * 128], rhs=Wx[:, kc, 1024:1032],
                                        start=(kc == 0), stop=(kc == 7)),
                                     reads=[Wx.b, hnT_.b], writes=[psm.b])
                            for c in range(8):
                                for j in range(4):
                                    S.pe(_I("matmul", pX[c // 4][:, (c % 4) * 128:(c % 4 + 1) * 128], lhsT=cwd[:, c, j, :], rhs=xc[:, c, j:j + 128],
                                            start=(j == 0), stop=(j == 3)),
                                         reads=[cwd.b, xc.b], writes=[pX[c // 4].b])
                            for i in range(2):
                                S.dve(_I("tensor_tensor", out=acc[:, i * 4:(i + 1) * 4, :], in0=pX[i][:].rearrange("p (c t) -> p c t", t=128),
                                         in1=cbias[:, i * 4:(i + 1) * 4].unsqueeze(2).to_broadcast([128, 4, 128]), op=ALU.add),
                                      reads=[pX[i].b, cbias.b], writes=[acc.b])
                            if t == 15:
                                S.dve(_I("tensor_copy", out=haloo[:], in_=xc[:, :, 128:131]), reads=[xc.b], writes=[haloo.b])
                            else:
                                S.dve(_I("tensor_copy", out=xc[:, :, 0:3], in_=xc[:, :, 128:131]), reads=[xc.b], writes=[xc.b])
                            S.act(_I("activation", out=tmp[:], in_=acc[:], func=AF.Exp, scale=-1.0), reads=[acc.b], writes=[tmp.b])
                            S.act(_I("activation", out=tmp[:], in_=tmp[:], func=AF.Ln, bias=onec[:, 0:1], scale=1.0), reads=[tmp.b, onec.b], writes=[tmp.b])
                            S.act(_I("activation", out=tmp[:], in_=tmp[:], func=AF.Exp, scale=-1.0), reads=[tmp.b], writes=[tmp.b])
                            S.dve(_I("tensor_tensor", out=xs32[:], in0=acc[:, 0:4, :], in1=tmp[:, 0:4, :], op=ALU.mult), reads=[acc.b, tmp.b], writes=[xs32.b])
                            S.dve(_I("tensor_tensor", out=bc16[:], in0=acc[:, 4:8, :], in1=tmp[:, 4:8, :], op=ALU.mult), reads=[acc.b, tmp.b], writes=[bc16.b])
                            S.dve(_I("tensor_tensor", out=dt[:], in0=psm[:, 0:8], in1=dtb[:], op=ALU.add), reads=[psm.b, dtb.b], writes=[dt.b])
                            S.act(_I("activation", out=dt[:], in_=dt[:], func=AF.Exp), reads=[dt.b], writes=[dt.b])
                            S.act(_I("activation", out=dt[:], in_=dt[:], func=AF.Ln, bias=onec[:, 0:1], scale=1.0), reads=[dt.b, onec.b], writes=[dt.b])
                            S.dve(_I("tensor_tensor", out=av[:], in0=dt[:], in1=ahead[:], op=ALU.mult), reads=[dt.b, ahead.b], writes=[av.b])
                            S.pe(_I("matmul", psm[:, 16:24], lhsT=triU[:], rhs=av[:], start=True, stop=True), reads=[triU.b, av.b], writes=[psm.b])
                            S.pe(_I("matmul", psm[:, 32:40], lhsT=ones32[:], rhs=av[:], start=True, stop=True), reads=[ones32.b, av.b], writes=[psm.b])
                            S.act(_I("activation", out=acs[:], in_=psm[:, 16:24], func=AF.Copy), reads=[psm.b], writes=[acs.b])
                            S.dve(_I("tensor_tensor", out=dst[:], in0=psm[:, 32:40], in1=acs[:], op=ALU.subtract), reads=[psm.b, acs.b], writes=[dst.b])
                            S.act(_I("activation", out=dst[:], in_=dst[:], func=AF.Exp), reads=[dst.b], writes=[dst.b])
                            S.act(_I("activation", out=dec[:], in_=psm[:, 32:40], func=AF.Exp), reads=[psm.b], writes=[dec.b])
                            S.dve(_I("tensor_tensor", out=dst[:], in0=dst[:], in1=dt[:], op=ALU.mult), reads=[dst.b, dt.b], writes=[dst.b])
                            pxs = bk[bm["pxs"]]
                            for j in range(4):
                                S.pe(_I("transpose", out=pxs[:, j * 128:(j + 1) * 128], in_=xs32[:, j, :], identity=ident32[:]),
                                     reads=[xs32.b, ident32.b], writes=[pxs.b])
                            pxs3 = pxs[:].rearrange("p (h d) -> p h d", d=64)
                            S.dve(_I("tensor_tensor", out=xh[:].rearrange("p (h d) -> p h d", d=64), in0=pxs3,
                                     in1=dt[:].unsqueeze(2).to_broadcast([128, 8, 64]), op=ALU.mult),
                                  reads=[pxs.b, dt.b], writes=[xh.b])
                            S.dve(_I("tensor_tensor", out=xhd[:].rearrange("p (h d) -> p h d", d=64), in0=pxs3,
                                     in1=dst[:].unsqueeze(2).to_broadcast([128, 8, 64]), op=ALU.mult),
                                  reads=[pxs.b, dst.b], writes=[xhd.b])
                            if full:
                                S.dve(_I("tensor_tensor", out=skp[:].rearrange("p (h d) -> p h d", d=64), in0=pxs3,
                                         in1=dsk[:].unsqueeze(2).to_broadcast([128, 8, 64]), op=ALU.mult),
                                      reads=[pxs.b, dsk.b], writes=[skp.b])
                            pbt = bk[bm["pbt"]]
                            pbtv = pbt[:].bitcast(BF16)
                            for g in range(2):
                                S.pe(_I("transpose", out=pbtv[:, g * 128:(g + 1) * 128], in_=bc16[:, g, :], identity=ident[:]),
                                     reads=[bc16.b, ident.b], writes=[pbt.b])
                            S.act(_I("activation", out=Btok[:], in_=pbtv[:, 0:256], func=AF.Copy), reads=[pbt.b], writes=[Btok.b])
                            if full:
                                S.act(_I("activation", out=eacs[:], in_=psm[:, 16:24], func=AF.Exp), reads=[psm.b], writes=[eacs.b])
                                S.dve(_I("tensor_scalar_mul", out=nav[:], in0=av[:], scalar1=-1.0), reads=[av.b], writes=[nav.b])
                                pyo = bk[bm["pyo"]]
                                for g in range(2):
                                    S.pe(_I("matmul", pyo[:, g * 256:(g + 1) * 256], lhsT=bc16[:, 2 + g, :], rhs=Sb[:, g * 256:(g + 1) * 256],
                                            start=True, stop=True),
                                         reads=[bc16.b, Sb.b], writes=[pyo.b])
                            pst = bk[bm["pst"]]
                            for g in range(2):
                                S.pe(_I("matmul", pst[:, g * 256:(g + 1) * 256], lhsT=Btok[:, g * 128:(g + 1) * 128], rhs=xhd[:, g * 256:(g + 1) * 256],
                                        start=True, stop=True),
                                     reads=[Btok.b, xhd.b], writes=[pst.b])
                            S.dve(_I("tensor_tensor", out=Sm[:].rearrange("p (h d) -> p h d", d=64), in0=Sm[:].rearrange("p (h d) -> p h d", d=64),
                                     in1=dec[:].unsqueeze(2).to_broadcast([128, 8, 64]), op=ALU.mult),
                                  reads=[Sm.b, dec.b], writes=[Sm.b])
                            S.dve(_I("tensor_tensor", out=Sm[:], in0=Sm[:], in1=pst[:], op=ALU.add), reads=[Sm.b, pst.b], writes=[Sm.b])
                            S.act(_I("activation", out=Sb[:], in_=Sm[:], func=AF.Copy), reads=[Sm.b], writes=[Sb.b])
                            if not full:
                                continue
                            pcb = bk[bm["pcb"]]
                            for g in range(2):
                                S.pe(_I("matmul", pcb[:, 128 + g * 128:128 + (g + 1) * 128], lhsT=bc16[:, g, :], rhs=bc16[:, 2 + g, :], start=True, stop=True),
                                     reads=[bc16.b], writes=[pcb.b])
                            S.dve(_I("tensor_tensor", out=cbm[:], in0=pcb[:, 128:384].rearrange("p (g l) -> p g l", l=128),
                                     in1=mask32[:].unsqueeze(1).to_broadcast([128, 2, 128]), op=ALU.mult),
                                  reads=[pcb.b, mask32.b], writes=[cbm.b])
                            pe_ = (bk[bm["pe0"]], bk[bm["pe1"]])
                            for h in range(8):
                                o_ = pe_[h // 4][:, (h % 4) * 128:(h % 4 + 1) * 128]
                                S.pe(_I("matmul", o_, lhsT=av[:, h:h + 1].to_broadcast([128, 128]), rhs=triU[:], start=True, stop=False),
                                     reads=[av.b, triU.b], writes=[pe_[h // 4].b])
                                S.pe(_I("matmul", o_, lhsT=triU[:], rhs=nav[:, h:h + 1].to_broadcast([128, 128]), start=False, stop=True),
                                     reads=[nav.b, triU.b], writes=[pe_[h // 4].b])
                            for i in range(2):
                                S.dve(_I("tensor_scalar_min", out=em[:, i * 4:(i + 1) * 4, :], in0=pe_[i][:].rearrange("p (h l) -> p h l", l=128),
                                         scalar1=0.0),
                                      reads=[pe_[i].b], writes=[em.b])
                            S.act(_I("activation", out=em[:], in_=em[:], func=AF.Exp), reads=[em.b], writes=[em.b])
                            for g in range(2):
                                S.pool(_I("tensor_tensor", out=MT[:, g * 4:(g + 1) * 4, :], in0=em[:, g * 4:(g + 1) * 4, :],
                                          in1=cbm[:, g:g + 1, :].to_broadcast([128, 4, 128]), op=ALU.mult),
                                       reads=[em.b, cbm.b], writes=[MT.b])
                            py = bk[bm["py"]]
                            for h in range(8):
                                S.pe(_I("matmul", py[:, h * 64:(h + 1) * 64], lhsT=MT[:, h, :], rhs=xh[:, h * 64:(h + 1) * 64], start=True, stop=True),
                                     reads=[MT.b, xh.b], writes=[py.b])
                            S.dve(_I("tensor_tensor", out=yv[:].rearrange("p (h d) -> p h d", d=64), in0=pyo[:].rearrange("p (h d) -> p h d", d=64),
                                     in1=eacs[:].unsqueeze(2).to_broadcast([128, 8, 64]), op=ALU.mult),
                                  reads=[pyo.b, eacs.b], writes=[yv.b])
                            S.dve(_I("tensor_tensor", out=yv[:], in0=yv[:], in1=py[:], op=ALU.add), reads=[yv.b, py.b], writes=[yv.b])
                            S.pool(_I("tensor_tensor", out=yv[:], in0=yv[:], in1=skp[:], op=ALU.add), reads=[yv.b, skp.b], writes=[yv.b])
                            pz = bk[bm["pz"]]
                            for kc in range(8):
                                S.pe(_I("matmul", pz[:], lhsT=hnT_[:, kc * 128:(kc + 1) * 128], rhs=Wz[:, kc, :], start=(kc == 0), stop=(kc == 7)),
                                     reads=[hnT_.b, Wz.b], writes=[pz.b])
                            S.act(_I("activation", out=sz[:], in_=pz[:], func=AF.Exp, scale=-1.0), reads=[pz.b], writes=[sz.b])
                            S.act(_I("activation", out=sz[:], in_=sz[:], func=AF.Ln, bias=onec[:, 0:1], scale=1.0), reads=[sz.b, onec.b], writes=[sz.b])
                            S.act(_I("activation", out=sz[:], in_=sz[:], func=AF.Exp, scale=-1.0), reads=[sz.b], writes=[sz.b])
                            S.dve(_I("tensor_tensor", out=sz[:], in0=sz[:], in1=pz[:], op=ALU.mult), reads=[sz.b, pz.b], writes=[sz.b])
                            S.dve(_I("tensor_tensor", out=yv[:], in0=yv[:], in1=sz[:], op=ALU.mult), reads=[yv.b, sz.b], writes=[yv.b])
                            for g in range(2):
                                S.act(_I("activation", out=junk[:, 0:256], in_=yv[:, g * 256:(g + 1) * 256], func=AF.Square, accum_out=ssg[:, g:g + 1]),
                                      reads=[yv.b], writes=[junk.b, ssg.b])
                            S.act(_I("activation", out=ssg[:, 2:4], in_=ssg[:, 0:2], func=AF.Ln, scale=1.0 / 256, bias=epsc[:, 0:1]),
                                  reads=[ssg.b, epsc.b], writes=[ssg.b])
                            S.act(_I("activation", out=ssg[:, 2:4], in_=ssg[:, 2:4], func=AF.Exp, scale=-0.5), reads=[ssg.b], writes=[ssg.b])
                            for g in range(2):
                                S.dve(_I("scalar_tensor_tensor", out=yn[:, g * 256:(g + 1) * 256], in0=yv[:, g * 256:(g + 1) * 256],
                                         scalar=ssg[:, 2 + g:3 + g], in1=snw[:, g * 256:(g + 1) * 256], op0=ALU.mult, op1=ALU.mult),
                                      reads=[yv.b, ssg.b, snw.b], writes=[yn.b])
                            pTf = bk[bm["pTf"]]
                            pTfv = pTf[:].bitcast(BF16)
                            for j in range(4):
                                S.pe(_I("transpose", out=pTfv[:, j * 128:(j + 1) * 128], in_=yn[:, j * 128:(j + 1) * 128], identity=ident[:]),
                                     reads=[yn.b, ident.b], writes=[pTf.b])
                            S.act(_I("activation", out=OT[:, 0:4, t * 128:(t + 1) * 128], in_=pTfv[:, 0:512].rearrange("p (c t) -> p c t", t=128),
                                     func=AF.Copy),
                                  reads=[pTf.b], writes=[OT.b])

                def hgrn_pass(full, P, junk, bk, hn_cache, bm):
                    if True:
                        lo, hi_ = (0, 2048) if full else (512, 1536)
                        Wh = C.sb(P, "Wh", [128, 8, hi_ - lo], BF16)
                        for kc in range(8):
                            S.dma("pool", _I("dma_start", out=Wh[:, kc, :], in_=swv[:, kc, 1544 + lo:1544 + hi_]), writes=[Wh.b])
                        OF = 512 - lo
                        OI = 1024 - lo
                        sig = C.sb(P, "sig", [128, 512], F32)
                        gl = C.sb(P, "gl", [128, 512], F32)
                        kin = C.sb(P, "kin", [128, 512], F32)
                        gcum = C.sb(P, "gcum", [128, 512], F32)
                        rr = C.sb(P, "rr", [128, 512], F32)
                        kdl = C.sb(P, "kdl", [128, 512], BF16)
                        egl = C.sb(P, "egl", [128, 8], F32)
                        vb = C.sb(P, "vb", [64, 512], BF16)
                        kdlT = C.sb(P, "kdlT", [64, 512], BF16)
                        if full:
                            mask32 = C.sb(P, "mask32", [128, 128], F32)
                            S.dma("sp", _I("dma_start", out=mask32[:], in_=maskT_d), writes=[mask32.b])
                            qs = C.sb(P, "qs", [128, 512], F32)
                            eg = C.sb(P, "eg", [128, 512], F32)
                            qg = C.sb(P, "qg", [128, 512], BF16)
                            kd = C.sb(P, "kd", [128, 512], BF16)
                            AT = C.sb(P, "AT", [64, 4, 64], BF16)
                            on = C.sb(P, "on", [64, 512], F32)
                            sgt = C.sb(P, "sgt", [64, 512], F32)
                            ob = C.sb(P, "ob", [64, 512], BF16)
                            ss4 = C.sb(P, "ss4", [64, 8], F32)
                        for t in range(16):
                            yield
                            hnT_ = hn_cache[t]
                            pf = bk[bm["pf"]]
                            for j in range(4):
                                for kc in range(8):
                                    S.pe(_I("matmul", pf[:, j * 128:(j + 1) * 128], lhsT=Wh[:, kc, OF + j * 128:OF + (j + 1) * 128],
                                            rhs=hnT_[:, kc * 128:(kc + 1) * 128], start=(kc == 0), stop=(kc == 7)),
                                         reads=[Wh.b, hnT_.b], writes=[pf.b])
                            if full:
                                pq = bk[bm["pq"]]
                                for j in range(4):
                                    for kc in range(8):
                                        S.pe(_I("matmul", pq[:, j * 128:(j + 1) * 128], lhsT=Wh[:, kc, j * 128:(j + 1) * 128],
                                                rhs=hnT_[:, kc * 128:(kc + 1) * 128], start=(kc == 0), stop=(kc == 7)),
                                             reads=[Wh.b, hnT_.b], writes=[pq.b])
                            v3 = lambda tl: tl[:].rearrange("p (h t) -> p h t", t=128)
                            S.act(_I("activation", out=sig[:], in_=pf[:], func=AF.Exp, scale=-1.0), reads=[pf.b], writes=[sig.b])
                            S.act(_I("activation", out=sig[:], in_=sig[:], func=AF.Ln, bias=onec[:, 0:1], scale=1.0), reads=[sig.b, onec.b], writes=[sig.b])
                            S.act(_I("activation", out=sig[:], in_=sig[:], func=AF.Exp, scale=-1.0), reads=[sig.b], writes=[sig.b])
                            S.dve(_I("tensor_tensor", out=v3(sig), in0=v3(sig), in1=oml[:].unsqueeze(2).to_broadcast([128, 4, 128]), op=ALU.mult),
                                  reads=[sig.b, oml.b], writes=[sig.b])
                            S.dve(_I("tensor_tensor", out=v3(sig), in0=v3(sig), in1=lb[:].unsqueeze(2).to_broadcast([128, 4, 128]), op=ALU.add),
                                  reads=[sig.b, lb.b], writes=[sig.b])
                            S.act(_I("activation", out=gl[:], in_=sig[:], func=AF.Ln), reads=[sig.b], writes=[gl.b])
                            S.dve(_I("tensor_scalar", out=kin[:], in0=sig[:], scalar1=-1.0, scalar2=1.0, op0=ALU.mult, op1=ALU.add),
                                  reads=[sig.b], writes=[kin.b])
                            S.dve(_I("tensor_tensor_scan", out=gcum[:], data0=rmask[:], data1=gl[:], initial=0.0, op0=ALU.mult, op1=ALU.add),
                                  reads=[rmask.b, gl.b], writes=[gcum.b])
                            g8 = gcum[:].rearrange("p (a l) -> p a l", l=64)
                            S.dve(_I("tensor_tensor", out=rr[:].rearrange("p (a l) -> p a l", l=64), in0=g8[:, :, 63:64].to_broadcast([128, 8, 64]),
                                     in1=g8, op=ALU.subtract),
                                  reads=[gcum.b], writes=[rr.b])
                            S.act(_I("activation", out=rr[:], in_=rr[:], func=AF.Exp), reads=[rr.b], writes=[rr.b])
                            S.dve(_I("tensor_tensor", out=kdl[:], in0=kin[:], in1=rr[:], op=ALU.mult), reads=[kin.b, rr.b], writes=[kdl.b])
                            S.act(_I("activation", out=egl[:].unsqueeze(2), in_=g8[:, :, 63:64], func=AF.Exp), reads=[gcum.b], writes=[egl.b])
                            if full:
                                S.act(_I("activation", out=qs[:], in_=pq[:], func=AF.Exp, scale=-1.0), reads=[pq.b], writes=[qs.b])
                                S.act(_I("activation", out=qs[:], in_=qs[:], func=AF.Ln, bias=onec[:, 0:1], scale=1.0), reads=[qs.b, onec.b], writes=[qs.b])
                                S.act(_I("activation", out=qs[:], in_=qs[:], func=AF.Exp, scale=-1.0), reads=[qs.b], writes=[qs.b])
                                S.dve(_I("tensor_tensor", out=qs[:], in0=qs[:], in1=pq[:], op=ALU.mult), reads=[qs.b, pq.b], writes=[qs.b])
                                S.act(_I("activation", out=eg[:], in_=gcum[:], func=AF.Exp), reads=[gcum.b], writes=[eg.b])
                                S.dve(_I("tensor_tensor", out=qg[:], in0=qs[:], in1=eg[:], op=ALU.mult), reads=[qs.b, eg.b], writes=[qg.b])
                                S.act(_I("activation", out=eg[:], in_=gcum[:], func=AF.Exp, scale=-1.0), reads=[gcum.b], writes=[eg.b])
                                S.dve(_I("tensor_tensor", out=kd[:], in0=kin[:], in1=eg[:], op=ALU.mult), reads=[kin.b, eg.b], writes=[kd.b])
                            for c in range(2):
                                pi_ = bk[bm["pi"]]
                                for kc in range(8):
                                    S.pe(_I("matmul", pi_[0:64, :], lhsT=hnT_[:, kc * 128 + c * 64:kc * 128 + c * 64 + 64], rhs=Wh[:, kc, OI:OI + 512],
                                            start=(kc == 0), stop=(kc == 7)),
                                         reads=[hnT_.b, Wh.b], writes=[pi_.b])
                                S.act(_I("activation", out=vb[:], in_=pi_[0:64, :], func=AF.Copy), reads=[pi_.b], writes=[vb.b])
                                pk = bk[bm["pk"]]
                                pkv = pk[:].bitcast(BF16)
                                for h in range(4):
                                    S.pe(_I("transpose", out=pkv[0:64, h * 128:(h + 1) * 128], in_=kdl[:, h * 128 + c * 64:h * 128 + c * 64 + 64],
                                            identity=ident[:]),
                                         reads=[kdl.b, ident.b], writes=[pk.b])
                                S.act(_I("activation", out=kdlT[:], in_=pkv[0:64, 0:512], func=AF.Copy), reads=[pk.b], writes=[kdlT.b])
                                if full:
                                    psc = bk[bm["psc"]]
                                    for h in range(4):
                                        sl = slice(h * 128 + c * 64, h * 128 + c * 64 + 64)
                                        S.pe(_I("matmul", psc[0:64, h * 64:(h + 1) * 64], lhsT=kd[:, sl], rhs=qg[:, sl], start=True, stop=True),
                                             reads=[kd.b, qg.b], writes=[psc.b])
                                    S.dve(_I("tensor_tensor", out=AT[:], in0=psc[0:64, 0:256].rearrange("p (h l) -> p h l", l=64),
                                             in1=mask32[0:64, 0:64].unsqueeze(1).to_broadcast([64, 4, 64]), op=ALU.mult),
                                          reads=[psc.b, mask32.b], writes=[AT.b])
                                    po = bk[bm["po"]]
                                    for h in range(4):
                                        sl = slice(h * 128 + c * 64, h * 128 + c * 64 + 64)
                                        S.pe(_I("matmul", po[0:64, h * 128:(h + 1) * 128], lhsT=AT[:, h, :], rhs=vb[:, h * 128:(h + 1) * 128],
                                                start=True, stop=False),
                                             reads=[AT.b, vb.b], writes=[po.b])
                                        S.pe(_I("matmul", po[0:64, h * 128:(h + 1) * 128], lhsT=qg[:, sl], rhs=Hb[:, h * 128:(h + 1) * 128],
                                                start=False, stop=True),
                                             reads=[qg.b, Hb.b], writes=[po.b])
                                pst = bk[bm["hpst"]]
                                for h in range(4):
                                    S.pe(_I("matmul", pst[:, h * 128:(h + 1) * 128], lhsT=kdlT[:, h * 128:(h + 1) * 128], rhs=vb[:, h * 128:(h + 1) * 128],
                                            start=True, stop=True),
                                         reads=[kdlT.b, vb.b], writes=[pst.b])
                                S.dve(_I("tensor_tensor", out=Hm[:].rearrange("p (h v) -> p h v", v=128), in0=Hm[:].rearrange("p (h v) -> p h v", v=128),
                                         in1=egl[:].rearrange("p (h c) -> p h c", c=2)[:, :, c:c + 1].to_broadcast([128, 4, 128]), op=ALU.mult),
                                      reads=[Hm.b, egl.b], writes=[Hm.b])
                                S.dve(_I("tensor_tensor", out=Hm[:], in0=Hm[:], in1=pst[:], op=ALU.add), reads=[Hm.b, pst.b], writes=[Hm.b])
                                S.act(_I("activation", out=Hb[:], in_=Hm[:], func=AF.Copy), reads=[Hm.b], writes=[Hb.b])
                                if not full:
                                    continue
                                for h in range(4):
                                    S.act(_I("activation", out=junk[0:64, 0:128], in_=po[0:64, h * 128:(h + 1) * 128], func=AF.Square,
                                             accum_out=ss4[:, h:h + 1]),
                                          reads=[po.b], writes=[junk.b, ss4.b])
                                S.act(_I("activation", out=ss4[:, 4:8], in_=ss4[:, 0:4], func=AF.Ln, scale=1.0 / 128, bias=epsc[0:64, 0:1]),
                                      reads=[ss4.b, epsc.b], writes=[ss4.b])
                                S.act(_I("activation", out=ss4[:, 4:8], in_=ss4[:, 4:8], func=AF.Exp, scale=-0.5), reads=[ss4.b], writes=[ss4.b])
                                S.dve(_I("tensor_tensor", out=on[:].rearrange("p (h v) -> p h v", v=128), in0=po[0:64, :].rearrange("p (h v) -> p h v", v=128),
                                         in1=ss4[:, 4:8].unsqueeze(2).to_broadcast([64, 4, 128]), op=ALU.mult),
                                      reads=[po.b, ss4.b], writes=[on.b])
                                S.pool(_I("tensor_tensor", out=on[:].rearrange("p (h v) -> p h v", v=128), in0=on[:].rearrange("p (h v) -> p h v", v=128),
                                          in1=gnw[0:64, :].unsqueeze(1).to_broadcast([64, 4, 128]), op=ALU.mult),
                                       reads=[on.b, gnw.b], writes=[on.b])
                                pg = bk[bm["pg"]]
                                for kc in range(8):
                                    S.pe(_I("matmul", pg[0:64, :], lhsT=hnT_[:, kc * 128 + c * 64:kc * 128 + c * 64 + 64], rhs=Wh[:, kc, 1536:2048],
                                            start=(kc == 0), stop=(kc == 7)),
                                         reads=[hnT_.b, Wh.b], writes=[pg.b])
                                S.act(_I("activation", out=sgt[:], in_=pg[0:64, :], func=AF.Exp, scale=-1.0), reads=[pg.b], writes=[sgt.b])
                                S.act(_I("activation", out=sgt[:], in_=sgt[:], func=AF.Ln, bias=onec[0:64, 0:1], scale=1.0), reads=[sgt.b, onec.b], writes=[sgt.b])
                                S.act(_I("activation", out=sgt[:], in_=sgt[:], func=AF.Exp, scale=-1.0), reads=[sgt.b], writes=[sgt.b])
                                S.dve(_I("tensor_tensor", out=sgt[:], in0=sgt[:], in1=pg[0:64, :], op=ALU.mult), reads=[sgt.b, pg.b], writes=[sgt.b])
                                S.dve(_I("tensor_tensor", out=ob[:], in0=on[:], in1=sgt[:], op=ALU.mult), reads=[on.b, sgt.b], writes=[ob.b])
                                pT2 = bk[bm["pT2"]]
                                pT2v = pT2[:].bitcast(BF16)
                                for h in range(4):
                                    S.pe(_I("transpose", out=pT2v[:, h * 64:(h + 1) * 64], in_=ob[:, h * 128:(h + 1) * 128], identity=ident[0:64, 0:64]),
                                         reads=[ob.b, ident.b], writes=[pT2.b])
                                S.act(_I("activation", out=OT[:, 4:8, t * 128 + c * 64:t * 128 + c * 64 + 64],
                                         in_=pT2v[:, 0:256].rearrange("p (h t) -> p h t", t=64), func=AF.Copy),
                                      reads=[pT2.b], writes=[OT.b])

                SSD_FULL = dict(pX0=2, pX1=3, psm=4, pxs=5, pbt=6, pst=6, pyo=7, pcb=4, pe0=2, pe1=3, py=2, pz=5, pTf=3)
                HG_FULL = dict(pf=2, pq=3, pi=4, pk=5, psc=6, po=7, hpst=2, pg=3, pT2=6)
                SSD_ST = dict(pX0=2, pX1=3, psm=4, pxs=2, pbt=3, pst=3)
                HG_ST = dict(pf=5, hpst=5, pi=6, pk=7)

                def run_passes(specs, sched=False, pe_groups=False):
                    with contextlib.ExitStack() as P:
                        junk = C.sb(P, "junk", [128, D], F32)
                        ss = C.sb(P, "ss", [128, 2], F32)
                        rstd = C.sb(P, "rstd", [128, 1], F32)
                        hn = [C.sb(P, "hn", [128, D], BF16) for _ in range(2)]
                        hnT = [C.sb(P, "hnT", [128, D], BF16) for _ in range(2)]
                        stb = (junk, ss, rstd, hn, hnT)
                        bk = [C.ps(P, "bk", [128, 512], F32) for _ in range(8)]
                        hn_cache = {}
                        gens = [fn(full, P, junk, bk, hn_cache, bm) for fn, full, bm in specs]
                        for g in gens:
                            next(g)
                        hn_cache[0] = hn_tile1(stb, bk, 0)
                        for t in range(16):
                            if t + 1 < 16:
                                hn_cache[t + 1] = hn_tile1(stb, bk, t + 1)
                            for g in gens:
                                next(g, None)
                        S.flush(sched=sched, pe_groups=pe_groups)

                run_passes([(ssd_pass, False, SSD_ST), (hgrn_pass, False, HG_ST)], sched=SCHED_MASK[0])
                S.dma("sp", _I("dma_start", out=pay_d[:, 0:512], in_=Sm[:]), reads=[Sm.b], writes=[payb])
                S.dma("sp", _I("dma_start", out=pay_d[:, 512:1024], in_=Hm[:]), reads=[Hm.b], writes=[payb])
                S.dma("sp", _I("dma_start", out=pay_d[:, 1024:1048], in_=haloo[:].rearrange("p c j -> p (c j)")), reads=[haloo.b], writes=[payb])
                S.cc(_I("collective_compute", "AllGather", ALU.bypass, replica_groups=[[0, 1], [2, 3], [4, 5], [6, 7]],
                        ins=[pay_d.opt()], outs=[gath_d.opt()]),
                     reads=[payb], writes=[gathb])
                S.dma("sp", _I("dma_start", out=recv[:], in_=gath_d[0:128, :]), reads=[gathb], writes=[recv.b])
                S.dve(_I("tensor_scalar_mul", out=Sm[:], in0=recv[:, 0:512], scalar1=flag[:, 0:1]), reads=[recv.b, flag.b], writes=[Sm.b])
                S.dve(_I("tensor_scalar_mul", out=Hm[:], in0=recv[:, 512:1024], scalar1=flag[:, 0:1]), reads=[recv.b, flag.b], writes=[Hm.b])
                S.dve(_I("tensor_scalar_mul", out=halo0[:].rearrange("p c j -> p (c j)"), in0=recv[:, 1024:1048], scalar1=flag[:, 0:1]),
                      reads=[recv.b, flag.b], writes=[halo0.b])
                S.act(_I("activation", out=Sb[:], in_=Sm[:], func=AF.Copy), reads=[Sm.b], writes=[Sb.b])
                S.act(_I("activation", out=Hb[:], in_=Hm[:], func=AF.Copy), reads=[Hm.b], writes=[Hb.b])
                S.flush(sched=SCHED.get("exchange", False))
                run_passes([(ssd_pass, True, SSD_FULL)], sched=SCHED_MASK[1], pe_groups=SCHED_MASK[1])
                run_passes([(hgrn_pass, True, HG_FULL)], sched=SCHED_MASK[2])

        def outproj1():
            with contextlib.ExitStack() as Ff:
                wo = C.sb(Ff, "wo", [128, 8, D], BF16)
                pm = [C.ps(Ff, "pm", [128, 512], F32) for _ in range(4)]
                S.dma("pool", _I("dma_start", out=wo[:], in_=swout_d.rearrange("(c p) n -> p c n", p=128)), writes=[wo.b])
                for t in range(16):
                    for hf in range(2):
                        pm_ = pm[(2 * t + hf) % 4]
                        for kc in range(8):
                            S.pe(_I("matmul", pm_[:], lhsT=OT[:, kc, t * 128:(t + 1) * 128], rhs=wo[:, kc, hf * 512:(hf + 1) * 512],
                                    start=(kc == 0), stop=(kc == 7)),
                                 reads=[OT.b, wo.b], writes=[pm_.b])
                        S.dve(_I("tensor_tensor", out=xres[:, t, hf * 512:(hf + 1) * 512], in0=pm_[:], in1=xres[:, t, hf * 512:(hf + 1) * 512],
                                 op=ALU.add),
                              reads=[pm_.b, xres.b], writes=[xres.b])
                S.flush(sched=SCHED.get("outproj1", False))

        def final_store():
            with contextlib.ExitStack() as Ff:
                nw = C.sb(Ff, "nw", [128, D], F32)
                S.dma("sp", _I("dma_start", out=nw[:], in_=nfin_d.partition_broadcast(128)), writes=[nw.b])
                junk = C.sb(Ff, "junk", [128, D], F32)
                ss = C.sb(Ff, "ss", [128, 2], F32)
                rstd = C.sb(Ff, "rstd", [128, 1], F32)
                ot = [C.sb(Ff, "ot", [128, D], F32) for _ in range(2)]
                for t in range(16):
                    rms_stats(S, xres[:, t, :], [xres.b], D, junk, ss, rstd, epsc)
                    ot_ = ot[t % 2]
                    S.dve(_I("scalar_tensor_tensor", out=ot_[:], in0=xres[:, t, :], scalar=rstd[:, 0:1], in1=nw[:], op0=ALU.mult, op1=ALU.mult),
                          reads=[xres.b, rstd.b, nw.b], writes=[ot_.b])
                    S.dma("sp", _I("dma_start", out=out_d[t * 128:(t + 1) * 128, :], in_=ot_[:]), reads=[ot_.b])
                S.flush(sched=SCHED.get("final", False))

        def store_x():
            for t in range(16):
                S.dma("sp", _I("dma_start", out=out_d[t * 128:(t + 1) * 128, :], in_=xres[:, t, :]), reads=[xres.b])
            S.flush(sched=SCHED.get("storex", False))

        layer0_attention()
        xres = C.sb(top, "xres", [128, NT, D], F32)
        outproj0()
        if stage == "attn0":
            store_x()
            return nc
        ffn(0)
        if stage == "l0":
            store_x()
            return nc
        layer1()
        outproj1()
        if stage == "l1mix":
            store_x()
            return nc
        ffn(1)
        final_store()
    return nc


def _rope_tables(pos, dim):
    inv = (1.0 / (10000.0 ** (np.arange(0, dim, 2, dtype=np.float32) / np.float32(dim)))).astype(np.float32)
    ang = pos.astype(np.float32)[:, None] * inv[None, :]
    ang = np.concatenate([ang, ang], axis=-1)
    cos = np.cos(ang).astype(np.float32)
    sin = np.sin(ang).astype(np.float32)
    half = dim // 2
    sin_s = np.concatenate([-sin[:, :half], sin[:, half:]], axis=-1)
    f = lambda a: np.ascontiguousarray(a.reshape(32, 128, dim).transpose(1, 0, 2))
    return f(cos), f(sin_s)


def make_in_maps(inp):
    x = np.asarray(inp["x"], np.float32)
    f = lambda k: np.ascontiguousarray(np.asarray(inp[k], np.float32))
    wuq = f("a_w_uq")[0].reshape(384, 8, 96)
    wuq = np.ascontiguousarray(np.concatenate([wuq[:, :, :64].reshape(384, 512), wuq[:, :, 64:].reshape(384, 256)], axis=1))
    wukv = f("a_w_ukv")[0].reshape(256, 8, 128)
    wukv = np.ascontiguousarray(np.concatenate([wukv[:, :, :64].reshape(256, 512), wukv[:, :, 64:].reshape(256, 512)], axis=1))
    lam = np.ascontiguousarray(np.stack([f("a_lq1")[0], f("a_lk1")[0], f("a_lq2")[0], f("a_lk2")[0]]))
    rmask = np.ones((128, 512), np.float32)
    rmask[:, ::64] = 0.0
    shared = {
        "norm_mix": f("norm_mix"), "norm_ffn": f("norm_ffn"), "norm_final": f("norm_final"),
        "a_w_in": f("a_w_in")[0], "a_q_norm": f("a_q_norm")[0], "a_w_uq": wuq, "a_kv_norm": f("a_kv_norm")[0],
        "a_w_ukv": wukv, "a_lam": lam, "a_subln": f("a_subln")[0].reshape(128, 1).copy(), "a_w_out": f("a_w_out")[0],
        "ffn_gate": f("ffn_gate"), "ffn_up": f("ffn_up"), "ffn_down": f("ffn_down"),
        "s_w_in": f("s_w_in")[0], "s_w_out": f("s_w_out")[0],
        "s_conv_w": np.ascontiguousarray(f("s_conv_w")[0].T.reshape(8, 128, 4).transpose(1, 0, 2)),
        "s_conv_b": np.ascontiguousarray(f("s_conv_b")[0].reshape(8, 128).T),
        "s_dt_bias": f("s_dt_bias")[0], "s_a_log": f("s_a_log")[0], "s_d": f("s_d")[0], "s_norm": f("s_norm")[0],
        "h_g_norm": f("h_g_norm")[0],
        "h_lb": np.ascontiguousarray(f("h_lower_bound").reshape(2, 4, 128).transpose(2, 0, 1)),
        "rmask": rmask,
        "ident": np.eye(128, dtype=np.float32),
        "maskT": np.triu(np.ones((128, 128), np.float32)),
    }
    maps = []
    for c in range(8):
        b, hf = c // 2, c % 2
        m = dict(shared)
        m["x"] = np.ascontiguousarray(x[b, hf * TOK:(hf + 1) * TOK])
        if hf == 1:
            m["xp"] = np.ascontiguousarray(x[b, 0:TOK])
            pos = np.arange(4096)
            kones = np.ones((128, 32), np.float32)
        else:
            m["xp"] = np.zeros((TOK, D), np.float32)
            pos = np.concatenate([np.arange(TOK), np.arange(TOK)])
            kones = np.ones((128, 32), np.float32)
            kones[:, :16] = 0.0
        m["kones"] = kones
        m["flag"] = np.full((128, 1), float(hf), np.float32)
        m["cos32"], m["sin32"] = _rope_tables(pos, 32)
        m["cos64"], m["sin64"] = _rope_tables(pos, 64)
        maps.append(m)
    return maps


_NC_CACHE = {}


def kernel(**inputs):
    stage = inputs.pop("_stage", "all")
    if stage not in _NC_CACHE:
        _NC_CACHE[stage] = build_program(stage)
    nc = _NC_CACHE[stage]
    maps = make_in_maps(inputs)
    res = run_bass_kernel_spmd(nc, maps, core_ids=list(range(8)))
    out = np.zeros((4, 4096, D), np.float32)
    for c in range(8):
        b, hf = c // 2, c % 2
        out[b, hf * TOK:(hf + 1) * TOK] = res.results[c]["out"]
    return out
```
